# Optimizing a Trainium2 kernel written in Bass

```python
import jax, jax.numpy as jnp
from jax import lax
import numpy as np

D_MODEL = 4096
BATCH = 4
SEQ = 4096
DEPTH = 1

D_MIX = D_MODEL
RWKV_W = D_MIX // 2
CONV_W = D_MIX - RWKV_W
HEAD_SIZE = 64
N_HEADS = RWKV_W // HEAD_SIZE
LORA_W = 96
LORA_A = 96
CONV_K = 31
NORM_EPS = 1e-6
LN_EPS = 1e-5
GN_EPS = 1e-5 * HEAD_SIZE
SHIFT_COLS = 3 * RWKV_W + LORA_W + LORA_A
IN_COLS = SHIFT_COLS + RWKV_W + 2 * CONV_W + CONV_W

kernel_name = "hybrid_rwkv7_conformer_parallel"


def rms_norm(x, g):
    xf = x.astype(jnp.float32)
    y = xf * lax.rsqrt(jnp.mean(xf * xf, axis=-1, keepdims=True) + NORM_EPS)
    return (y * g.astype(jnp.float32)).astype(x.dtype)


def layer_norm(x, g, b):
    xf = x.astype(jnp.float32)
    mu = jnp.mean(xf, axis=-1, keepdims=True)
    var = jnp.mean(jnp.square(xf - mu), axis=-1, keepdims=True)
    y = (xf - mu) * lax.rsqrt(var + LN_EPS)
    return (y * g.astype(jnp.float32) + b.astype(jnp.float32)).astype(x.dtype)


def rwkv7_recurrence(r, w, k, v, kk, a):
    B, T, H, N = r.shape
    tm = lambda z: jnp.moveaxis(z, 1, 0)
    seq = (tm(r), tm(w), tm(k), tm(v), tm(kk), tm(kk * a))

    def step(S, inp):
        r_t, w_t, k_t, v_t, kk_t, b_t = inp
        sa = jnp.einsum('bhij,bhj->bhi', S, -kk_t)
        S = S * w_t[:, :, None, :] + sa[..., None] * b_t[:, :, None, :] + v_t[..., None] * k_t[:, :, None, :]
        y_t = jnp.einsum('bhij,bhj->bhi', S, r_t)
        return S, y_t

    S0 = jnp.zeros((B, H, N, N), jnp.float32)
    _, ys = lax.scan(step, S0, seq)
    return jnp.moveaxis(ys, 0, 1)


def setup_inputs(seed: int = 0) -> dict:
    key = jax.random.key(seed)
    ks = jax.random.split(key, 24)
    f32 = jnp.float32
    nrm = lambda k, s, sc: jax.random.normal(k, s, f32) * sc
    return {
        "x": nrm(ks[0], (BATCH, SEQ, D_MODEL), 1.0),
        "norm_pre_g": 1.0 + nrm(ks[1], (D_MODEL,), 0.02),
        "w_in": nrm(ks[2], (D_MODEL, IN_COLS), D_MODEL ** -0.5),
        "mu_shift": jax.random.uniform(ks[3], (SHIFT_COLS,), f32, 0.0, 1.0),
        "w0": jax.random.uniform(ks[4], (RWKV_W,), f32, -5.5, -0.5),
        "w_lora_up": nrm(ks[5], (LORA_W, RWKV_W), 0.5 * LORA_W ** -0.5),
        "a0": nrm(ks[6], (RWKV_W,), 0.1),
        "a_lora_up": nrm(ks[7], (LORA_A, RWKV_W), 0.5 * LORA_A ** -0.5),
        "k_k": 0.85 + nrm(ks[8], (RWKV_W,), 0.02),
        "k_a": 1.0 + nrm(ks[9], (RWKV_W,), 0.02),
        "r_k": nrm(ks[10], (N_HEADS, HEAD_SIZE), 0.1),
        "lnx_g": 1.0 + nrm(ks[11], (RWKV_W,), 0.02),
        "lnx_b": nrm(ks[12], (RWKV_W,), 0.01),
        "conv_w": nrm(ks[13], (CONV_K, CONV_W), CONV_K ** -0.5),
        "conv_b": nrm(ks[14], (CONV_W,), 0.01),
        "cln_g": 1.0 + nrm(ks[15], (CONV_W,), 0.02),
        "cln_b": nrm(ks[16], (CONV_W,), 0.01),
        "w_pw2": nrm(ks[17], (CONV_W, CONV_W), CONV_W ** -0.5),
        "b_pw2": nrm(ks[18], (CONV_W,), 0.01),
        "w_out": nrm(ks[19], (D_MIX, D_MODEL), D_MIX ** -0.5),
        "norm_post_g": 1.0 + nrm(ks[20], (D_MODEL,), 0.02),
    }


def hybrid_layer(x, norm_pre_g, w_in, mu_shift, w0, w_lora_up, a0, a_lora_up, k_k, k_a, r_k,
                 lnx_g, lnx_b, conv_w, conv_b, cln_g, cln_b, w_pw2, b_pw2, w_out, norm_post_g):
    B, T, _ = x.shape
    f32 = jnp.float32
    h = rms_norm(x, norm_pre_g)
    proj = jnp.einsum('btd,dc->btc', h, w_in)
    c0 = SHIFT_COLS
    c1 = c0 + RWKV_W
    c2 = c1 + CONV_W
    c3 = c2 + CONV_W
    rwkv_in, g_rwkv, glu_v, glu_g, g_conv = jnp.split(proj, [c0, c1, c2, c3], axis=-1)

    prev = jnp.pad(rwkv_in, ((0, 0), (1, 0), (0, 0)))[:, :-1]
    xs = rwkv_in + (prev - rwkv_in) * mu_shift
    r, k, v, w_low, a_low = jnp.split(xs, [RWKV_W, 2 * RWKV_W, 3 * RWKV_W, 3 * RWKV_W + LORA_W], axis=-1)
    w_log = -jax.nn.softplus(-(w0 + jnp.tanh(w_low) @ w_lora_up).astype(f32)) - 0.5
    decay = jnp.exp(-jnp.exp(w_log))
    a = jax.nn.sigmoid((a0 + a_low @ a_lora_up).astype(f32))
    hs = lambda z: z.astype(f32).reshape(B, T, N_HEADS, HEAD_SIZE)
    r_h, k_f, v_h, a_h, w_h = hs(r), hs(k), hs(v), hs(a), hs(decay)
    k_k_h = k_k.astype(f32).reshape(N_HEADS, HEAD_SIZE)
    k_a_h = k_a.astype(f32).reshape(N_HEADS, HEAD_SIZE)
    kk = k_f * k_k_h
    kk = kk / jnp.maximum(jnp.linalg.norm(kk, axis=-1, keepdims=True), 1e-12)
    k_h = k_f * (1.0 + (a_h - 1.0) * k_a_h)
    y = rwkv7_recurrence(r_h, w_h, k_h, v_h, kk, a_h)
    mu = jnp.mean(y, axis=-1, keepdims=True)
    var = jnp.mean(jnp.square(y - mu), axis=-1, keepdims=True)
    y = ((y - mu) * lax.rsqrt(var + GN_EPS)).reshape(B, T, RWKV_W)
    y = y * lnx_g.astype(f32) + lnx_b.astype(f32)
    bonus = jnp.sum(r_h * k_h * r_k.astype(f32), axis=-1, keepdims=True) * v_h
    y = (y + bonus.reshape(B, T, RWKV_W)).astype(x.dtype)
    y_rwkv = y * jax.nn.silu(g_rwkv)

    u = glu_v * jax.nn.sigmoid(glu_g)
    u_pad = jnp.pad(u, ((0, 0), (CONV_K - 1, 0), (0, 0)))
    c = lax.conv_general_dilated(u_pad, conv_w[:, None, :].astype(u.dtype), window_strides=(1,),
                                 padding='VALID', dimension_numbers=('NWC', 'WIO', 'NWC'),
                                 feature_group_count=CONV_W) + conv_b
    c = jax.nn.silu(layer_norm(c, cln_g, cln_b))
    c = jnp.einsum('btc,ce->bte', c, w_pw2) + b_pw2
    y_conv = c * jax.nn.silu(g_conv)

    mix = jnp.concatenate([y_rwkv, y_conv], axis=-1)
    out = jnp.einsum('btc,cd->btd', mix, w_out)
    return x + rms_norm(out, norm_post_g)


def reference(x, norm_pre_g, w_in, mu_shift, w0, w_lora_up, a0, a_lora_up, k_k, k_a, r_k,
              lnx_g, lnx_b, conv_w, conv_b, cln_g, cln_b, w_pw2, b_pw2, w_out, norm_post_g):
    for _ in range(DEPTH):
        x = hybrid_layer(x, norm_pre_g, w_in, mu_shift, w0, w_lora_up, a0, a_lora_up, k_k, k_a, r_k,
                         lnx_g, lnx_b, conv_w, conv_b, cln_g, cln_b, w_pw2, b_pw2, w_out, norm_post_g)
    return x
```

```python
import contextlib
import numpy as np
import concourse.bass as bass
import concourse.mybir as mybir
from concourse.bass_utils import run_bass_kernel_spmd

F32 = mybir.dt.float32
BF16 = mybir.dt.bfloat16
AF = mybir.ActivationFunctionType
ALU = mybir.AluOpType

D = 4096
T = 4096
NCORES = 8
TOKC = 2048
HALO = 32
NPC = 720
C0 = 0.6065306597126334
EPOCH = 20000


class Sched:
    def __init__(self, nc):
        self.nc = nc
        self.ops = []
        self.last_w = {}
        self.readers = {}
        self.dma_keys = {}

    def _deps(self, i, reads, writes):
        deps = set()
        for k in reads:
            w = self.last_w.get(k)
            if w is not None:
                deps.add((w, "raw"))
        for k in writes:
            w = self.last_w.get(k)
            if w is not None:
                deps.add((w, "waw"))
            for r in self.readers.get(k, ()):
                deps.add((r, "war"))
        for k in reads:
            self.readers.setdefault(k, []).append(i)
        for k in writes:
            self.last_w[k] = i
            self.readers[k] = []
        return deps

    def op(self, eng, fn, reads=(), writes=()):
        i = len(self.ops)
        self.ops.append(dict(eng=eng, fn=fn, deps=self._deps(i, reads, writes), dma=None))

    def dma(self, eng, fn, reads=(), writes=(), key=None, inc=16):
        i = len(self.ops)
        deps = self._deps(i, reads, writes)
        n = self.dma_keys.get(key, (0, inc))[0] + 1
        self.dma_keys[key] = (n, inc)
        self.ops.append(dict(eng=eng, fn=fn, deps=deps, dma=(key, n, inc)))

    def emit(self, final_waits=()):
        nc = self.nc
        ops = self.ops
        need = [False] * len(ops)
        for i, o in enumerate(ops):
            for (d, kind) in o["deps"]:
                p = ops[d]
                if p["dma"] is not None:
                    continue
                if p["eng"] == o["eng"] and (p["eng"] == "pe" or kind != "raw"):
                    continue
                need[d] = True
        cnt = {e: 0 for e in ("pe", "act", "dve", "pool", "sp")}
        sig = [None] * len(ops)
        for i, o in enumerate(ops):
            if need[i]:
                e = o["eng"]
                sig[i] = (e, cnt[e] // EPOCH, cnt[e] % EPOCH + 1)
                cnt[e] += 1
        stack = contextlib.ExitStack()
        sems = {}
        for e in cnt:
            for ep in range((cnt[e] + EPOCH - 1) // EPOCH):
                sems[(e, ep)] = stack.enter_context(nc.semaphore(f"c_{e}_{ep}"))
        dsems = {}
        for k in self.dma_keys:
            dsems[k] = stack.enter_context(nc.semaphore(f"d_{len(dsems)}"))
        per_eng = {e: [] for e in cnt}
        for i, o in enumerate(ops):
            per_eng[o["eng"]].append(i)
        block = stack.enter_context(nc.Block())
        engs = {"pe": nc.tensor, "act": nc.scalar, "dve": nc.vector, "pool": nc.gpsimd, "sp": nc.sync}

        def run_engine(ename, final):
            eng = engs[ename]
            seen = {}
            for i in per_eng[ename]:
                o = ops[i]
                waits = {}
                for (d, kind) in o["deps"]:
                    p = ops[d]
                    if p["dma"] is not None:
                        key, n, inc = p["dma"]
                        s, v = ("d", key), inc * n
                    else:
                        if sig[d] is None:
                            continue
                        if p["eng"] == ename and (ename == "pe" or kind != "raw"):
                            continue
                        e, ep, v = sig[d]
                        s = (e, ep)
                    if seen.get(s, 0) >= v:
                        continue
                    waits[s] = max(waits.get(s, 0), v)
                for s, v in waits.items():
                    seen[s] = v
                    eng.wait_ge(dsems[s[1]] if s[0] == "d" else sems[s], v)
                ins = o["fn"](eng)
                if o["dma"] is not None:
                    ins.then_inc(dsems[o["dma"][0]], o["dma"][2])
                elif sig[i] is not None:
                    ins.then_inc(sems[sig[i][:2]], 1)
            for key in final:
                n, inc = self.dma_keys[key]
                eng.wait_ge(dsems[key], inc * n)

        block.tensor(lambda e: run_engine("pe", ()))
        block.scalar(lambda e: run_engine("act", ()))
        block.vector(lambda e: run_engine("dve", ()))
        block.gpsimd(lambda e: run_engine("pool", ()))
        block.sync(lambda e: run_engine("sp", final_waits))
        stack.close()
        return {e: len(per_eng[e]) for e in per_eng}


def build_program(n_rwkv_blocks=8, n_conv_blocks=4):
    nc = bass.Bass("TRN2", target_bir_lowering=False)
    dt_in = lambda name, shape: nc.dram_tensor(name, shape, F32, kind="ExternalInput").ap()
    xf = dt_in("xf", [T, D])
    xc = dt_in("xc", [HALO + TOKC, D])
    wr = dt_in("wr", [34, 128, 4096])
    wc = dt_in("wc", [48, 128, 4096])
    wp = dt_in("wp", [16, 128, 2048])
    wo = dt_in("wo", [32, 128, 4096])
    lora = dt_in("lora", [96, 2048])
    pcd = dt_in("pc", [128, NPC])
    cstd = dt_in("cst", [128, 1536])
    seld = dt_in("sel", [128, 2])
    yout = nc.dram_tensor("y", [TOKC, D], F32, kind="ExternalOutput").ap()
    zsend = [nc.dram_tensor(f"zsend{i}", [1024, 512], BF16).ap() for i in range(8)]
    zall = [nc.dram_tensor(f"zall{i}", [2048, 512], BF16).ap() for i in range(8)]

    S = Sched(nc)
    st = contextlib.ExitStack()
    sb = lambda name, shape, dt: st.enter_context(nc.sbuf_tensor("s_" + name, shape, dt))
    ps = lambda name, shape, dt: st.enter_context(nc.psum_tensor(name, shape, dt))

    NR, NM = 32, 36
    hT = sb("hT", [128, 32, 544], BF16)
    NW = 2
    wbuf = [sb(f"wbuf{i}", [128, 4096], BF16) for i in range(NW)]
    io32 = sb("io32", [128, 4096], F32)
    Rt = sb("Rt", [128, NR, 544], F32)
    Mt = sb("Mt", [128, NM, 512], BF16)
    xbf = Mt[:, 28:36, :].rearrange("p a t -> p (a t)")
    XBK = [("M", i_) for i_ in range(28, 36)]
    XT = sb("XT", [128, 4, 512], F32)
    pc = sb("pc", [128, NPC], F32)
    cst = sb("cst", [128, 1536], F32)
    cstb = sb("cstb", [128, 128], BF16)
    lor = sb("lor", [96, 2048], F32)
    sel = sb("sel", [128, 2], F32)
    S32 = sb("S32", [128, 8, 128], F32)
    Sbf = sb("Sbf", [128, 8, 128], BF16)
    carry = sb("carry", [128, 32], F32)
    ucar = sb("ucar", [128, 16, 32], F32)
    small = sb("small", [128, 16], F32)
    ystg = sb("ystg", [128, 4, 512], BF16)
    pb = [ps(f"pb{i}", [128, 512], F32) for i in range(7)]
    pb7 = ps("pb7", [128, 1024], BF16)

    def R(i):
        return Rt[:, i, :]

    def R5(i):
        return Rt[:, i, 0:512]

    def M(i):
        return Mt[:, i, :]

    def M8(i):
        return Mt[:, i:i + 2, :].rearrange("p a (c t) -> p (a c) t", t=128)

    def M4(i):
        return Mt[:, i, :].rearrange("p (c t) -> p c t", t=128)

    ident_f = cst[:, 0:128]
    bdones = cst[:, 128:256]
    ones_f = cst[:, 256:384]
    maskA = cst[:, 384:896]
    rst = cst[:, 896:1408]
    ident_b = cstb[:, 0:128]

    def op(engn, meth, *args, r=(), w=(), **kw):
        S.op(engn, lambda e: getattr(e, meth)(*args, **kw), reads=list(r), writes=list(w))

    def dma(engn, out, in_, r=(), w=(), key=None):
        S.dma(engn, lambda e: e.dma_start(out=out, in_=in_), reads=list(r), writes=list(w), key=key)

    dma("sp", pc[:], pcd, w=["pc"], key="pc")
    dma("sp", cst[:], cstd, w=["cst"], key="cst")
    dma("sp", lor[:], lora, w=["lor"], key="lor")
    dma("sp", sel[:], seld, w=["sel"], key="sel")
    dma("pool", cstb[:], cstd[:, 0:128], w=["cstb"], key="cstb")
    op("dve", "memset", S32[:], 0.0, w=[("S32", i) for i in range(8)])
    op("dve", "memset", Sbf[:], 0.0, w=[("Sbf", i) for i in range(8)])
    op("dve", "memset", carry[:], 0.0, w=["carry"])
    op("pool", "memset", Mt[:, 0:8, :], 0.0, w=[("M", i) for i in range(8)])
    for hp in range(8):
        op("dve", "tensor_scalar", pc[:, 706 + hp:707 + hp], pc[:, hp * 10 + 6:hp * 10 + 7], -1.0, 1.0, ALU.mult, ALU.add,
           r=["pc"], w=["pc"])

    wctr = [0]

    def load_w(src, ncols=4096):
        i = wctr[0] % NW
        wctr[0] += 1
        dma("pool", wbuf[i][:, 0:ncols], src, w=[("w", i)], key=("w", i))
        return i

    def proj(wi, M_, nk, rhs_fn, out_ap, wkey):
        for k in range(nk):
            op("pe", "matmul", out_ap, wbuf[wi][:, k * 128:k * 128 + M_], rhs_fn(k), start=(k == 0), stop=(k == nk - 1),
               r=[("w", wi), "hT"], w=[wkey])

    def build_hT(xsrc, tiles, col0):
        col = col0
        for (r0, n) in tiles:
            dma("sp", io32[0:n, :], xsrc[r0:r0 + n, :], w=["io32"], key="io32")
            op("act", "activation", xbf[0:n, :], io32[0:n, :], AF.Square, accum_out=small[0:n, 0:1], r=["io32"], w=XBK + ["sm0"])
            op("dve", "tensor_scalar", small[0:n, 1:2], small[0:n, 0:1], 1.0 / D, 1e-6, ALU.mult, ALU.add, r=["sm0"], w=["sm1"])
            op("act", "activation", small[0:n, 1:2], small[0:n, 1:2], AF.Sqrt, r=["sm1"], w=["sm1"])
            op("dve", "reciprocal", small[0:n, 2:3], small[0:n, 1:2], r=["sm1"], w=["sm2"])
            op("dve", "tensor_scalar", xbf[0:n, :], io32[0:n, :], small[0:n, 2:3], None, ALU.mult, r=["io32", "sm2"], w=XBK)
            for kg in range(4):
                for kk in range(8):
                    k = kg * 8 + kk
                    op("pe", "transpose", pb7[:, kk * 128:kk * 128 + n], xbf[0:n, k * 128:(k + 1) * 128], ident_b[0:n, 0:n],
                       r=XBK + ["cstb"], w=["pb7"])
                for kk in range(8):
                    k = kg * 8 + kk
                    if kk % 2 == 0:
                        op("act", "mul", hT[:, k, col:col + n], pb7[:, kk * 128:kk * 128 + n], pc[:, 642 + k:643 + k],
                           r=["pb7", "pc"], w=["hT"])
                    else:
                        op("dve", "tensor_scalar", hT[:, k, col:col + n], pb7[:, kk * 128:kk * 128 + n], pc[:, 642 + k:643 + k], None,
                           ALU.mult, r=["pb7", "pc"], w=["hT"])
            col += n

    pbi = [0]

    def next_pb():
        pbi[0] ^= 1
        return pbi[0]

    def shift(pbk, np_, raw, ccol, mu, tmp, out):
        op("act", "copy", R(raw)[0:np_, 1:513], pb[pbk][0:np_, :], r=[("pb", pbk)], w=[("R", raw)])
        op("act", "copy", R(raw)[0:np_, 0:1], carry[0:np_, ccol:ccol + 1], r=["carry"], w=[("R", raw)])
        op("dve", "tensor_tensor", R5(tmp)[0:np_], R(raw)[0:np_, 0:512], R(raw)[0:np_, 1:513], ALU.subtract, r=[("R", raw)], w=[("R", tmp)])
        op("dve", "scalar_tensor_tensor", R5(out)[0:np_], R5(tmp)[0:np_], mu, R(raw)[0:np_, 1:513], ALU.mult, ALU.add,
           r=[("R", tmp), ("R", raw), "pc"], w=[("R", out)])
        op("dve", "tensor_copy", carry[0:np_, ccol:ccol + 1], R(raw)[0:np_, 512:513], r=[("R", raw)], w=["carry"])

    hcols = lambda k: hT[:, k, 32:544]
    KB, BB, KHB, VB = 0, 2, 4, 6
    RT_ = 8
    AM0 = 9
    QA, QB, PA, PB, TA, TB = 17, 18, 19, 20, 21, 22
    TTF, BT, KHT, VT = 23, 25, 27, 29
    XU, YO = 31, 32

    def rwkv_block(tb):
        build_hT(xf, [(tb * 512 + tt * 128, 128) for tt in range(4)], 32)
        for j in range(2):
            wi = load_w(wr[j])
            k_ = next_pb()
            proj(wi, 96, 32, hcols, pb[k_][0:96, :], ("pb", k_))
            shift(k_, 96, 0, 24 + j, pc[0:96, 80 + j:81 + j], 1, 17 + j)
        op("act", "activation", R5(17)[0:96], R5(17)[0:96], AF.Tanh, r=[("R", 17)], w=[("R", 17)])
        for hp in range(8):
            P = lambda c: pc[:, hp * 10 + c:hp * 10 + c + 1]
            for j, outslot in enumerate((3, 4, 5)):
                wi = load_w(wr[2 + hp * 4 + j])
                k_ = next_pb()
                proj(wi, 128, 32, hcols, pb[k_][:, :], ("pb", k_))
                shift(k_, 128, j, hp * 3 + j, P(j), 9, outslot)
            wi = load_w(wr[2 + hp * 4 + 3])
            k_ = next_pb()
            proj(wi, 128, 32, hcols, pb[k_][:, :], ("pb", k_))
            op("act", "activation", R5(6), pb[k_][:, :], AF.Silu, r=[("pb", k_)], w=[("R", 6)])
            r32, kx, v32, sg = R5(3), R5(4), R5(5), R5(6)
            op("pe", "matmul", pb[2][:, :], lor[:, hp * 128:(hp + 1) * 128], R5(17)[0:96], start=True, stop=True,
               r=["lor", ("R", 17)], w=[("pb", 2)])
            op("act", "activation", R5(7), pb[2][:, :], AF.Sigmoid, bias=P(3), r=[("pb", 2), "pc"], w=[("R", 7)])
            op("pe", "matmul", pb[2][:, :], lor[:, 1024 + hp * 128:1024 + (hp + 1) * 128], R5(18)[0:96], start=True, stop=True,
               r=["lor", ("R", 18)], w=[("pb", 2)])
            op("act", "activation", R5(8), pb[2][:, :], AF.Sigmoid, bias=P(4), r=[("pb", 2), "pc"], w=[("R", 8)])
            sgw, a32 = R5(7), R5(8)
            op("act", "activation", R5(9), kx, AF.Square, scale=P(5), r=[("R", 4), "pc"], w=[("R", 9)])
            op("pe", "matmul", pb[2][:, :], bdones, R5(9), start=True, stop=True, r=["cst", ("R", 9)], w=[("pb", 2)])
            op("dve", "tensor_scalar", R5(9), pb[2][:, :], 1e-24, None, ALU.max, r=[("pb", 2)], w=[("R", 9)])
            op("act", "activation", R5(9), R5(9), AF.Sqrt, r=[("R", 9)], w=[("R", 9)])
            op("dve", "reciprocal", R5(9), R5(9), r=[("R", 9)], w=[("R", 9)])
            op("dve", "scalar_tensor_tensor", R5(10), kx, P(5), R5(9), ALU.mult, ALU.mult, r=[("R", 4), ("R", 9), "pc"], w=[("R", 10)])
            op("dve", "tensor_scalar", R5(20), a32, P(6), pc[:, 706 + hp:707 + hp], ALU.mult, ALU.add, r=[("R", 8), "pc"], w=[("R", 20)])
            op("dve", "tensor_tensor", R5(11), kx, R5(20), ALU.mult, r=[("R", 4), ("R", 20)], w=[("R", 11)])
            op("dve", "tensor_tensor", R5(12), R5(10), a32, ALU.mult, r=[("R", 10), ("R", 8)], w=[("R", 12)])
            kap, kh, b32 = R5(10), R5(11), R5(12)
            op("dve", "scalar_tensor_tensor", R5(16), r32, P(7), kh, ALU.mult, ALU.mult, r=[("R", 3), ("R", 11), "pc"], w=[("R", 16)])
            op("pe", "matmul", pb[2][:, :], bdones, R5(16), start=True, stop=True, r=["cst", ("R", 16)], w=[("pb", 2)])
            op("dve", "tensor_tensor", R5(16), pb[2][:, :], v32, ALU.mult, r=[("pb", 2), ("R", 5)], w=[("R", 16)])
            op("dve", "tensor_tensor_scan", R5(21), rst, sgw, 0.0, ALU.mult, ALU.add, r=["cst", ("R", 7)], w=[("R", 21)])
            op("act", "activation", R5(13), R5(21), AF.Exp, scale=-C0, r=[("R", 21)], w=[("R", 13)])
            op("dve", "tensor_tensor", R5(20), R5(21), sgw, ALU.subtract, r=[("R", 21), ("R", 7)], w=[("R", 20)])
            op("act", "activation", R5(14), R5(20), AF.Exp, scale=-C0, r=[("R", 20)], w=[("R", 14)])
            op("act", "activation", R5(15), R5(21), AF.Exp, scale=C0, r=[("R", 21)], w=[("R", 15)])
            eL, eLm, einv = R5(13), R5(14), R5(15)
            op("dve", "tensor_tensor", M(RT_), r32, eL, ALU.mult, r=[("R", 3), ("R", 13)], w=[("M", RT_)])
            c3 = lambda ap, h: ap[h * 64:(h + 1) * 64, :].rearrange("p (c j) -> p c j", j=64)
            for h in range(2):
                blk = lambda s: M8(s)[h * 64:(h + 1) * 64, :, h * 64:(h + 1) * 64]
                e1 = "dve" if h == 0 else "pool"
                op(e1, "tensor_tensor", blk(KB), c3(kap, h), c3(eLm, h), ALU.mult, r=[("R", 10), ("R", 14)], w=[("M", KB), ("M", KB + 1)])
                op(e1, "tensor_tensor", blk(BB), c3(b32, h), c3(einv, h), ALU.mult, r=[("R", 12), ("R", 15)], w=[("M", BB), ("M", BB + 1)])
                op(e1, "tensor_tensor", blk(KHB), c3(kh, h), c3(einv, h), ALU.mult, r=[("R", 11), ("R", 15)], w=[("M", KHB), ("M", KHB + 1)])
                op("pool", "tensor_copy", blk(VB), c3(v32, h), r=[("R", 5)], w=[("M", VB), ("M", VB + 1)])
            mk = lambda s: [("M", s), ("M", s + 1)]
            for c in range(8):
                kb, bbv, khb = M8(KB)[:, c, :], M8(BB)[:, c, :], M8(KHB)[:, c, :]
                rt = M(RT_)[:, c * 64:(c + 1) * 64]
                rd = mk(KB) + mk(BB) + mk(KHB) + [("M", RT_)]
                op("pe", "matmul", pb[3][:, 0:128], bbv, kb, start=True, stop=True, r=rd, w=[("pb", 3)])
                op("pe", "matmul", pb[3][:, 128:256], kb, bbv, start=True, stop=True, r=rd, w=[("pb", 3)])
                op("pe", "matmul", pb[3][:, 256:384], khb, kb, start=True, stop=True, r=rd, w=[("pb", 3)])
                op("pe", "matmul", pb[3][:, 384:448], bbv, rt, start=True, stop=True, r=rd, w=[("pb", 3)])
                op("pe", "matmul", pb[3][:, 448:512], khb, rt, start=True, stop=True, r=rd, w=[("pb", 3)])
                op("dve", "tensor_tensor", M(AM0 + c), pb[3][:, :], maskA, ALU.mult, r=[("pb", 3), "cst"], w=[("M", AM0 + c)])
            for src, dst in ((BB, BT), (KHB, KHT), (VB, VT)):
                for c in range(8):
                    op("pe", "transpose", pb7[:, c * 128:(c + 1) * 128], M8(src)[:, c, :], ident_b, r=mk(src) + ["cstb"], w=["pb7"])
                op("act", "copy", Mt[:, dst:dst + 2, :].rearrange("p a t -> p (a t)"), pb7[:, :], r=["pb7"], w=mk(dst))
            for g in range(2):
                cs = [4 * g + j for j in range(4)]
                amk = [("M", AM0 + c) for c in cs]
                AMg = Mt[:, AM0 + 4 * g:AM0 + 4 * g + 4, :]
                op("dve", "tensor_tensor", M4(TA), AMg[:, :, 0:128], ident_b.unsqueeze(1).to_broadcast([128, 4, 128]), ALU.add,
                   r=amk + ["cstb"], w=[("M", TA)])
                q_of = lambda j: AMg[:, j, 0:128]
                p_of = lambda j: AMg[:, j, 128:256]
                qk, pk_, tcur = amk, amk, TA
                qs, ps_ = [QA, QB], [PA, PB]
                for l in range(1, 6):
                    qn, pn = qs[l % 2], ps_[l % 2]
                    tn = TB if tcur == TA else TA
                    if l < 5:
                        for j in range(4):
                            op("pe", "matmul", pb[5][:, j * 128:(j + 1) * 128], p_of(j), q_of(j), start=True, stop=True, r=qk + pk_, w=[("pb", 5)])
                        op("act", "copy", M(qn), pb[5][:, :], r=[("pb", 5)], w=[("M", qn)])
                    for j in range(4):
                        op("pe", "matmul", pb[6][:, j * 128:(j + 1) * 128], q_of(j), p_of(j), start=True, stop=True, r=qk + pk_, w=[("pb6", i_) for i_ in range(4)])
                    op("dve", "tensor_copy", M(pn), pb[6][:, :], r=[("pb6", i_) for i_ in range(4)], w=[("M", pn)])
                    for j in range(4):
                        op("pe", "matmul", pb[4][:, j * 128:(j + 1) * 128], M4(pn)[:, j, :], M4(tcur)[:, j, :], start=True, stop=False,
                           r=[("M", pn), ("M", tcur)], w=[("pb", 4)])
                        op("pe", "matmul", pb[4][:, j * 128:(j + 1) * 128], ident_b, M4(tcur)[:, j, :], start=False, stop=True,
                           r=["cstb", ("M", tcur)], w=[("pb", 4)])
                    if l < 5:
                        op("act", "copy", M(tn), pb[4][:, :], r=[("pb", 4)], w=[("M", tn)])
                        tcur = tn
                    else:
                        op("act", "copy", M(TTF + g), pb[4][:, :], r=[("pb", 4)], w=[("M", TTF + g)])
                    if l < 5:
                        q_of = (lambda qn_: (lambda j: M4(qn_)[:, j, :]))(qn)
                        qk = [("M", qn)]
                    p_of = (lambda pn_: (lambda j: M4(pn_)[:, j, :]))(pn)
                    pk_ = [("M", pn)]
            sbf = Sbf[:, hp, :]
            s32 = S32[:, hp, :]
            skey, s32key = ("Sbf", hp), ("S32", hp)
            xn, uu = M(XU)[:, 0:128], M(XU)[:, 128:256]
            for c in range(8):
                am = M(AM0 + c)
                amk1 = [("M", AM0 + c)]
                rt = M(RT_)[:, c * 64:(c + 1) * 64]
                vt, bt, kht = M8(VT)[:, c, :], M8(BT)[:, c, :], M8(KHT)[:, c, :]
                op("pe", "matmul", pb[6][:, 0:128], M8(KB)[:, c, :], sbf, start=True, stop=False, r=mk(KB) + [skey], w=[("pb6", 0)])
                op("pe", "matmul", pb[6][:, 0:128], am[:, 256:384], vt, start=False, stop=True, r=amk1 + mk(VT), w=[("pb6", 0)])
                op("act", "mul", xn, pb[6][:, 0:128], -1.0, r=[("pb6", 0)], w=[("XU", 0)])
                op("pe", "matmul", pb[6][:, 128:256], M8(TTF)[:, c, :], xn, start=True, stop=True, r=mk(TTF) + [("XU", 0)], w=[("pb6", 1)])
                op("dve", "tensor_copy", uu, pb[6][:, 128:256], r=[("pb6", 1)], w=[("XU", 1)])
                ysl = pb[4][:, c * 64:(c + 1) * 64]
                op("pe", "matmul", ysl, sbf, rt, start=True, stop=False, r=[skey, ("M", RT_)], w=[("pb", 4)])
                op("pe", "matmul", ysl, uu, am[:, 384:448], start=False, stop=False, r=[("XU", 1)] + amk1, w=[("pb", 4)])
                op("pe", "matmul", ysl, vt, am[:, 448:512], start=False, stop=True, r=mk(VT) + amk1, w=[("pb", 4)])
                op("pe", "matmul", pb[6][:, 256:384], bt, uu, start=True, stop=False, r=mk(BT) + [("XU", 1)], w=[("pb6", 2)])
                op("pe", "matmul", pb[6][:, 256:384], kht, vt, start=False, stop=True, r=mk(KHT) + mk(VT), w=[("pb6", 2)])
                op("dve", "tensor_tensor", s32, s32, pb[6][:, 256:384], ALU.add, r=[s32key, ("pb6", 2)], w=[s32key])
                op("dve", "tensor_scalar", s32, s32, R5(13)[:, c * 64 + 63:c * 64 + 64], None, ALU.mult, r=[s32key, ("R", 13)], w=[s32key])
                op("act", "copy", sbf, s32, r=[s32key], w=[skey])
            op("act", "copy", R5(19), pb[4][:, :], r=[("pb", 4)], w=[("R", 19)])
            op("act", "activation", R5(20), R5(19), AF.Square, r=[("R", 19)], w=[("R", 20)])
            op("pe", "matmul", pb[2][:, :], bdones, R5(19), start=True, stop=True, r=["cst", ("R", 19)], w=[("pb", 2)])
            op("act", "mul", R5(22), pb[2][:, :], 1.0 / 64, r=[("pb", 2)], w=[("R", 22)])
            op("pe", "matmul", pb[2][:, :], bdones, R5(20), start=True, stop=True, r=["cst", ("R", 20)], w=[("pb", 2)])
            op("dve", "tensor_tensor", R5(20), R5(22), R5(22), ALU.mult, r=[("R", 22)], w=[("R", 20)])
            op("dve", "scalar_tensor_tensor", R5(20), pb[2][:, :], 1.0 / 64, R5(20), ALU.mult, ALU.subtract, r=[("pb", 2), ("R", 20)], w=[("R", 20)])
            op("dve", "tensor_scalar", R5(20), R5(20), 64e-5, None, ALU.add, r=[("R", 20)], w=[("R", 20)])
            op("act", "activation", R5(20), R5(20), AF.Sqrt, r=[("R", 20)], w=[("R", 20)])
            op("dve", "reciprocal", R5(20), R5(20), r=[("R", 20)], w=[("R", 20)])
            op("dve", "tensor_tensor", R5(19), R5(19), R5(22), ALU.subtract, r=[("R", 19), ("R", 22)], w=[("R", 19)])
            op("dve", "tensor_tensor", R5(19), R5(19), R5(20), ALU.mult, r=[("R", 19), ("R", 20)], w=[("R", 19)])
            op("dve", "tensor_scalar", R5(19), R5(19), P(8), P(9), ALU.mult, ALU.add, r=[("R", 19), "pc"], w=[("R", 19)])
            op("dve", "tensor_tensor", R5(19), R5(19), R5(16), ALU.add, r=[("R", 19), ("R", 16)], w=[("R", 19)])
            op("dve", "tensor_tensor", M(YO), R5(19), sg, ALU.mult, r=[("R", 19), ("R", 6)], w=[("M", YO)])
            dma("sp", zsend[tb][hp * 128:(hp + 1) * 128, :], M(YO), r=[("M", YO)], w=[("zsend", tb)], key="zs")
        if n_conv_blocks:
            S.dma("pool", lambda e: e.collective_compute("AllGather", ALU.bypass, replica_groups=[[0, 1], [2, 3], [4, 5], [6, 7]],
                                                         ins=[zsend[tb]], outs=[zall[tb]]), reads=[("zsend", tb)], writes=[("zall", tb)],
                  key=("cc", tb), inc=1)

    for tb in range(n_rwkv_blocks):
        rwkv_block(tb)

    if n_conv_blocks:
        for tb in range(n_rwkv_blocks):
            op("pool", "memset", small[:, 8:9], 0.0, r=[("zall", tb)], w=[("zall", tb), "sm8"])

    CT = lambda cc: R5(cc)
    SV = lambda cc: Rt[:, 16 + cc // 2, :].bitcast(BF16)[:, (cc % 2) * 512:(cc % 2) * 512 + 512]
    skey_ = lambda cc: ("R", 16 + cc // 2)
    UB = [24, 25]
    YR0, YC0 = 0, 16

    def conv_block(b):
        first = (b == 0)
        if first:
            build_hT(xc, [(0, 32)] + [(32 + tt * 128, 128) for tt in range(4)], 0)
        else:
            build_hT(xc, [(HALO + b * 512 + tt * 128, 128) for tt in range(4)], 32)
        for cc in range(16):
            cp = lambda c: pc[:, 82 + cc * 4 + c:82 + cc * 4 + c + 1]
            ub = UB[cc % 2]
            U = Rt[:, ub, 0:544]
            wv = load_w(wc[2 * cc])
            proj(wv, 128, 32, hcols, pb[0][:, :], ("pb", 0))
            if first:
                proj(wv, 128, 32, lambda k: hT[:, k, 0:32], pb[2][:, 0:32], ("pb", 2))
            wg = load_w(wc[2 * cc + 1])
            proj(wg, 128, 32, hcols, pb[1][:, :], ("pb", 1))
            if first:
                proj(wg, 128, 32, lambda k: hT[:, k, 0:32], pb[2][:, 32:64], ("pb", 2))
            op("act", "activation", XT[:, 0, 0:512], pb[1][:, :], AF.Sigmoid, r=[("pb", 1)], w=[("XT", 0)])
            op("dve", "tensor_tensor", U[:, 32:544], pb[0][:, :], XT[:, 0, 0:512], ALU.mult, r=[("pb", 0), ("XT", 0)], w=[("R", ub)])
            if first:
                op("act", "activation", XT[:, 1, 0:32], pb[2][:, 32:64], AF.Sigmoid, r=[("pb", 2)], w=[("XT", 1)])
                op("dve", "tensor_tensor", U[:, 0:32], pb[2][:, 0:32], XT[:, 1, 0:32], ALU.mult, r=[("pb", 2), ("XT", 1)], w=[("R", ub)])
            else:
                op("act", "copy", U[:, 0:32], ucar[:, cc, :], r=[("ucar", cc)], w=[("R", ub)])
            op("act", "copy", ucar[:, cc, :], U[:, 512:544], r=[("R", ub)], w=[("ucar", cc)])
            ce = "dve"
            wk = lambda k: pc[:, 146 + cc * 31 + k:146 + cc * 31 + k + 1]
            op(ce, "tensor_scalar", CT(cc), U[:, 2:514], wk(0), cp(0), ALU.mult, ALU.add, r=[("R", ub), "pc"], w=[("R", cc)])
            for k in range(1, 31):
                op(ce, "scalar_tensor_tensor", CT(cc), U[:, 2 + k:514 + k], wk(k), CT(cc), ALU.mult, ALU.add,
                   r=[("R", ub), ("R", cc), "pc"], w=[("R", cc)])
            op("act", "activation", XT[:, 2, 0:512], CT(cc), AF.Square, r=[("R", cc)], w=[("XT", 2)])
            op("pe", "matmul", pb[3][:, :], ones_f, CT(cc), start=(cc == 0), stop=(cc == 15), r=["cst", ("R", cc)], w=[("pb", 3)])
            op("pe", "matmul", pb[4][:, :], ones_f, XT[:, 2, 0:512], start=(cc == 0), stop=(cc == 15), r=["cst", ("XT", 2)], w=[("pb", 4)])
        mean, rstd = XT[:, 0, 0:512], XT[:, 1, 0:512]
        op("act", "mul", mean, pb[3][:, :], 1.0 / 2048, r=[("pb", 3)], w=[("XT", 0)])
        op("dve", "tensor_tensor", XT[:, 2, 0:512], mean, mean, ALU.mult, r=[("XT", 0)], w=[("XT", 2)])
        op("dve", "scalar_tensor_tensor", rstd, pb[4][:, :], 1.0 / 2048, XT[:, 2, 0:512], ALU.mult, ALU.subtract, r=[("pb", 4), ("XT", 2)], w=[("XT", 1)])
        op("dve", "tensor_scalar", rstd, rstd, 1e-5, None, ALU.add, r=[("XT", 1)], w=[("XT", 1)])
        op("act", "activation", rstd, rstd, AF.Sqrt, r=[("XT", 1)], w=[("XT", 1)])
        op("dve", "reciprocal", rstd, rstd, r=[("XT", 1)], w=[("XT", 1)])
        for cc in range(16):
            cp = lambda c: pc[:, 82 + cc * 4 + c:82 + cc * 4 + c + 1]
            op("dve", "tensor_tensor", CT(cc), CT(cc), mean, ALU.subtract, r=[("R", cc), ("XT", 0)], w=[("R", cc)])
            op("pool", "tensor_tensor", CT(cc), CT(cc), rstd, ALU.mult, r=[("R", cc), ("XT", 1)], w=[("R", cc)])
            op("act", "activation", SV(cc), CT(cc), AF.Silu, bias=cp(2), scale=cp(1), r=[("R", cc), "pc"], w=[skey_(cc)])
        for e_ in range(16):
            cp = lambda c: pc[:, 82 + e_ * 4 + c:82 + e_ * 4 + c + 1]
            wgc = load_w(wc[32 + e_])
            proj(wgc, 128, 32, hcols, pb[0][:, :], ("pb", 0))
            op("act", "activation", XT[:, 2, 0:512], pb[0][:, :], AF.Silu, r=[("pb", 0)], w=[("XT", 2)])
            wpi = load_w(wp[e_], 2048)
            for k in range(16):
                op("pe", "matmul", pb[1][:, :], wbuf[wpi][:, k * 128:(k + 1) * 128], SV(k), start=(k == 0), stop=(k == 15),
                   r=[("w", wpi), skey_(k)], w=[("pb", 1)])
            op("dve", "scalar_tensor_tensor", M(YC0 + e_), pb[1][:, :], cp(3), XT[:, 2, 0:512], ALU.add, ALU.mult,
               r=[("pb", 1), ("XT", 2), "pc"], w=[("M", YC0 + e_)])
        for k in range(16):
            sl = k % 2
            dma("sp", ystg[:, 2 * sl, :], zall[b][k * 128:(k + 1) * 128, :], r=[("zall", b)], w=[("ys", 2 * sl)], key=("ys", 2 * sl))
            op("pool", "tensor_scalar", M(YR0 + k), ystg[:, 2 * sl, :], sel[:, 0:1], None, ALU.mult, r=[("ys", 2 * sl), "sel"], w=[("M", YR0 + k)])
            if 4 + b >= n_rwkv_blocks:
                continue
            dma("sp", ystg[:, 2 * sl + 1, :], zall[4 + b][k * 128:(k + 1) * 128, :], r=[("zall", 4 + b)], w=[("ys", 2 * sl + 1)],
                key=("ys", 2 * sl + 1))
            op("pool", "tensor_scalar", ystg[:, 2 * sl + 1, :], ystg[:, 2 * sl + 1, :], sel[:, 1:2], None, ALU.mult,
               r=[("ys", 2 * sl + 1), "sel"], w=[("ys", 2 * sl + 1)])
            op("pool", "tensor_tensor", M(YR0 + k), M(YR0 + k), ystg[:, 2 * sl + 1, :], ALU.add,
               r=[("ys", 2 * sl + 1), ("M", YR0 + k)], w=[("M", YR0 + k)])
        for dch in range(32):
            wi = load_w(wo[dch])
            k_ = dch % 2
            for k in range(32):
                op("pe", "matmul", pb[k_][:, :], wbuf[wi][:, k * 128:(k + 1) * 128], M(k), start=(k == 0), stop=(k == 31),
                   r=[("w", wi), ("M", k)], w=[("pb", k_)])
            op("act", "copy", R5(dch), pb[k_][:, :], r=[("pb", k_)], w=[("R", dch)])
            op("act", "activation", XT[:, dch % 2, 0:512], pb[k_][:, :], AF.Square, r=[("pb", k_)], w=[("XT", dch % 2)])
            op("pe", "matmul", pb[5][:, :], ones_f, XT[:, dch % 2, 0:512], start=(dch == 0), stop=(dch == 31), r=["cst", ("XT", dch % 2)], w=[("pb", 5)])
        rs = XT[:, 2, 0:512]
        op("dve", "tensor_scalar", rs, pb[5][:, :], 1.0 / D, 1e-6, ALU.mult, ALU.add, r=[("pb", 5)], w=[("XT", 2)])
        op("act", "activation", rs, rs, AF.Sqrt, r=[("XT", 2)], w=[("XT", 2)])
        op("dve", "reciprocal", rs, rs, r=[("XT", 2)], w=[("XT", 2)])
        for dch in range(32):
            op("dve", "scalar_tensor_tensor", R5(dch), R5(dch), pc[:, 674 + dch:675 + dch], rs, ALU.mult, ALU.mult,
               r=[("R", dch), ("XT", 2), "pc"], w=[("R", dch)])
        for tt in range(4):
            row_x = HALO + b * 512 + tt * 128
            dma("sp", io32[:, :], xc[row_x:row_x + 128, :], w=["io32"], key="io32")
            for g8 in range(8):
                k_ = g8 % 2
                for j in range(4):
                    dch = g8 * 4 + j
                    op("pe", "transpose", pb[k_][:, j * 128:(j + 1) * 128], R5(dch)[:, tt * 128:(tt + 1) * 128], ident_f,
                       r=[("R", dch), "cst"], w=[("pb", k_)])
                op("dve", "tensor_tensor", io32[:, g8 * 512:(g8 + 1) * 512], io32[:, g8 * 512:(g8 + 1) * 512], pb[k_][:, :], ALU.add,
                   r=["io32", ("pb", k_)], w=["io32"])
            row_y = b * 512 + tt * 128
            dma("sp", yout[row_y:row_y + 128, :], io32[:, :], r=["io32"], key="yout")

    for b in range(n_conv_blocks):
        conv_block(b)

    counts = S.emit(final_waits=["yout"] if n_conv_blocks else ["zs"])
    st.close()
    return nc, counts


def _chunk_layout(wcols):
    m = wcols.shape[1]
    nk = wcols.shape[0] // 128
    a = np.zeros((nk, 128, 128), np.float32)
    a[:, :, :m] = wcols.reshape(nk, 128, m)
    return np.ascontiguousarray(a.transpose(1, 0, 2)).reshape(128, nk * 128)


def _consts():
    cst = np.zeros((128, 1536), np.float32)
    cst[:, 0:128] = np.eye(128, dtype=np.float32)
    idx = np.arange(128)
    same = (idx[:, None] // 64) == (idx[None, :] // 64)
    cst[:, 128:256] = same.astype(np.float32)
    cst[:, 256:384] = 1.0
    s = idx[:, None] % 64
    t = idx[None, :] % 64
    cst[:, 384:512] = -(same & (s < t)).astype(np.float32)
    cst[:, 512:640] = -(same & (s > t)).astype(np.float32)
    cst[:, 640:768] = (same & (s < t)).astype(np.float32)
    incl = (s[:, 0:1] <= np.arange(64)[None, :]).astype(np.float32)
    cst[:, 768:832] = incl
    cst[:, 832:896] = incl
    r = np.ones(512, np.float32)
    r[0::64] = 0.0
    cst[:, 896:1408] = r[None, :]
    return cst


_CACHE = {}
_NB = (8, 4)


def kernel(x, norm_pre_g, w_in, mu_shift, w0, w_lora_up, a0, a_lora_up, k_k, k_a, r_k, lnx_g, lnx_b, conv_w, conv_b,
           cln_g, cln_b, w_pw2, b_pw2, w_out, norm_post_g):
    f = lambda a: np.asarray(a, dtype=np.float32)
    x, w_in, w_out, w_pw2 = f(x), f(w_in), f(w_out), f(w_pw2)
    mu_shift, w0, a0, k_k, k_a = f(mu_shift), f(w0), f(a0), f(k_k), f(k_a)
    r_k, lnx_g, lnx_b = f(r_k).reshape(-1), f(lnx_g), f(lnx_b)
    conv_w, conv_b, cln_g, cln_b, b_pw2 = f(conv_w), f(conv_b), f(cln_g), f(cln_b), f(b_pw2)
    norm_pre_g, norm_post_g = f(norm_pre_g), f(norm_post_g)
    w_lora_up, a_lora_up = f(w_lora_up), f(a_lora_up)

    if "nc" not in _CACHE:
        _CACHE["nc"] = build_program(*_NB)[0]
    nc = _CACHE["nc"]

    cst = _consts()
    wc_l = np.empty((48, 128, 4096), np.float32)
    c1 = 6336 + 2048
    for cc in range(16):
        wc_l[2 * cc] = _chunk_layout(w_in[:, c1 + cc * 128:c1 + (cc + 1) * 128])
        wc_l[2 * cc + 1] = _chunk_layout(w_in[:, c1 + 2048 + cc * 128:c1 + 2048 + (cc + 1) * 128])
        wc_l[32 + cc] = _chunk_layout(w_in[:, c1 + 4096 + cc * 128:c1 + 4096 + (cc + 1) * 128])
    wp_l = np.stack([_chunk_layout(w_pw2[:, e * 128:(e + 1) * 128]) for e in range(16)])
    wo_l = np.stack([_chunk_layout(w_out[:, dch * 128:(dch + 1) * 128]) for dch in range(32)])

    in_maps = []
    for c in range(NCORES):
        b, hh = c // 2, c % 2
        ch0 = hh * 1024
        wr_l = np.empty((34, 128, 4096), np.float32)
        wr_l[0] = _chunk_layout(w_in[:, 6144:6240])
        wr_l[1] = _chunk_layout(w_in[:, 6240:6336])
        pcm = np.zeros((128, NPC), np.float32)
        for hp in range(8):
            lo = ch0 + hp * 128
            for j in range(3):
                wr_l[2 + hp * 4 + j] = _chunk_layout(w_in[:, j * 2048 + lo:j * 2048 + lo + 128])
            wr_l[2 + hp * 4 + 3] = _chunk_layout(w_in[:, 6336 + lo:6336 + lo + 128])
            cols = [mu_shift[lo:lo + 128], mu_shift[2048 + lo:2048 + lo + 128], mu_shift[4096 + lo:4096 + lo + 128],
                    w0[lo:lo + 128], a0[lo:lo + 128], k_k[lo:lo + 128], k_a[lo:lo + 128], r_k[lo:lo + 128],
                    lnx_g[lo:lo + 128], lnx_b[lo:lo + 128]]
            for j, v in enumerate(cols):
                pcm[:, hp * 10 + j] = v
        pcm[0:96, 80] = mu_shift[6144:6240]
        pcm[0:96, 81] = mu_shift[6240:6336]
        for cc in range(16):
            sl = slice(cc * 128, (cc + 1) * 128)
            pcm[:, 82 + cc * 4 + 0] = conv_b[sl]
            pcm[:, 82 + cc * 4 + 1] = cln_g[sl]
            pcm[:, 82 + cc * 4 + 2] = cln_b[sl]
            pcm[:, 82 + cc * 4 + 3] = b_pw2[sl]
            pcm[:, 146 + cc * 31:146 + (cc + 1) * 31] = conv_w[:, sl].T
        pcm[:, 642:674] = norm_pre_g.reshape(32, 128).T
        pcm[:, 674:706] = norm_post_g.reshape(32, 128).T
        lor = np.concatenate([w_lora_up[:, ch0:ch0 + 1024], a_lora_up[:, ch0:ch0 + 1024]], axis=1)
        xcv = np.zeros((HALO + TOKC, D), np.float32)
        t0 = hh * TOKC
        if hh == 1:
            xcv[0:HALO] = x[b, t0 - HALO:t0]
        xcv[HALO:] = x[b, t0:t0 + TOKC]
        selv = np.zeros((128, 2), np.float32)
        selv[:, hh] = 1.0
        in_maps.append({"xf": np.ascontiguousarray(x[b]), "xc": xcv, "wr": wr_l, "wc": wc_l, "wp": wp_l, "wo": wo_l,
                        "lora": np.ascontiguousarray(lor), "pc": pcm, "cst": cst, "sel": selv})
    res = run_bass_kernel_spmd(nc, in_maps, core_ids=list(range(NCORES)))
    out = np.empty((4, T, D), np.float32)
    for c in range(NCORES):
        b, hh = c // 2, c % 2
        out[b, hh * TOKC:(hh + 1) * TOKC] = res.results[c]["y"]
    return out
```

```python
import contextlib
import numpy as np
import concourse.bass as bass
import concourse.mybir as mybir
from concourse.bass_utils import run_bass_kernel_spmd

F32 = mybir.dt.float32
BF16 = mybir.dt.bfloat16
AF = mybir.ActivationFunctionType
ALU = mybir.AluOpType

D = 4096
T = 4096
NCORES = 8
TOKC = 2048
HALO = 32
NPC = 720
C0 = 0.6065306597126334
EPOCH = 20000


class Sched:
    def __init__(self, nc):
        self.nc = nc
        self.ops = []
        self.last_w = {}
        self.readers = {}
        self.dma_keys = {}

    def _deps(self, i, reads, writes):
        deps = set()
        for k in reads:
            w = self.last_w.get(k)
            if w is not None:
                deps.add((w, "raw"))
        for k in writes:
            w = self.last_w.get(k)
            if w is not None:
                deps.add((w, "waw"))
            for r in self.readers.get(k, ()):
                deps.add((r, "war"))
        for k in reads:
            self.readers.setdefault(k, []).append(i)
        for k in writes:
            self.last_w[k] = i
            self.readers[k] = []
        return deps

    def op(self, eng, fn, reads=(), writes=()):
        i = len(self.ops)
        self.ops.append(dict(eng=eng, fn=fn, deps=self._deps(i, reads, writes), dma=None))

    def dma(self, eng, fn, reads=(), writes=(), key=None, inc=16):
        i = len(self.ops)
        deps = self._deps(i, reads, writes)
        n = self.dma_keys.get(key, (0, inc))[0] + 1
        self.dma_keys[key] = (n, inc)
        self.ops.append(dict(eng=eng, fn=fn, deps=deps, dma=(key, n, inc)))

    def emit(self, final_waits=()):
        nc = self.nc
        ops = self.ops
        need = [False] * len(ops)
        for i, o in enumerate(ops):
            for (d, kind) in o["deps"]:
                p = ops[d]
                if p["dma"] is not None:
                    continue
                if p["eng"] == o["eng"] and (p["eng"] == "pe" or kind != "raw"):
                    continue
                need[d] = True
        cnt = {e: 0 for e in ("pe", "act", "dve", "pool", "sp")}
        sig = [None] * len(ops)
        for i, o in enumerate(ops):
            if need[i]:
                e = o["eng"]
                sig[i] = (e, cnt[e] // EPOCH, cnt[e] % EPOCH + 1)
                cnt[e] += 1
        stack = contextlib.ExitStack()
        sems = {}
        for e in cnt:
            for ep in range((cnt[e] + EPOCH - 1) // EPOCH):
                sems[(e, ep)] = stack.enter_context(nc.semaphore(f"c_{e}_{ep}"))
        dsems = {}
        for k in self.dma_keys:
            dsems[k] = stack.enter_context(nc.semaphore(f"d_{len(dsems)}"))
        per_eng = {e: [] for e in cnt}
        for i, o in enumerate(ops):
            per_eng[o["eng"]].append(i)
        block = stack.enter_context(nc.Block())
        engs = {"pe": nc.tensor, "act": nc.scalar, "dve": nc.vector, "pool": nc.gpsimd, "sp": nc.sync}

        def run_engine(ename, final):
            eng = engs[ename]
            seen = {}
            for i in per_eng[ename]:
                o = ops[i]
                waits = {}
                for (d, kind) in o["deps"]:
                    p = ops[d]
                    if p["dma"] is not None:
                        key, n, inc = p["dma"]
                        s, v = ("d", key), inc * n
                    else:
                        if sig[d] is None:
                            continue
                        if p["eng"] == ename and (ename == "pe" or kind != "raw"):
                            continue
                        e, ep, v = sig[d]
                        s = (e, ep)
                    if seen.get(s, 0) >= v:
                        continue
                    waits[s] = max(waits.get(s, 0), v)
                for s, v in waits.items():
                    seen[s] = v
                    eng.wait_ge(dsems[s[1]] if s[0] == "d" else sems[s], v)
                ins = o["fn"](eng)
                if o["dma"] is not None:
                    ins.then_inc(dsems[o["dma"][0]], o["dma"][2])
                elif sig[i] is not None:
                    ins.then_inc(sems[sig[i][:2]], 1)
            for key in final:
                n, inc = self.dma_keys[key]
                eng.wait_ge(dsems[key], inc * n)

        block.tensor(lambda e: run_engine("pe", ()))
        block.scalar(lambda e: run_engine("act", ()))
        block.vector(lambda e: run_engine("dve", ()))
        block.gpsimd(lambda e: run_engine("pool", ()))
        block.sync(lambda e: run_engine("sp", final_waits))
        stack.close()
        return {e: len(per_eng[e]) for e in per_eng}


def build_program(n_rwkv_blocks=8, n_conv_blocks=4):
    nc = bass.Bass("TRN2", target_bir_lowering=False)
    dt_in = lambda name, shape: nc.dram_tensor(name, shape, F32, kind="ExternalInput").ap()
    xf = dt_in("xf", [T, D])
    xc = dt_in("xc", [HALO + TOKC, D])
    wr = dt_in("wr", [34, 128, 4096])
    wc = dt_in("wc", [48, 128, 4096])
    wp = dt_in("wp", [16, 128, 2048])
    wo = dt_in("wo", [32, 128, 4096])
    lora = dt_in("lora", [96, 2048])
    pcd = dt_in("pc", [128, NPC])
    cstd = dt_in("cst", [128, 1408])
    seld = dt_in("sel", [128, 2])
    yout = nc.dram_tensor("y", [TOKC, D], F32, kind="ExternalOutput").ap()
    zsend = [nc.dram_tensor(f"zsend{i}", [1024, 512], BF16).ap() for i in range(8)]
    zall = [nc.dram_tensor(f"zall{i}", [2048, 512], BF16).ap() for i in range(8)]

    S = Sched(nc)
    st = contextlib.ExitStack()
    sb = lambda name, shape, dt: st.enter_context(nc.sbuf_tensor("s_" + name, shape, dt))
    ps = lambda name, shape, dt: st.enter_context(nc.psum_tensor(name, shape, dt))

    NR, NM = 32, 36
    hT = sb("hT", [128, 32, 544], BF16)
    NW = 3
    wbuf = [sb(f"wbuf{i}", [128, 4096], BF16) for i in range(NW)]
    io32 = sb("io32", [128, 4096], F32)
    Rt = sb("Rt", [128, NR, 544], F32)
    Mt = sb("Mt", [128, NM, 512], BF16)
    xbf = Mt[:, 28:36, :].rearrange("p a t -> p (a t)")
    XBK = [("M", i_) for i_ in range(28, 36)]
    XT = sb("XT", [128, 3, 512], F32)
    pc = sb("pc", [128, NPC], F32)
    cst = sb("cst", [128, 1408], F32)
    cstb = sb("cstb", [128, 128], BF16)
    lor = sb("lor", [96, 2, 256], F32)
    sel = sb("sel", [128, 2], F32)
    S32 = sb("S32", [128, 8, 128], F32)
    Sbf = sb("Sbf", [128, 8, 128], BF16)
    carry = sb("carry", [128, 32], F32)
    ucar = sb("ucar", [128, 16, 32], F32)
    small = sb("small", [128, 16], F32)
    ystg = sb("ystg", [128, 4, 512], BF16)
    pb = [ps(f"pb{i}", [128, 512], F32) for i in range(7)]
    pb7 = ps("pb7", [128, 1024], BF16)

    def R(i):
        return Rt[:, i, :]

    def R5(i):
        return Rt[:, i, 0:512]

    def M(i):
        return Mt[:, i, :]

    def M8(i):
        return Mt[:, i:i + 2, :].rearrange("p a (c t) -> p (a c) t", t=128)

    def M4(i):
        return Mt[:, i, :].rearrange("p (c t) -> p c t", t=128)

    ident_f = cst[:, 0:128]
    bdones = cst[:, 128:256]
    ones_f = cst[:, 256:384]
    maskA = cst[:, 384:896]
    rst = cst[:, 896:1408]
    ident_b = cstb[:, 0:128]

    def op(engn, meth, *args, r=(), w=(), **kw):
        S.op(engn, lambda e: getattr(e, meth)(*args, **kw), reads=list(r), writes=list(w))

    def dma(engn, out, in_, r=(), w=(), key=None):
        S.dma(engn, lambda e: e.dma_start(out=out, in_=in_), reads=list(r), writes=list(w), key=key)

    dma("sp", pc[:], pcd, w=["pc"], key="pc")
    dma("sp", cst[:], cstd, w=["cst"], key="cst")
    dma("sp", sel[:], seld, w=["sel"], key="sel")
    dma("pool", cstb[:], cstd[:, 0:128], w=["cstb"], key="cstb")
    op("dve", "memset", S32[:], 0.0, w=[("S32", i) for i in range(8)])
    op("dve", "memset", Sbf[:], 0.0, w=[("Sbf", i) for i in range(8)])
    op("dve", "memset", carry[:], 0.0, w=["carry"])
    op("pool", "memset", Mt[:, 0:8, :], 0.0, w=[("M", i) for i in range(8)])
    for hp in range(8):
        op("dve", "tensor_scalar", pc[:, 706 + hp:707 + hp], pc[:, hp * 10 + 6:hp * 10 + 7], -1.0, 1.0, ALU.mult, ALU.add,
           r=["pc"], w=["pc"])

    wctr = [0]

    def load_w(src, ncols=4096):
        i = wctr[0] % NW
        wctr[0] += 1
        dma("pool", wbuf[i][:, 0:ncols], src, w=[("w", i)], key=("w", i))
        return i

    def proj(wi, M_, nk, rhs_fn, out_ap, wkey):
        for k in range(nk):
            op("pe", "matmul", out_ap, wbuf[wi][:, k * 128:k * 128 + M_], rhs_fn(k), start=(k == 0), stop=(k == nk - 1),
               r=[("w", wi), "hT"], w=[wkey])

    def build_hT(xsrc, tiles, col0):
        col = col0
        for (r0, n) in tiles:
            dma("sp", io32[0:n, :], xsrc[r0:r0 + n, :], w=["io32"], key="io32")
            op("act", "activation", xbf[0:n, :], io32[0:n, :], AF.Square, accum_out=small[0:n, 0:1], r=["io32"], w=XBK + ["sm0"])
            op("dve", "tensor_scalar", small[0:n, 1:2], small[0:n, 0:1], 1.0 / D, 1e-6, ALU.mult, ALU.add, r=["sm0"], w=["sm1"])
            op("act", "activation", small[0:n, 1:2], small[0:n, 1:2], AF.Sqrt, r=["sm1"], w=["sm1"])
            op("dve", "reciprocal", small[0:n, 2:3], small[0:n, 1:2], r=["sm1"], w=["sm2"])
            op("dve", "tensor_scalar", xbf[0:n, :], io32[0:n, :], small[0:n, 2:3], None, ALU.mult, r=["io32", "sm2"], w=XBK)
            for kg in range(4):
                for kk in range(8):
                    k = kg * 8 + kk
                    op("pe", "transpose", pb7[:, kk * 128:kk * 128 + n], xbf[0:n, k * 128:(k + 1) * 128], ident_b[0:n, 0:n],
                       r=XBK + ["cstb"], w=["pb7"])
                for kk in range(8):
                    k = kg * 8 + kk
                    if kk % 2 == 0:
                        op("act", "mul", hT[:, k, col:col + n], pb7[:, kk * 128:kk * 128 + n], pc[:, 642 + k:643 + k],
                           r=["pb7", "pc"], w=["hT"])
                    else:
                        op("dve", "tensor_scalar", hT[:, k, col:col + n], pb7[:, kk * 128:kk * 128 + n], pc[:, 642 + k:643 + k], None,
                           ALU.mult, r=["pb7", "pc"], w=["hT"])
            col += n

    pbi = [0]

    def next_pb():
        pbi[0] ^= 1
        return pbi[0]

    def shift(pbk, np_, raw, ccol, mu, tmp, out):
        op("act", "copy", R(raw)[0:np_, 1:513], pb[pbk][0:np_, :], r=[("pb", pbk)], w=[("R", raw)])
        op("act", "copy", R(raw)[0:np_, 0:1], carry[0:np_, ccol:ccol + 1], r=["carry"], w=[("R", raw)])
        op("dve", "tensor_tensor", R5(tmp)[0:np_], R(raw)[0:np_, 0:512], R(raw)[0:np_, 1:513], ALU.subtract, r=[("R", raw)], w=[("R", tmp)])
        op("dve", "scalar_tensor_tensor", R5(out)[0:np_], R5(tmp)[0:np_], mu, R(raw)[0:np_, 1:513], ALU.mult, ALU.add,
           r=[("R", tmp), ("R", raw), "pc"], w=[("R", out)])
        op("dve", "tensor_copy", carry[0:np_, ccol:ccol + 1], R(raw)[0:np_, 512:513], r=[("R", raw)], w=["carry"])

    hcols = lambda k: hT[:, k, 32:544]
    KB, BB, KHB, VB = 0, 2, 4, 6
    RT_ = 8
    AM0 = 9
    QA, QB, PA, PB, TA, TB = 17, 18, 19, 20, 21, 22
    TTF, BT, KHT, VT = 23, 25, 27, 29
    XU, YO = 31, 32

    wct = sb("wct", [128, 2, 8], F32)

    class Filler:
        def __init__(self):
            self.gens = []

        def add(self, g):
            self.gens.append(g)

        def fill(self, n):
            while n > 0 and self.gens:
                try:
                    next(self.gens[0])
                    n -= 1
                except StopIteration:
                    self.gens.pop(0)

        def drain(self):
            self.fill(10 ** 9)

    def gproj(wi, M_, nk, rhs_fn, out_ap, wkey, step=4):
        for k in range(nk):
            op("pe", "matmul", out_ap, wbuf[wi][:, k * 128:k * 128 + M_], rhs_fn(k), start=(k == 0), stop=(k == nk - 1),
               r=[("w", wi), "hT"], w=[wkey])
            if k % step == step - 1:
                yield

    SG = (6, 23)
    BON = (16, 24)
    G1, G2, G3 = 25, 26, 27

    def gen_P(tb, hp):
        par = hp % 2
        P = lambda c: pc[:, hp * 10 + c:hp * 10 + c + 1]
        for j, outslot in enumerate((3, 4, 5)):
            wi = load_w(wr[2 + hp * 4 + j])
            k_ = next_pb()
            yield from gproj(wi, 128, 32, hcols, pb[k_][:, :], ("pb", k_))
            shift(k_, 128, j, hp * 3 + j, P(j), 9, outslot)
            yield
        wi = load_w(wr[2 + hp * 4 + 3])
        k_ = next_pb()
        yield from gproj(wi, 128, 32, hcols, pb[k_][:, :], ("pb", k_))
        op("act", "activation", R5(SG[par]), pb[k_][:, :], AF.Silu, r=[("pb", k_)], w=[("R", SG[par])])
        yield

    def gen_ER(tb, hp):
        par = hp % 2
        P = lambda c: pc[:, hp * 10 + c:hp * 10 + c + 1]
        r32, kx, v32 = R5(3), R5(4), R5(5)
        dma("sp", lor[:, par, 0:128], lora[:, hp * 128:(hp + 1) * 128], w=[("lor", par, 0)], key=("lor", par, 0))
        dma("sp", lor[:, par, 128:256], lora[:, 1024 + hp * 128:1024 + (hp + 1) * 128], w=[("lor", par, 1)], key=("lor", par, 1))
        op("pe", "matmul", pb[2][:, :], lor[:, par, 0:128], R5(17)[0:96], start=True, stop=True,
           r=[("lor", par, 0), ("R", 17)], w=[("pb", 2)])
        op("act", "activation", R5(7), pb[2][:, :], AF.Sigmoid, bias=P(3), r=[("pb", 2), "pc"], w=[("R", 7)])
        yield
        op("pe", "matmul", pb[2][:, :], lor[:, par, 128:256], R5(18)[0:96], start=True, stop=True,
           r=[("lor", par, 1), ("R", 18)], w=[("pb", 2)])
        op("act", "activation", R5(8), pb[2][:, :], AF.Sigmoid, bias=P(4), r=[("pb", 2), "pc"], w=[("R", 8)])
        yield
        sgw, a32 = R5(7), R5(8)
        op("act", "activation", R5(9), kx, AF.Square, scale=P(5), r=[("R", 4), "pc"], w=[("R", 9)])
        op("pe", "matmul", pb[2][:, :], bdones, R5(9), start=True, stop=True, r=["cst", ("R", 9)], w=[("pb", 2)])
        op("dve", "tensor_scalar", R5(9), pb[2][:, :], 1e-24, None, ALU.max, r=[("pb", 2)], w=[("R", 9)])
        yield
        op("act", "activation", R5(9), R5(9), AF.Sqrt, r=[("R", 9)], w=[("R", 9)])
        op("dve", "reciprocal", R5(9), R5(9), r=[("R", 9)], w=[("R", 9)])
        yield
        op("dve", "scalar_tensor_tensor", R5(10), kx, P(5), R5(9), ALU.mult, ALU.mult, r=[("R", 4), ("R", 9), "pc"], w=[("R", 10)])
        yield
        op("dve", "tensor_scalar", R5(20), a32, P(6), pc[:, 706 + hp:707 + hp], ALU.mult, ALU.add, r=[("R", 8), "pc"], w=[("R", 20)])
        yield
        op("dve", "tensor_tensor", R5(11), kx, R5(20), ALU.mult, r=[("R", 4), ("R", 20)], w=[("R", 11)])
        yield
        op("dve", "tensor_tensor", R5(12), R5(10), a32, ALU.mult, r=[("R", 10), ("R", 8)], w=[("R", 12)])
        yield
        bon = BON[par]
        op("dve", "scalar_tensor_tensor", R5(bon), r32, P(7), R5(11), ALU.mult, ALU.mult, r=[("R", 3), ("R", 11), "pc"], w=[("R", bon)])
        op("pe", "matmul", pb[2][:, :], bdones, R5(bon), start=True, stop=True, r=["cst", ("R", bon)], w=[("pb", 2)])
        yield
        op("dve", "tensor_tensor", R5(bon), pb[2][:, :], v32, ALU.mult, r=[("pb", 2), ("R", 5)], w=[("R", bon)])
        yield
        op("dve", "tensor_tensor_scan", R5(21), rst, sgw, 0.0, ALU.mult, ALU.add, r=["cst", ("R", 7)], w=[("R", 21)])
        yield
        op("act", "activation", R5(13), R5(21), AF.Exp, scale=-C0, r=[("R", 21)], w=[("R", 13)])
        op("dve", "tensor_tensor", R5(20), R5(21), sgw, ALU.subtract, r=[("R", 21), ("R", 7)], w=[("R", 20)])
        yield
        op("act", "activation", R5(14), R5(20), AF.Exp, scale=-C0, r=[("R", 20)], w=[("R", 14)])
        op("act", "activation", R5(15), R5(21), AF.Exp, scale=C0, r=[("R", 21)], w=[("R", 15)])
        yield

    def rwkv_unit_rest(tb, hp, filler):
        par = hp % 2
        P = lambda c: pc[:, hp * 10 + c:hp * 10 + c + 1]
        r32, kx, v32 = R5(3), R5(4), R5(5)
        kap, kh, b32 = R5(10), R5(11), R5(12)
        eL, eLm, einv = R5(13), R5(14), R5(15)
        wc_ = wct[:, par, :]
        wkey = ("wct", par)
        op("dve", "tensor_copy", wc_, eL.rearrange("p (c j) -> p c j", j=64)[:, :, 63], r=[("R", 13)], w=[wkey])
        op("dve", "tensor_tensor", R5(21).rearrange("p (c j) -> p c j", j=64), einv.rearrange("p (c j) -> p c j", j=64),
           wc_.unsqueeze(2).to_broadcast([128, 8, 64]), ALU.mult, r=[("R", 15), wkey], w=[("R", 21)])
        ehat = R5(21)
        op("dve", "tensor_tensor", M(RT_), r32, eL, ALU.mult, r=[("R", 3), ("R", 13)], w=[("M", RT_)])
        c3 = lambda ap, h: ap[h * 64:(h + 1) * 64, :].rearrange("p (c j) -> p c j", j=64)
        mk = lambda s: [("M", s), ("M", s + 1)]
        for h in range(2):
            blk = lambda s: M8(s)[h * 64:(h + 1) * 64, :, h * 64:(h + 1) * 64]
            e1 = "dve" if h == 0 else "pool"
            op(e1, "tensor_tensor", blk(KB), c3(kap, h), c3(eLm, h), ALU.mult, r=[("R", 10), ("R", 14)], w=mk(KB))
            op(e1, "tensor_tensor", blk(BB), c3(b32, h), c3(einv, h), ALU.mult, r=[("R", 12), ("R", 15)], w=mk(BB))
            op(e1, "tensor_tensor", blk(KHB), c3(kh, h), c3(einv, h), ALU.mult, r=[("R", 11), ("R", 15)], w=mk(KHB))
            op("pool", "tensor_copy", blk(VB), c3(v32, h), r=[("R", 5)], w=mk(VB))
        for c in range(8):
            kb, bbv, khb = M8(KB)[:, c, :], M8(BB)[:, c, :], M8(KHB)[:, c, :]
            rt = M(RT_)[:, c * 64:(c + 1) * 64]
            rd = mk(KB) + mk(BB) + mk(KHB) + [("M", RT_)]
            k3 = 3
            op("pe", "matmul", pb[k3][:, 0:128], bbv, kb, start=True, stop=True, r=rd, w=[("pb", k3)])
            op("pe", "matmul", pb[k3][:, 128:256], kb, bbv, start=True, stop=True, r=rd, w=[("pb", k3)])
            op("pe", "matmul", pb[k3][:, 256:384], khb, kb, start=True, stop=True, r=rd, w=[("pb", k3)])
            op("pe", "matmul", pb[k3][:, 384:448], bbv, rt, start=True, stop=True, r=rd, w=[("pb", k3)])
            op("pe", "matmul", pb[k3][:, 448:512], khb, rt, start=True, stop=True, r=rd, w=[("pb", k3)])
            op("dve" if c % 2 == 0 else "dve", "tensor_tensor", M(AM0 + c), pb[k3][:, :], maskA, ALU.mult, r=[("pb", k3), "cst"], w=[("M", AM0 + c)])
        for h in range(2):
            blk = lambda s: M8(s)[h * 64:(h + 1) * 64, :, h * 64:(h + 1) * 64]
            e1 = "dve" if h == 0 else "pool"
            op(e1, "tensor_tensor", blk(BB), c3(b32, h), c3(ehat, h), ALU.mult, r=[("R", 12), ("R", 21)], w=mk(BB))
            op(e1, "tensor_tensor", blk(KHB), c3(kh, h), c3(ehat, h), ALU.mult, r=[("R", 11), ("R", 21)], w=mk(KHB))
        LQ = [(QA, QB), (25, 26)]
        LP = [(PA, PB), (27, 28)]
        LT = [(TA, TB), (29, 30)]
        QBANK, PBANK, TBANK = (5, 3), (6, 2), (4, 5)
        pkeys = lambda bk: [("pb6", i_) for i_ in range(4)] if bk == 6 else [("pb", bk)]
        state = []
        for g in range(2):
            amk = [("M", AM0 + 4 * g + j) for j in range(4)]
            AMg = Mt[:, AM0 + 4 * g:AM0 + 4 * g + 4, :]
            op("dve", "tensor_tensor", M4(LT[g][0]), AMg[:, :, 0:128], ident_b.unsqueeze(1).to_broadcast([128, 4, 128]), ALU.add,
               r=amk + ["cstb"], w=[("M", LT[g][0])])
            state.append(dict(q_of=(lambda A: (lambda j: A[:, j, 0:128]))(AMg), p_of=(lambda A: (lambda j: A[:, j, 128:256]))(AMg),
                              qk=amk, pk=amk, tcur=LT[g][0]))
        for l in range(1, 6):
            for g in range(2):
                s_ = state[g]
                qn, pn = LQ[g][l % 2], LP[g][l % 2]
                if l < 5:
                    bk = QBANK[g]
                    for j in range(4):
                        op("pe", "matmul", pb[bk][:, j * 128:(j + 1) * 128], s_["p_of"](j), s_["q_of"](j), start=True, stop=True,
                           r=s_["qk"] + s_["pk"], w=pkeys(bk))
                    op("act", "copy", M(qn), pb[bk][:, :], r=pkeys(bk), w=[("M", qn)])
                bk = PBANK[g]
                for j in range(4):
                    op("pe", "matmul", pb[bk][:, j * 128:(j + 1) * 128], s_["q_of"](j), s_["p_of"](j), start=True, stop=True,
                       r=s_["qk"] + s_["pk"], w=pkeys(bk))
                op("dve", "tensor_copy", M(pn), pb[bk][:, :], r=pkeys(bk), w=[("M", pn)])
            filler.fill(3)
            for g in range(2):
                s_ = state[g]
                qn, pn = LQ[g][l % 2], LP[g][l % 2]
                tcur = s_["tcur"]
                tn = LT[g][1] if tcur == LT[g][0] else LT[g][0]
                bk = TBANK[g]
                for j in range(4):
                    op("pe", "matmul", pb[bk][:, j * 128:(j + 1) * 128], M4(pn)[:, j, :], M4(tcur)[:, j, :], start=True, stop=False,
                       r=[("M", pn), ("M", tcur)], w=pkeys(bk))
                    op("pe", "matmul", pb[bk][:, j * 128:(j + 1) * 128], ident_b, M4(tcur)[:, j, :], start=False, stop=True,
                       r=["cstb", ("M", tcur)], w=pkeys(bk))
                if l < 5:
                    op("act" if g == 0 else "dve", "copy" if g == 0 else "tensor_copy", M(tn), pb[bk][:, :], r=pkeys(bk), w=[("M", tn)])
                    s_["tcur"] = tn
                    s_["q_of"] = (lambda qn_: (lambda j: M4(qn_)[:, j, :]))(qn)
                    s_["qk"] = [("M", qn)]
                else:
                    op("act" if g == 0 else "dve", "copy" if g == 0 else "tensor_copy", M(TTF + g), pb[bk][:, :], r=pkeys(bk), w=[("M", TTF + g)])
                s_["p_of"] = (lambda pn_: (lambda j: M4(pn_)[:, j, :]))(pn)
                s_["pk"] = [("M", pn)]
            filler.fill(3)
        for src, dst in ((BB, BT), (KHB, KHT), (VB, VT)):
            for c in range(8):
                op("pe", "transpose", pb7[:, c * 128:(c + 1) * 128], M8(src)[:, c, :], ident_b, r=mk(src) + ["cstb"], w=["pb7"])
            op("act", "copy", Mt[:, dst:dst + 2, :].rearrange("p a t -> p (a t)"), pb7[:, :], r=["pb7"], w=mk(dst))
            filler.fill(2)
        sbf = Sbf[:, hp, :]
        s32 = S32[:, hp, :]
        skey, s32key = ("Sbf", hp), ("S32", hp)
        xn, uu = M(XU)[:, 0:128], M(XU)[:, 128:256]
        for c in range(8):
            am = M(AM0 + c)
            amk1 = [("M", AM0 + c)]
            rt = M(RT_)[:, c * 64:(c + 1) * 64]
            vt, bt, kht = M8(VT)[:, c, :], M8(BT)[:, c, :], M8(KHT)[:, c, :]
            op("pe", "matmul", pb[6][:, 0:128], M8(KB)[:, c, :], sbf, start=True, stop=False, r=mk(KB) + [skey], w=[("pb6", 0)])
            op("pe", "matmul", pb[6][:, 0:128], am[:, 256:384], vt, start=False, stop=True, r=amk1 + mk(VT), w=[("pb6", 0)])
            ysl = pb[4][:, c * 64:(c + 1) * 64]
            op("pe", "matmul", ysl, sbf, rt, start=True, stop=False, r=[skey, ("M", RT_)], w=[("pb", 4)])
            op("pe", "matmul", ysl, vt, am[:, 448:512], start=False, stop=False, r=mk(VT) + amk1, w=[("pb", 4)])
            op("act", "mul", xn, pb[6][:, 0:128], -1.0, r=[("pb6", 0)], w=[("XU", 0)])
            filler.fill(1)
            op("pe", "matmul", pb[6][:, 128:256], M8(TTF)[:, c, :], xn, start=True, stop=True, r=mk(TTF) + [("XU", 0)], w=[("pb6", 1)])
            op("dve", "tensor_copy", uu, pb[6][:, 128:256], r=[("pb6", 1)], w=[("XU", 1)])
            filler.fill(1)
            op("pe", "matmul", pb[6][:, 256:384], bt, uu, start=True, stop=False, r=mk(BT) + [("XU", 1)], w=[("pb6", 2)])
            op("pe", "matmul", pb[6][:, 256:384], kht, vt, start=False, stop=True, r=mk(KHT) + mk(VT), w=[("pb6", 2)])
            op("pe", "matmul", ysl, uu, am[:, 384:448], start=False, stop=True, r=[("XU", 1)] + amk1, w=[("pb", 4)])
            op("dve", "scalar_tensor_tensor", sbf, s32, wc_[:, c:c + 1], pb[6][:, 256:384], ALU.mult, ALU.add,
               r=[s32key, wkey, ("pb6", 2)], w=[skey])
            op("dve", "scalar_tensor_tensor", s32, s32, wc_[:, c:c + 1], pb[6][:, 256:384], ALU.mult, ALU.add,
               r=[s32key, wkey, ("pb6", 2)], w=[s32key])
            filler.fill(1)
        filler.drain()
        sgs, bon = SG[par], BON[par]
        op("act", "copy", R5(G1), pb[4][:, :], r=[("pb", 4)], w=[("R", G1)])
        op("act", "activation", R5(G2), R5(G1), AF.Square, r=[("R", G1)], w=[("R", G2)])
        op("pe", "matmul", pb[2][:, :], bdones, R5(G1), start=True, stop=True, r=["cst", ("R", G1)], w=[("pb", 2)])
        op("act", "mul", R5(G3), pb[2][:, :], 1.0 / 64, r=[("pb", 2)], w=[("R", G3)])
        op("pe", "matmul", pb[2][:, :], bdones, R5(G2), start=True, stop=True, r=["cst", ("R", G2)], w=[("pb", 2)])
        op("dve", "tensor_tensor", R5(G2), R5(G3), R5(G3), ALU.mult, r=[("R", G3)], w=[("R", G2)])
        op("dve", "scalar_tensor_tensor", R5(G2), pb[2][:, :], 1.0 / 64, R5(G2), ALU.mult, ALU.subtract, r=[("pb", 2), ("R", G2)], w=[("R", G2)])
        op("dve", "tensor_scalar", R5(G2), R5(G2), 64e-5, None, ALU.add, r=[("R", G2)], w=[("R", G2)])
        op("act", "activation", R5(G2), R5(G2), AF.Sqrt, r=[("R", G2)], w=[("R", G2)])
        op("dve", "reciprocal", R5(G2), R5(G2), r=[("R", G2)], w=[("R", G2)])
        op("dve", "tensor_tensor", R5(G1), R5(G1), R5(G3), ALU.subtract, r=[("R", G1), ("R", G3)], w=[("R", G1)])
        op("dve", "tensor_tensor", R5(G1), R5(G1), R5(G2), ALU.mult, r=[("R", G1), ("R", G2)], w=[("R", G1)])
        op("dve", "tensor_scalar", R5(G1), R5(G1), P(8), P(9), ALU.mult, ALU.add, r=[("R", G1), "pc"], w=[("R", G1)])
        op("dve", "tensor_tensor", R5(G1), R5(G1), R5(bon), ALU.add, r=[("R", G1), ("R", bon)], w=[("R", G1)])
        op("dve", "tensor_tensor", M(YO), R5(G1), R5(sgs), ALU.mult, r=[("R", G1), ("R", sgs)], w=[("M", YO)])
        dma("sp", zsend[tb][hp * 128:(hp + 1) * 128, :], M(YO), r=[("M", YO)], w=[("zsend", tb)], key="zs")

    def rwkv_block(tb):
        build_hT(xf, [(tb * 512 + tt * 128, 128) for tt in range(4)], 32)
        for j in range(2):
            wi = load_w(wr[j])
            k_ = next_pb()
            proj(wi, 96, 32, hcols, pb[k_][0:96, :], ("pb", k_))
            shift(k_, 96, 0, 24 + j, pc[0:96, 80 + j:81 + j], 1, 17 + j)
        op("act", "activation", R5(17)[0:96], R5(17)[0:96], AF.Tanh, r=[("R", 17)], w=[("R", 17)])
        filler = Filler()
        filler.add(gen_P(tb, 0))
        filler.add(gen_ER(tb, 0))
        filler.drain()
        for hp in range(8):
            if hp + 1 < 8:
                filler.add(gen_P(tb, hp + 1))
                filler.add(gen_ER(tb, hp + 1))
            rwkv_unit_rest(tb, hp, filler)
        if n_conv_blocks:
            S.dma("pool", lambda e: e.collective_compute("AllGather", ALU.bypass, replica_groups=[[0, 1], [2, 3], [4, 5], [6, 7]],
                                                         ins=[zsend[tb]], outs=[zall[tb]]), reads=[("zsend", tb)], writes=[("zall", tb)],
                  key=("cc", tb), inc=1)

    for tb in range(n_rwkv_blocks):
        rwkv_block(tb)

    if n_conv_blocks:
        for tb in range(n_rwkv_blocks):
            op("pool", "memset", small[:, 8:9], 0.0, r=[("zall", tb)], w=[("zall", tb), "sm8"])

    CT = lambda cc: R5(cc)
    SV = lambda cc: Rt[:, 16 + cc // 2, :].bitcast(BF16)[:, (cc % 2) * 512:(cc % 2) * 512 + 512]
    skey_ = lambda cc: ("R", 16 + cc // 2)
    UB = [24, 25]
    YR0, YC0 = 0, 16

    def conv_block(b):
        first = (b == 0)
        if first:
            build_hT(xc, [(0, 32)] + [(32 + tt * 128, 128) for tt in range(4)], 0)
        else:
            build_hT(xc, [(HALO + b * 512 + tt * 128, 128) for tt in range(4)], 32)
        for cc in range(16):
            cp = lambda c: pc[:, 82 + cc * 4 + c:82 + cc * 4 + c + 1]
            ub = UB[cc % 2]
            U = Rt[:, ub, 0:544]
            wv = load_w(wc[2 * cc])
            proj(wv, 128, 32, hcols, pb[0][:, :], ("pb", 0))
            if first:
                proj(wv, 128, 32, lambda k: hT[:, k, 0:32], pb[2][:, 0:32], ("pb", 2))
            wg = load_w(wc[2 * cc + 1])
            proj(wg, 128, 32, hcols, pb[1][:, :], ("pb", 1))
            if first:
                proj(wg, 128, 32, lambda k: hT[:, k, 0:32], pb[2][:, 32:64], ("pb", 2))
            op("act", "activation", XT[:, 0, 0:512], pb[1][:, :], AF.Sigmoid, r=[("pb", 1)], w=[("XT", 0)])
            op("dve", "tensor_tensor", U[:, 32:544], pb[0][:, :], XT[:, 0, 0:512], ALU.mult, r=[("pb", 0), ("XT", 0)], w=[("R", ub)])
            if first:
                op("act", "activation", XT[:, 1, 0:32], pb[2][:, 32:64], AF.Sigmoid, r=[("pb", 2)], w=[("XT", 1)])
                op("dve", "tensor_tensor", U[:, 0:32], pb[2][:, 0:32], XT[:, 1, 0:32], ALU.mult, r=[("pb", 2), ("XT", 1)], w=[("R", ub)])
            else:
                op("act", "copy", U[:, 0:32], ucar[:, cc, :], r=[("ucar", cc)], w=[("R", ub)])
            op("act", "copy", ucar[:, cc, :], U[:, 512:544], r=[("R", ub)], w=[("ucar", cc)])
            ce = "dve"
            wk = lambda k: pc[:, 146 + cc * 31 + k:146 + cc * 31 + k + 1]
            op(ce, "tensor_scalar", CT(cc), U[:, 2:514], wk(0), cp(0), ALU.mult, ALU.add, r=[("R", ub), "pc"], w=[("R", cc)])
            for k in range(1, 31):
                op(ce, "scalar_tensor_tensor", CT(cc), U[:, 2 + k:514 + k], wk(k), CT(cc), ALU.mult, ALU.add,
                   r=[("R", ub), ("R", cc), "pc"], w=[("R", cc)])
            op("act", "activation", XT[:, 2, 0:512], CT(cc), AF.Square, r=[("R", cc)], w=[("XT", 2)])
            op("pe", "matmul", pb[3][:, :], ones_f, CT(cc), start=(cc == 0), stop=(cc == 15), r=["cst", ("R", cc)], w=[("pb", 3)])
            op("pe", "matmul", pb[4][:, :], ones_f, XT[:, 2, 0:512], start=(cc == 0), stop=(cc == 15), r=["cst", ("XT", 2)], w=[("pb", 4)])
        mean, rstd = XT[:, 0, 0:512], XT[:, 1, 0:512]
        op("act", "mul", mean, pb[3][:, :], 1.0 / 2048, r=[("pb", 3)], w=[("XT", 0)])
        op("dve", "tensor_tensor", XT[:, 2, 0:512], mean, mean, ALU.mult, r=[("XT", 0)], w=[("XT", 2)])
        op("dve", "scalar_tensor_tensor", rstd, pb[4][:, :], 1.0 / 2048, XT[:, 2, 0:512], ALU.mult, ALU.subtract, r=[("pb", 4), ("XT", 2)], w=[("XT", 1)])
        op("dve", "tensor_scalar", rstd, rstd, 1e-5, None, ALU.add, r=[("XT", 1)], w=[("XT", 1)])
        op("act", "activation", rstd, rstd, AF.Sqrt, r=[("XT", 1)], w=[("XT", 1)])
        op("dve", "reciprocal", rstd, rstd, r=[("XT", 1)], w=[("XT", 1)])
        for cc in range(16):
            cp = lambda c: pc[:, 82 + cc * 4 + c:82 + cc * 4 + c + 1]
            op("dve", "tensor_tensor", CT(cc), CT(cc), mean, ALU.subtract, r=[("R", cc), ("XT", 0)], w=[("R", cc)])
            op("pool", "tensor_tensor", CT(cc), CT(cc), rstd, ALU.mult, r=[("R", cc), ("XT", 1)], w=[("R", cc)])
            op("act", "activation", SV(cc), CT(cc), AF.Silu, bias=cp(2), scale=cp(1), r=[("R", cc), "pc"], w=[skey_(cc)])
        for e_ in range(16):
            cp = lambda c: pc[:, 82 + e_ * 4 + c:82 + e_ * 4 + c + 1]
            wgc = load_w(wc[32 + e_])
            proj(wgc, 128, 32, hcols, pb[0][:, :], ("pb", 0))
            op("act", "activation", XT[:, 2, 0:512], pb[0][:, :], AF.Silu, r=[("pb", 0)], w=[("XT", 2)])
            wpi = load_w(wp[e_], 2048)
            for k in range(16):
                op("pe", "matmul", pb[1][:, :], wbuf[wpi][:, k * 128:(k + 1) * 128], SV(k), start=(k == 0), stop=(k == 15),
                   r=[("w", wpi), skey_(k)], w=[("pb", 1)])
            op("dve", "scalar_tensor_tensor", M(YC0 + e_), pb[1][:, :], cp(3), XT[:, 2, 0:512], ALU.add, ALU.mult,
               r=[("pb", 1), ("XT", 2), "pc"], w=[("M", YC0 + e_)])
        for k in range(16):
            sl = k % 2
            dma("sp", ystg[:, 2 * sl, :], zall[b][k * 128:(k + 1) * 128, :], r=[("zall", b)], w=[("ys", 2 * sl)], key=("ys", 2 * sl))
            op("pool", "tensor_scalar", M(YR0 + k), ystg[:, 2 * sl, :], sel[:, 0:1], None, ALU.mult, r=[("ys", 2 * sl), "sel"], w=[("M", YR0 + k)])
            if 4 + b >= n_rwkv_blocks:
                continue
            dma("sp", ystg[:, 2 * sl + 1, :], zall[4 + b][k * 128:(k + 1) * 128, :], r=[("zall", 4 + b)], w=[("ys", 2 * sl + 1)],
                key=("ys", 2 * sl + 1))
            op("pool", "tensor_scalar", ystg[:, 2 * sl + 1, :], ystg[:, 2 * sl + 1, :], sel[:, 1:2], None, ALU.mult,
               r=[("ys", 2 * sl + 1), "sel"], w=[("ys", 2 * sl + 1)])
            op("pool", "tensor_tensor", M(YR0 + k), M(YR0 + k), ystg[:, 2 * sl + 1, :], ALU.add,
               r=[("ys", 2 * sl + 1), ("M", YR0 + k)], w=[("M", YR0 + k)])
        for dch in range(32):
            wi = load_w(wo[dch])
            k_ = dch % 2
            for k in range(32):
                op("pe", "matmul", pb[k_][:, :], wbuf[wi][:, k * 128:(k + 1) * 128], M(k), start=(k == 0), stop=(k == 31),
                   r=[("w", wi), ("M", k)], w=[("pb", k_)])
            op("act", "copy", R5(dch), pb[k_][:, :], r=[("pb", k_)], w=[("R", dch)])
            op("act", "activation", XT[:, dch % 2, 0:512], pb[k_][:, :], AF.Square, r=[("pb", k_)], w=[("XT", dch % 2)])
            op("pe", "matmul", pb[5][:, :], ones_f, XT[:, dch % 2, 0:512], start=(dch == 0), stop=(dch == 31), r=["cst", ("XT", dch % 2)], w=[("pb", 5)])
        rs = XT[:, 2, 0:512]
        op("dve", "tensor_scalar", rs, pb[5][:, :], 1.0 / D, 1e-6, ALU.mult, ALU.add, r=[("pb", 5)], w=[("XT", 2)])
        op("act", "activation", rs, rs, AF.Sqrt, r=[("XT", 2)], w=[("XT", 2)])
        op("dve", "reciprocal", rs, rs, r=[("XT", 2)], w=[("XT", 2)])
        for dch in range(32):
            op("dve", "scalar_tensor_tensor", R5(dch), R5(dch), pc[:, 674 + dch:675 + dch], rs, ALU.mult, ALU.mult,
               r=[("R", dch), ("XT", 2), "pc"], w=[("R", dch)])
        for tt in range(4):
            row_x = HALO + b * 512 + tt * 128
            dma("sp", io32[:, :], xc[row_x:row_x + 128, :], w=["io32"], key="io32")
            for g8 in range(8):
                k_ = g8 % 2
                for j in range(4):
                    dch = g8 * 4 + j
                    op("pe", "transpose", pb[k_][:, j * 128:(j + 1) * 128], R5(dch)[:, tt * 128:(tt + 1) * 128], ident_f,
                       r=[("R", dch), "cst"], w=[("pb", k_)])
                op("dve", "tensor_tensor", io32[:, g8 * 512:(g8 + 1) * 512], io32[:, g8 * 512:(g8 + 1) * 512], pb[k_][:, :], ALU.add,
                   r=["io32", ("pb", k_)], w=["io32"])
            row_y = b * 512 + tt * 128
            dma("sp", yout[row_y:row_y + 128, :], io32[:, :], r=["io32"], key="yout")

    for b in range(n_conv_blocks):
        conv_block(b)

    counts = S.emit(final_waits=["yout"] if n_conv_blocks else ["zs"])
    st.close()
    return nc, counts


def _chunk_layout(wcols):
    m = wcols.shape[1]
    nk = wcols.shape[0] // 128
    a = np.zeros((nk, 128, 128), np.float32)
    a[:, :, :m] = wcols.reshape(nk, 128, m)
    return np.ascontiguousarray(a.transpose(1, 0, 2)).reshape(128, nk * 128)


def _consts():
    cst = np.zeros((128, 1408), np.float32)
    cst[:, 0:128] = np.eye(128, dtype=np.float32)
    idx = np.arange(128)
    same = (idx[:, None] // 64) == (idx[None, :] // 64)
    cst[:, 128:256] = same.astype(np.float32)
    cst[:, 256:384] = 1.0
    s = idx[:, None] % 64
    t = idx[None, :] % 64
    cst[:, 384:512] = -(same & (s < t)).astype(np.float32)
    cst[:, 512:640] = -(same & (s > t)).astype(np.float32)
    cst[:, 640:768] = (same & (s < t)).astype(np.float32)
    incl = (s[:, 0:1] <= np.arange(64)[None, :]).astype(np.float32)
    cst[:, 768:832] = incl
    cst[:, 832:896] = incl
    r = np.ones(512, np.float32)
    r[0::64] = 0.0
    cst[:, 896:1408] = r[None, :]
    return cst


_CACHE = {}
_NB = (8, 4)


def kernel(x, norm_pre_g, w_in, mu_shift, w0, w_lora_up, a0, a_lora_up, k_k, k_a, r_k, lnx_g, lnx_b, conv_w, conv_b,
           cln_g, cln_b, w_pw2, b_pw2, w_out, norm_post_g):
    f = lambda a: np.asarray(a, dtype=np.float32)
    x, w_in, w_out, w_pw2 = f(x), f(w_in), f(w_out), f(w_pw2)
    mu_shift, w0, a0, k_k, k_a = f(mu_shift), f(w0), f(a0), f(k_k), f(k_a)
    r_k, lnx_g, lnx_b = f(r_k).reshape(-1), f(lnx_g), f(lnx_b)
    conv_w, conv_b, cln_g, cln_b, b_pw2 = f(conv_w), f(conv_b), f(cln_g), f(cln_b), f(b_pw2)
    norm_pre_g, norm_post_g = f(norm_pre_g), f(norm_post_g)
    w_lora_up, a_lora_up = f(w_lora_up), f(a_lora_up)

    if "nc" not in _CACHE:
        _CACHE["nc"] = build_program(*_NB)[0]
    nc = _CACHE["nc"]

    cst = _consts()
    wc_l = np.empty((48, 128, 4096), np.float32)
    c1 = 6336 + 2048
    for cc in range(16):
        wc_l[2 * cc] = _chunk_layout(w_in[:, c1 + cc * 128:c1 + (cc + 1) * 128])
        wc_l[2 * cc + 1] = _chunk_layout(w_in[:, c1 + 2048 + cc * 128:c1 + 2048 + (cc + 1) * 128])
        wc_l[32 + cc] = _chunk_layout(w_in[:, c1 + 4096 + cc * 128:c1 + 4096 + (cc + 1) * 128])
    wp_l = np.stack([_chunk_layout(w_pw2[:, e * 128:(e + 1) * 128]) for e in range(16)])
    wo_l = np.stack([_chunk_layout(w_out[:, dch * 128:(dch + 1) * 128]) for dch in range(32)])

    in_maps = []
    for c in range(NCORES):
        b, hh = c // 2, c % 2
        ch0 = hh * 1024
        wr_l = np.empty((34, 128, 4096), np.float32)
        wr_l[0] = _chunk_layout(w_in[:, 6144:6240])
        wr_l[1] = _chunk_layout(w_in[:, 6240:6336])
        pcm = np.zeros((128, NPC), np.float32)
        for hp in range(8):
            lo = ch0 + hp * 128
            for j in range(3):
                wr_l[2 + hp * 4 + j] = _chunk_layout(w_in[:, j * 2048 + lo:j * 2048 + lo + 128])
            wr_l[2 + hp * 4 + 3] = _chunk_layout(w_in[:, 6336 + lo:6336 + lo + 128])
            cols = [mu_shift[lo:lo + 128], mu_shift[2048 + lo:2048 + lo + 128], mu_shift[4096 + lo:4096 + lo + 128],
                    w0[lo:lo + 128], a0[lo:lo + 128], k_k[lo:lo + 128], k_a[lo:lo + 128], r_k[lo:lo + 128],
                    lnx_g[lo:lo + 128], lnx_b[lo:lo + 128]]
            for j, v in enumerate(cols):
                pcm[:, hp * 10 + j] = v
        pcm[0:96, 80] = mu_shift[6144:6240]
        pcm[0:96, 81] = mu_shift[6240:6336]
        for cc in range(16):
            sl = slice(cc * 128, (cc + 1) * 128)
            pcm[:, 82 + cc * 4 + 0] = conv_b[sl]
            pcm[:, 82 + cc * 4 + 1] = cln_g[sl]
            pcm[:, 82 + cc * 4 + 2] = cln_b[sl]
            pcm[:, 82 + cc * 4 + 3] = b_pw2[sl]
            pcm[:, 146 + cc * 31:146 + (cc + 1) * 31] = conv_w[:, sl].T
        pcm[:, 642:674] = norm_pre_g.reshape(32, 128).T
        pcm[:, 674:706] = norm_post_g.reshape(32, 128).T
        lor = np.concatenate([w_lora_up[:, ch0:ch0 + 1024], a_lora_up[:, ch0:ch0 + 1024]], axis=1)
        xcv = np.zeros((HALO + TOKC, D), np.float32)
        t0 = hh * TOKC
        if hh == 1:
            xcv[0:HALO] = x[b, t0 - HALO:t0]
        xcv[HALO:] = x[b, t0:t0 + TOKC]
        selv = np.zeros((128, 2), np.float32)
        selv[:, hh] = 1.0
        in_maps.append({"xf": np.ascontiguousarray(x[b]), "xc": xcv, "wr": wr_l, "wc": wc_l, "wp": wp_l, "wo": wo_l,
                        "lora": np.ascontiguousarray(lor), "pc": pcm, "cst": cst, "sel": selv})
    res = run_bass_kernel_spmd(nc, in_maps, core_ids=list(range(NCORES)))
    out = np.empty((4, T, D), np.float32)
    for c in range(NCORES):
        b, hh = c // 2, c % 2
        out[b, hh * TOKC:(hh + 1) * TOKC] = res.results[c]["y"]
    return out
```

```python
import contextlib
import numpy as np
import concourse.bass as bass
import concourse.mybir as mybir
from concourse.bass_utils import run_bass_kernel_spmd

F32 = mybir.dt.float32
BF16 = mybir.dt.bfloat16
AF = mybir.ActivationFunctionType
ALU = mybir.AluOpType

D = 4096
T = 4096
NCORES = 8
TOKC = 2048
HALO = 32
NPC = 720
C0 = 0.6065306597126334
EPOCH = 20000
OPT = dict(prefetch=True, aalt=True, ln=True, gdefer=False)


class Sched:
    def __init__(self, nc):
        self.nc = nc
        self.ops = []
        self.last_w = {}
        self.readers = {}
        self.dma_keys = {}

    def _deps(self, i, reads, writes):
        deps = set()
        for k in reads:
            w = self.last_w.get(k)
            if w is not None:
                deps.add((w, "raw"))
        for k in writes:
            w = self.last_w.get(k)
            if w is not None:
                deps.add((w, "waw"))
            for r in self.readers.get(k, ()):
                deps.add((r, "war"))
        for k in reads:
            self.readers.setdefault(k, []).append(i)
        for k in writes:
            self.last_w[k] = i
            self.readers[k] = []
        return deps

    def op(self, eng, fn, reads=(), writes=()):
        i = len(self.ops)
        self.ops.append(dict(eng=eng, fn=fn, deps=self._deps(i, reads, writes), dma=None))

    def dma(self, eng, fn, reads=(), writes=(), key=None, inc=16):
        i = len(self.ops)
        deps = self._deps(i, reads, writes)
        n = self.dma_keys.get(key, (0, inc))[0] + 1
        self.dma_keys[key] = (n, inc)
        self.ops.append(dict(eng=eng, fn=fn, deps=deps, dma=(key, n, inc)))

    def emit(self, final_waits=()):
        nc = self.nc
        ops = self.ops
        need = [False] * len(ops)
        for i, o in enumerate(ops):
            for (d, kind) in o["deps"]:
                p = ops[d]
                if p["dma"] is not None:
                    continue
                if p["eng"] == o["eng"] and (p["eng"] == "pe" or kind != "raw"):
                    continue
                need[d] = True
        cnt = {e: 0 for e in ("pe", "act", "dve", "pool", "sp")}
        sig = [None] * len(ops)
        for i, o in enumerate(ops):
            if need[i]:
                e = o["eng"]
                sig[i] = (e, cnt[e] // EPOCH, cnt[e] % EPOCH + 1)
                cnt[e] += 1
        stack = contextlib.ExitStack()
        sems = {}
        for e in cnt:
            for ep in range((cnt[e] + EPOCH - 1) // EPOCH):
                sems[(e, ep)] = stack.enter_context(nc.semaphore(f"c_{e}_{ep}"))
        dsems = {}
        for k in self.dma_keys:
            dsems[k] = stack.enter_context(nc.semaphore(f"d_{len(dsems)}"))
        per_eng = {e: [] for e in cnt}
        for i, o in enumerate(ops):
            per_eng[o["eng"]].append(i)
        block = stack.enter_context(nc.Block())
        engs = {"pe": nc.tensor, "act": nc.scalar, "dve": nc.vector, "pool": nc.gpsimd, "sp": nc.sync}

        def run_engine(ename, final):
            eng = engs[ename]
            seen = {}
            for i in per_eng[ename]:
                o = ops[i]
                waits = {}
                for (d, kind) in o["deps"]:
                    p = ops[d]
                    if p["dma"] is not None:
                        key, n, inc = p["dma"]
                        s, v = ("d", key), inc * n
                    else:
                        if sig[d] is None:
                            continue
                        if p["eng"] == ename and (ename == "pe" or kind != "raw"):
                            continue
                        e, ep, v = sig[d]
                        s = (e, ep)
                    if seen.get(s, 0) >= v:
                        continue
                    waits[s] = max(waits.get(s, 0), v)
                for s, v in waits.items():
                    seen[s] = v
                    eng.wait_ge(dsems[s[1]] if s[0] == "d" else sems[s], v)
                ins = o["fn"](eng)
                if o["dma"] is not None:
                    ins.then_inc(dsems[o["dma"][0]], o["dma"][2])
                elif sig[i] is not None:
                    ins.then_inc(sems[sig[i][:2]], 1)
            for key in final:
                n, inc = self.dma_keys[key]
                eng.wait_ge(dsems[key], inc * n)

        block.tensor(lambda e: run_engine("pe", ()))
        block.scalar(lambda e: run_engine("act", ()))
        block.vector(lambda e: run_engine("dve", ()))
        block.gpsimd(lambda e: run_engine("pool", ()))
        block.sync(lambda e: run_engine("sp", final_waits))
        stack.close()
        return {e: len(per_eng[e]) for e in per_eng}


def build_program(n_rwkv_blocks=8, n_conv_blocks=4):
    nc = bass.Bass("TRN2", target_bir_lowering=False)
    dt_in = lambda name, shape: nc.dram_tensor(name, shape, F32, kind="ExternalInput").ap()
    xf = dt_in("xf", [T, D])
    tiny = (n_conv_blocks == 0)
    xc = dt_in("xc", [HALO + TOKC, D] if not tiny else [128, 128])
    wr = dt_in("wr", [34, 128, 4096])
    wc = dt_in("wc", [48, 128, 4096] if not tiny else [1, 128, 128])
    wp = dt_in("wp", [16, 128, 2048] if not tiny else [1, 128, 128])
    wo = dt_in("wo", [32, 128, 4096] if not tiny else [1, 128, 128])
    lora = dt_in("lora", [96, 2048])
    pcd = dt_in("pc", [128, NPC])
    cstd = dt_in("cst", [128, 1408])
    seld = dt_in("sel", [128, 2])
    yout = nc.dram_tensor("y", [TOKC, D], F32, kind="ExternalOutput").ap()
    zsend = [nc.dram_tensor(f"zsend{i}", [1024, 512], BF16).ap() for i in range(8)]
    zall = [nc.dram_tensor(f"zall{i}", [2048, 512], BF16).ap() for i in range(8)]

    S = Sched(nc)
    st = contextlib.ExitStack()
    sb = lambda name, shape, dt: st.enter_context(nc.sbuf_tensor("s_" + name, shape, dt))
    ps = lambda name, shape, dt: st.enter_context(nc.psum_tensor(name, shape, dt))

    NR, NM = 32, 36
    hT = sb("hT", [128, 32, 544], BF16)
    NW = 3
    wbuf = [sb(f"wbuf{i}", [128, 4096], BF16) for i in range(NW)]
    io32 = sb("io32", [128, 4096], F32)
    Rt = sb("Rt", [128, NR, 544], F32)
    Mt = sb("Mt", [128, NM, 512], BF16)
    xbf = Mt[:, 28:36, :].rearrange("p a t -> p (a t)")
    XBK = [("M", i_) for i_ in range(28, 36)]
    XT = sb("XT", [128, 3, 512], F32)
    pc = sb("pc", [128, NPC], F32)
    cst = sb("cst", [128, 1408], F32)
    cstb = sb("cstb", [128, 128], BF16)
    lor = sb("lor", [96, 2, 256], F32)
    sel = sb("sel", [128, 2], F32)
    S32 = sb("S32", [128, 8, 128], F32)
    Sbf = sb("Sbf", [128, 8, 128], BF16)
    carry = sb("carry", [128, 32], F32)
    ucar = sb("ucar", [128, 16, 32], F32)
    small = sb("small", [128, 16], F32)
    ystg = sb("ystg", [128, 4, 512], BF16)
    pb = [ps(f"pb{i}", [128, 512], F32) for i in range(7)]
    pb7 = ps("pb7", [128, 1024], BF16)

    def R(i):
        return Rt[:, i, :]

    def R5(i):
        return Rt[:, i, 0:512]

    def M(i):
        return Mt[:, i, :]

    def M8(i):
        return Mt[:, i:i + 2, :].rearrange("p a (c t) -> p (a c) t", t=128)

    def M4(i):
        return Mt[:, i, :].rearrange("p (c t) -> p c t", t=128)

    ident_f = cst[:, 0:128]
    bdones = cst[:, 128:256]
    ones_f = cst[:, 256:384]
    maskA = cst[:, 384:896]
    rst = cst[:, 896:1408]
    ident_b = cstb[:, 0:128]

    def op(engn, meth, *args, r=(), w=(), **kw):
        S.op(engn, lambda e: getattr(e, meth)(*args, **kw), reads=list(r), writes=list(w))

    def dma(engn, out, in_, r=(), w=(), key=None):
        S.dma(engn, lambda e: e.dma_start(out=out, in_=in_), reads=list(r), writes=list(w), key=key)

    dma("sp", pc[:], pcd, w=["pc"], key="pc")
    dma("sp", cst[:], cstd, w=["cst"], key="cst")
    dma("sp", sel[:], seld, w=["sel"], key="sel")
    dma("pool", cstb[:], cstd[:, 0:128], w=["cstb"], key="cstb")
    op("dve", "memset", S32[:], 0.0, w=[("S32", i) for i in range(8)])
    op("dve", "memset", Sbf[:], 0.0, w=[("Sbf", i) for i in range(8)])
    op("dve", "memset", carry[:], 0.0, w=["carry"])
    op("pool", "memset", Mt[:, 0:8, :], 0.0, w=[("M", i) for i in range(8)])
    for hp in range(8):
        op("dve", "tensor_scalar", pc[:, 706 + hp:707 + hp], pc[:, hp * 10 + 6:hp * 10 + 7], -1.0, 1.0, ALU.mult, ALU.add,
           r=["pc"], w=["pc"])

    wctr = [0]

    pref = {}

    def load_w(src, ncols=4096, tag=None):
        if tag is not None and tag in pref:
            return pref.pop(tag)
        i = wctr[0] % NW
        wctr[0] += 1
        dma("pool", wbuf[i][:, 0:ncols], src, w=[("w", i)], key=("w", i))
        return i

    def prefetch(src, tag, ncols=4096):
        if OPT["prefetch"] and tag not in pref:
            pref[tag] = load_w(src, ncols)

    def proj(wi, M_, nk, rhs_fn, out_ap, wkey):
        for k in range(nk):
            op("pe", "matmul", out_ap, wbuf[wi][:, k * 128:k * 128 + M_], rhs_fn(k), start=(k == 0), stop=(k == nk - 1),
               r=[("w", wi), "hT"], w=[wkey])

    def build_hT(xsrc, tiles, col0):
        col = col0
        for (r0, n) in tiles:
            dma("sp", io32[0:n, :], xsrc[r0:r0 + n, :], w=["io32"], key="io32")
            op("act", "activation", xbf[0:n, :], io32[0:n, :], AF.Square, accum_out=small[0:n, 0:1], r=["io32"], w=XBK + ["sm0"])
            op("dve", "tensor_scalar", small[0:n, 1:2], small[0:n, 0:1], 1.0 / D, 1e-6, ALU.mult, ALU.add, r=["sm0"], w=["sm1"])
            op("act", "activation", small[0:n, 1:2], small[0:n, 1:2], AF.Sqrt, r=["sm1"], w=["sm1"])
            op("dve", "reciprocal", small[0:n, 2:3], small[0:n, 1:2], r=["sm1"], w=["sm2"])
            op("dve", "tensor_scalar", xbf[0:n, :], io32[0:n, :], small[0:n, 2:3], None, ALU.mult, r=["io32", "sm2"], w=XBK)
            for kg in range(4):
                for kk in range(8):
                    k = kg * 8 + kk
                    op("pe", "transpose", pb7[:, kk * 128:kk * 128 + n], xbf[0:n, k * 128:(k + 1) * 128], ident_b[0:n, 0:n],
                       r=XBK + ["cstb"], w=["pb7"])
                for kk in range(8):
                    k = kg * 8 + kk
                    if kk % 2 == 0:
                        op("act", "mul", hT[:, k, col:col + n], pb7[:, kk * 128:kk * 128 + n], pc[:, 642 + k:643 + k],
                           r=["pb7", "pc"], w=["hT"])
                    else:
                        op("dve", "tensor_scalar", hT[:, k, col:col + n], pb7[:, kk * 128:kk * 128 + n], pc[:, 642 + k:643 + k], None,
                           ALU.mult, r=["pb7", "pc"], w=["hT"])
            col += n

    pbi = [0]

    def next_pb():
        pbi[0] ^= 1
        return pbi[0]

    def shift(pbk, np_, raw, ccol, mu, tmp, out):
        op("act", "copy", R(raw)[0:np_, 1:513], pb[pbk][0:np_, :], r=[("pb", pbk)], w=[("R", raw)])
        op("act", "copy", R(raw)[0:np_, 0:1], carry[0:np_, ccol:ccol + 1], r=["carry"], w=[("R", raw)])
        op("dve", "tensor_tensor", R5(tmp)[0:np_], R(raw)[0:np_, 0:512], R(raw)[0:np_, 1:513], ALU.subtract, r=[("R", raw)], w=[("R", tmp)])
        op("dve", "scalar_tensor_tensor", R5(out)[0:np_], R5(tmp)[0:np_], mu, R(raw)[0:np_, 1:513], ALU.mult, ALU.add,
           r=[("R", tmp), ("R", raw), "pc"], w=[("R", out)])
        op("dve", "tensor_copy", carry[0:np_, ccol:ccol + 1], R(raw)[0:np_, 512:513], r=[("R", raw)], w=["carry"])

    hcols = lambda k: hT[:, k, 32:544]
    KB, BB, KHB, VB = 0, 2, 4, 6
    RT_ = 8
    AM0 = 9
    QA, QB, PA, PB, TA, TB = 17, 18, 19, 20, 21, 22
    TTF, BT, KHT, VT = 23, 25, 27, 29
    XU, YO = 31, 32

    wct = sb("wct", [128, 2, 8], F32)

    class Filler:
        def __init__(self):
            self.gens = []

        def add(self, g):
            self.gens.append(g)

        def fill(self, n):
            while n > 0 and self.gens:
                try:
                    next(self.gens[0])
                    n -= 1
                except StopIteration:
                    self.gens.pop(0)

        def drain(self):
            self.fill(10 ** 9)

    def gproj(wi, M_, nk, rhs_fn, out_ap, wkey, step=4):
        for k in range(nk):
            op("pe", "matmul", out_ap, wbuf[wi][:, k * 128:k * 128 + M_], rhs_fn(k), start=(k == 0), stop=(k == nk - 1),
               r=[("w", wi), "hT"], w=[wkey])
            if k % step == step - 1:
                yield

    SG = (6, 23)
    BON = (16, 24)
    G1, G2, G3 = 25, 26, 27

    def gen_P(tb, hp):
        par = hp % 2
        P = lambda c: pc[:, hp * 10 + c:hp * 10 + c + 1]
        for j, outslot in enumerate((3, 4, 5)):
            wi = load_w(wr[2 + hp * 4 + j], tag=("wr", tb, hp, j))
            k_ = next_pb()
            yield from gproj(wi, 128, 32, hcols, pb[k_][:, :], ("pb", k_))
            if j == 0:
                prefetch(wr[2 + hp * 4 + 3], ("wr", tb, hp, 3))
            shift(k_, 128, j, hp * 3 + j, P(j), 9, outslot)
            yield
        wi = load_w(wr[2 + hp * 4 + 3], tag=("wr", tb, hp, 3))
        k_ = next_pb()
        yield from gproj(wi, 128, 32, hcols, pb[k_][:, :], ("pb", k_))
        op("act", "activation", R5(SG[par]), pb[k_][:, :], AF.Silu, r=[("pb", k_)], w=[("R", SG[par])])
        yield

    def gen_ER(tb, hp):
        par = hp % 2
        P = lambda c: pc[:, hp * 10 + c:hp * 10 + c + 1]
        r32, kx, v32 = R5(3), R5(4), R5(5)
        dma("sp", lor[:, par, 0:128], lora[:, hp * 128:(hp + 1) * 128], w=[("lor", par, 0)], key=("lor", par, 0))
        dma("sp", lor[:, par, 128:256], lora[:, 1024 + hp * 128:1024 + (hp + 1) * 128], w=[("lor", par, 1)], key=("lor", par, 1))
        op("pe", "matmul", pb[2][:, :], lor[:, par, 0:128], R5(17)[0:96], start=True, stop=True,
           r=[("lor", par, 0), ("R", 17)], w=[("pb", 2)])
        op("act", "activation", R5(7), pb[2][:, :], AF.Sigmoid, bias=P(3), r=[("pb", 2), "pc"], w=[("R", 7)])
        yield
        op("pe", "matmul", pb[2][:, :], lor[:, par, 128:256], R5(18)[0:96], start=True, stop=True,
           r=[("lor", par, 1), ("R", 18)], w=[("pb", 2)])
        op("act", "activation", R5(8), pb[2][:, :], AF.Sigmoid, bias=P(4), r=[("pb", 2), "pc"], w=[("R", 8)])
        yield
        sgw, a32 = R5(7), R5(8)
        op("act", "activation", R5(9), kx, AF.Square, scale=P(5), r=[("R", 4), "pc"], w=[("R", 9)])
        op("pe", "matmul", pb[2][:, :], bdones, R5(9), start=True, stop=True, r=["cst", ("R", 9)], w=[("pb", 2)])
        op("dve", "tensor_scalar", R5(9), pb[2][:, :], 1e-24, None, ALU.max, r=[("pb", 2)], w=[("R", 9)])
        yield
        if OPT["ln"]:
            op("act", "activation", R5(9), R5(9), AF.Ln, r=[("R", 9)], w=[("R", 9)])
            op("act", "activation", R5(9), R5(9), AF.Exp, scale=-0.5, r=[("R", 9)], w=[("R", 9)])
        else:
            op("act", "activation", R5(9), R5(9), AF.Sqrt, r=[("R", 9)], w=[("R", 9)])
            op("dve", "reciprocal", R5(9), R5(9), r=[("R", 9)], w=[("R", 9)])
        yield
        op("dve", "scalar_tensor_tensor", R5(10), kx, P(5), R5(9), ALU.mult, ALU.mult, r=[("R", 4), ("R", 9), "pc"], w=[("R", 10)])
        yield
        op("dve", "tensor_scalar", R5(20), a32, P(6), pc[:, 706 + hp:707 + hp], ALU.mult, ALU.add, r=[("R", 8), "pc"], w=[("R", 20)])
        yield
        op("dve", "tensor_tensor", R5(11), kx, R5(20), ALU.mult, r=[("R", 4), ("R", 20)], w=[("R", 11)])
        yield
        op("dve", "tensor_tensor", R5(12), R5(10), a32, ALU.mult, r=[("R", 10), ("R", 8)], w=[("R", 12)])
        yield
        bon = BON[par]
        op("dve", "scalar_tensor_tensor", R5(bon), r32, P(7), R5(11), ALU.mult, ALU.mult, r=[("R", 3), ("R", 11), "pc"], w=[("R", bon)])
        op("pe", "matmul", pb[2][:, :], bdones, R5(bon), start=True, stop=True, r=["cst", ("R", bon)], w=[("pb", 2)])
        yield
        op("dve", "tensor_tensor", R5(bon), pb[2][:, :], v32, ALU.mult, r=[("pb", 2), ("R", 5)], w=[("R", bon)])
        yield
        op("dve", "tensor_tensor_scan", R5(21), rst, sgw, 0.0, ALU.mult, ALU.add, r=["cst", ("R", 7)], w=[("R", 21)])
        yield
        op("act", "activation", R5(13), R5(21), AF.Exp, scale=-C0, r=[("R", 21)], w=[("R", 13)])
        op("dve", "tensor_tensor", R5(20), R5(21), sgw, ALU.subtract, r=[("R", 21), ("R", 7)], w=[("R", 20)])
        yield
        op("act", "activation", R5(14), R5(20), AF.Exp, scale=-C0, r=[("R", 20)], w=[("R", 14)])
        op("act", "activation", R5(15), R5(21), AF.Exp, scale=C0, r=[("R", 21)], w=[("R", 15)])
        yield

    def rwkv_unit_rest(tb, hp, filler):
        par = hp % 2
        P = lambda c: pc[:, hp * 10 + c:hp * 10 + c + 1]
        r32, kx, v32 = R5(3), R5(4), R5(5)
        kap, kh, b32 = R5(10), R5(11), R5(12)
        eL, eLm, einv = R5(13), R5(14), R5(15)
        wc_ = wct[:, par, :]
        wkey = ("wct", par)
        if hp + 1 < 8:
            for j in range(3):
                prefetch(wr[2 + (hp + 1) * 4 + j], ("wr", tb, hp + 1, j))
        op("dve", "tensor_copy", wc_, eL.rearrange("p (c j) -> p c j", j=64)[:, :, 63], r=[("R", 13)], w=[wkey])
        op("dve", "tensor_tensor", R5(21).rearrange("p (c j) -> p c j", j=64), einv.rearrange("p (c j) -> p c j", j=64),
           wc_.unsqueeze(2).to_broadcast([128, 8, 64]), ALU.mult, r=[("R", 15), wkey], w=[("R", 21)])
        ehat = R5(21)
        op("dve", "tensor_tensor", M(RT_), r32, eL, ALU.mult, r=[("R", 3), ("R", 13)], w=[("M", RT_)])
        c3 = lambda ap, h: ap[h * 64:(h + 1) * 64, :].rearrange("p (c j) -> p c j", j=64)
        mk = lambda s: [("M", s), ("M", s + 1)]
        for h in range(2):
            blk = lambda s: M8(s)[h * 64:(h + 1) * 64, :, h * 64:(h + 1) * 64]
            e1 = "dve" if h == 0 else "pool"
            op(e1, "tensor_tensor", blk(KB), c3(kap, h), c3(eLm, h), ALU.mult, r=[("R", 10), ("R", 14)], w=mk(KB))
            op(e1, "tensor_tensor", blk(BB), c3(b32, h), c3(einv, h), ALU.mult, r=[("R", 12), ("R", 15)], w=mk(BB))
            op(e1, "tensor_tensor", blk(KHB), c3(kh, h), c3(einv, h), ALU.mult, r=[("R", 11), ("R", 15)], w=mk(KHB))
            op("pool", "tensor_copy", blk(VB), c3(v32, h), r=[("R", 5)], w=mk(VB))
        for c in range(8):
            kb, bbv, khb = M8(KB)[:, c, :], M8(BB)[:, c, :], M8(KHB)[:, c, :]
            rt = M(RT_)[:, c * 64:(c + 1) * 64]
            rd = mk(KB) + mk(BB) + mk(KHB) + [("M", RT_)]
            k3 = 3 if (c % 2 == 0 or not OPT["aalt"]) else 5
            op("pe", "matmul", pb[k3][:, 0:128], bbv, kb, start=True, stop=True, r=rd, w=[("pb", k3)])
            op("pe", "matmul", pb[k3][:, 128:256], kb, bbv, start=True, stop=True, r=rd, w=[("pb", k3)])
            op("pe", "matmul", pb[k3][:, 256:384], khb, kb, start=True, stop=True, r=rd, w=[("pb", k3)])
            op("pe", "matmul", pb[k3][:, 384:448], bbv, rt, start=True, stop=True, r=rd, w=[("pb", k3)])
            op("pe", "matmul", pb[k3][:, 448:512], khb, rt, start=True, stop=True, r=rd, w=[("pb", k3)])
            op("dve" if c % 2 == 0 else "dve", "tensor_tensor", M(AM0 + c), pb[k3][:, :], maskA, ALU.mult, r=[("pb", k3), "cst"], w=[("M", AM0 + c)])
        for h in range(2):
            blk = lambda s: M8(s)[h * 64:(h + 1) * 64, :, h * 64:(h + 1) * 64]
            e1 = "dve" if h == 0 else "pool"
            op(e1, "tensor_tensor", blk(BB), c3(b32, h), c3(ehat, h), ALU.mult, r=[("R", 12), ("R", 21)], w=mk(BB))
            op(e1, "tensor_tensor", blk(KHB), c3(kh, h), c3(ehat, h), ALU.mult, r=[("R", 11), ("R", 21)], w=mk(KHB))
        LQ = [(QA, QB), (25, 26)]
        LP = [(PA, PB), (27, 28)]
        LT = [(TA, TB), (29, 30)]
        QBANK, PBANK, TBANK = (5, 3), (6, 2), (4, 5)
        pkeys = lambda bk: [("pb6", i_) for i_ in range(4)] if bk == 6 else [("pb", bk)]
        state = []
        for g in range(2):
            amk = [("M", AM0 + 4 * g + j) for j in range(4)]
            AMg = Mt[:, AM0 + 4 * g:AM0 + 4 * g + 4, :]
            op("dve", "tensor_tensor", M4(LT[g][0]), AMg[:, :, 0:128], ident_b.unsqueeze(1).to_broadcast([128, 4, 128]), ALU.add,
               r=amk + ["cstb"], w=[("M", LT[g][0])])
            state.append(dict(q_of=(lambda A: (lambda j: A[:, j, 0:128]))(AMg), p_of=(lambda A: (lambda j: A[:, j, 128:256]))(AMg),
                              qk=amk, pk=amk, tcur=LT[g][0]))
        for l in range(1, 6):
            for g in range(2):
                s_ = state[g]
                qn, pn = LQ[g][l % 2], LP[g][l % 2]
                if l < 5:
                    bk = QBANK[g]
                    for j in range(4):
                        op("pe", "matmul", pb[bk][:, j * 128:(j + 1) * 128], s_["p_of"](j), s_["q_of"](j), start=True, stop=True,
                           r=s_["qk"] + s_["pk"], w=pkeys(bk))
                    op("act", "copy", M(qn), pb[bk][:, :], r=pkeys(bk), w=[("M", qn)])
                bk = PBANK[g]
                for j in range(4):
                    op("pe", "matmul", pb[bk][:, j * 128:(j + 1) * 128], s_["q_of"](j), s_["p_of"](j), start=True, stop=True,
                       r=s_["qk"] + s_["pk"], w=pkeys(bk))
                op("dve", "tensor_copy", M(pn), pb[bk][:, :], r=pkeys(bk), w=[("M", pn)])
            filler.fill(3)
            for g in range(2):
                s_ = state[g]
                qn, pn = LQ[g][l % 2], LP[g][l % 2]
                tcur = s_["tcur"]
                tn = LT[g][1] if tcur == LT[g][0] else LT[g][0]
                bk = TBANK[g]
                for j in range(4):
                    op("pe", "matmul", pb[bk][:, j * 128:(j + 1) * 128], M4(pn)[:, j, :], M4(tcur)[:, j, :], start=True, stop=False,
                       r=[("M", pn), ("M", tcur)], w=pkeys(bk))
                    op("pe", "matmul", pb[bk][:, j * 128:(j + 1) * 128], ident_b, M4(tcur)[:, j, :], start=False, stop=True,
                       r=["cstb", ("M", tcur)], w=pkeys(bk))
                if l < 5:
                    op("act" if g == 0 else "dve", "copy" if g == 0 else "tensor_copy", M(tn), pb[bk][:, :], r=pkeys(bk), w=[("M", tn)])
                    s_["tcur"] = tn
                    s_["q_of"] = (lambda qn_: (lambda j: M4(qn_)[:, j, :]))(qn)
                    s_["qk"] = [("M", qn)]
                else:
                    op("act" if g == 0 else "dve", "copy" if g == 0 else "tensor_copy", M(TTF + g), pb[bk][:, :], r=pkeys(bk), w=[("M", TTF + g)])
                s_["p_of"] = (lambda pn_: (lambda j: M4(pn_)[:, j, :]))(pn)
                s_["pk"] = [("M", pn)]
            filler.fill(3)
        for src, dst in ((BB, BT), (KHB, KHT), (VB, VT)):
            for c in range(8):
                op("pe", "transpose", pb7[:, c * 128:(c + 1) * 128], M8(src)[:, c, :], ident_b, r=mk(src) + ["cstb"], w=["pb7"])
            op("act", "copy", Mt[:, dst:dst + 2, :].rearrange("p a t -> p (a t)"), pb7[:, :], r=["pb7"], w=mk(dst))
            filler.fill(2)
        sbf = Sbf[:, hp, :]
        s32 = S32[:, hp, :]
        skey, s32key = ("Sbf", hp), ("S32", hp)
        xn, uu = M(XU)[:, 0:128], M(XU)[:, 128:256]
        for c in range(8):
            am = M(AM0 + c)
            amk1 = [("M", AM0 + c)]
            rt = M(RT_)[:, c * 64:(c + 1) * 64]
            vt, bt, kht = M8(VT)[:, c, :], M8(BT)[:, c, :], M8(KHT)[:, c, :]
            op("pe", "matmul", pb[6][:, 0:128], M8(KB)[:, c, :], sbf, start=True, stop=False, r=mk(KB) + [skey], w=[("pb6", 0)])
            op("pe", "matmul", pb[6][:, 0:128], am[:, 256:384], vt, start=False, stop=True, r=amk1 + mk(VT), w=[("pb6", 0)])
            ysl = pb[4][:, c * 64:(c + 1) * 64]
            op("pe", "matmul", ysl, sbf, rt, start=True, stop=False, r=[skey, ("M", RT_)], w=[("pb", 4)])
            op("pe", "matmul", ysl, vt, am[:, 448:512], start=False, stop=False, r=mk(VT) + amk1, w=[("pb", 4)])
            op("act", "mul", xn, pb[6][:, 0:128], -1.0, r=[("pb6", 0)], w=[("XU", 0)])
            filler.fill(1)
            op("pe", "matmul", pb[6][:, 128:256], M8(TTF)[:, c, :], xn, start=True, stop=True, r=mk(TTF) + [("XU", 0)], w=[("pb6", 1)])
            op("dve", "tensor_copy", uu, pb[6][:, 128:256], r=[("pb6", 1)], w=[("XU", 1)])
            filler.fill(1)
            op("pe", "matmul", pb[6][:, 256:384], bt, uu, start=True, stop=False, r=mk(BT) + [("XU", 1)], w=[("pb6", 2)])
            op("pe", "matmul", pb[6][:, 256:384], kht, vt, start=False, stop=True, r=mk(KHT) + mk(VT), w=[("pb6", 2)])
            op("pe", "matmul", ysl, uu, am[:, 384:448], start=False, stop=True, r=[("XU", 1)] + amk1, w=[("pb", 4)])
            op("dve", "scalar_tensor_tensor", sbf, s32, wc_[:, c:c + 1], pb[6][:, 256:384], ALU.mult, ALU.add,
               r=[s32key, wkey, ("pb6", 2)], w=[skey])
            op("dve", "scalar_tensor_tensor", s32, s32, wc_[:, c:c + 1], pb[6][:, 256:384], ALU.mult, ALU.add,
               r=[s32key, wkey, ("pb6", 2)], w=[s32key])
            filler.fill(1)
        filler.drain()
        sgs, bon = SG[par], BON[par]
        op("act", "copy", R5(G1), pb[4][:, :], r=[("pb", 4)], w=[("R", G1)])

        def gen_G():
            op("act", "activation", R5(G2), R5(G1), AF.Square, r=[("R", G1)], w=[("R", G2)])
            op("pe", "matmul", pb[2][:, :], bdones, R5(G1), start=True, stop=True, r=["cst", ("R", G1)], w=[("pb", 2)])
            op("act", "mul", R5(G3), pb[2][:, :], 1.0 / 64, r=[("pb", 2)], w=[("R", G3)])
            yield
            op("pe", "matmul", pb[2][:, :], bdones, R5(G2), start=True, stop=True, r=["cst", ("R", G2)], w=[("pb", 2)])
            op("dve", "tensor_tensor", R5(G2), R5(G3), R5(G3), ALU.mult, r=[("R", G3)], w=[("R", G2)])
            op("dve", "scalar_tensor_tensor", R5(G2), pb[2][:, :], 1.0 / 64, R5(G2), ALU.mult, ALU.subtract, r=[("pb", 2), ("R", G2)], w=[("R", G2)])
            yield
            op("dve", "tensor_scalar", R5(G2), R5(G2), 64e-5, None, ALU.add, r=[("R", G2)], w=[("R", G2)])
            if OPT["ln"]:
                op("act", "activation", R5(G2), R5(G2), AF.Ln, r=[("R", G2)], w=[("R", G2)])
                op("act", "activation", R5(G2), R5(G2), AF.Exp, scale=-0.5, r=[("R", G2)], w=[("R", G2)])
            else:
                op("act", "activation", R5(G2), R5(G2), AF.Sqrt, r=[("R", G2)], w=[("R", G2)])
                op("dve", "reciprocal", R5(G2), R5(G2), r=[("R", G2)], w=[("R", G2)])
            yield
            op("dve", "tensor_tensor", R5(G1), R5(G1), R5(G3), ALU.subtract, r=[("R", G1), ("R", G3)], w=[("R", G1)])
            yield
            op("dve", "tensor_tensor", R5(G1), R5(G1), R5(G2), ALU.mult, r=[("R", G1), ("R", G2)], w=[("R", G1)])
            yield
            op("dve", "tensor_scalar", R5(G1), R5(G1), P(8), P(9), ALU.mult, ALU.add, r=[("R", G1), "pc"], w=[("R", G1)])
            yield
            op("dve", "tensor_tensor", R5(G1), R5(G1), R5(bon), ALU.add, r=[("R", G1), ("R", bon)], w=[("R", G1)])
            yield
            op("dve", "tensor_tensor", M(YO), R5(G1), R5(sgs), ALU.mult, r=[("R", G1), ("R", sgs)], w=[("M", YO)])
            dma("sp", zsend[tb][hp * 128:(hp + 1) * 128, :], M(YO), r=[("M", YO)], w=[("zsend", tb)], key="zs")
            yield

        return gen_G()

    def rwkv_block(tb):
        prefetch(wr[0], ("wa", tb, 0))
        prefetch(wr[1], ("wa", tb, 1))
        prefetch(wr[2], ("wr", tb, 0, 0))
        build_hT(xf, [(tb * 512 + tt * 128, 128) for tt in range(4)], 32)
        for j in range(2):
            wi = load_w(wr[j], tag=("wa", tb, j))
            k_ = next_pb()
            proj(wi, 96, 32, hcols, pb[k_][0:96, :], ("pb", k_))
            shift(k_, 96, 0, 24 + j, pc[0:96, 80 + j:81 + j], 1, 17 + j)
        op("act", "activation", R5(17)[0:96], R5(17)[0:96], AF.Tanh, r=[("R", 17)], w=[("R", 17)])
        filler = Filler()
        filler.add(gen_P(tb, 0))
        filler.add(gen_ER(tb, 0))
        filler.drain()
        gG = None
        for hp in range(8):
            if gG is not None:
                filler.add(gG)
            if hp + 1 < 8:
                filler.add(gen_P(tb, hp + 1))
                filler.add(gen_ER(tb, hp + 1))
            gG = rwkv_unit_rest(tb, hp, filler)
            if not OPT["gdefer"]:
                for _ in gG:
                    pass
                gG = None
        if gG is not None:
            filler.add(gG)
        filler.drain()
        if n_conv_blocks:
            S.dma("pool", lambda e: e.collective_compute("AllGather", ALU.bypass, replica_groups=[[0, 1], [2, 3], [4, 5], [6, 7]],
                                                         ins=[zsend[tb]], outs=[zall[tb]]), reads=[("zsend", tb)], writes=[("zall", tb)],
                  key=("cc", tb), inc=1)

    for tb in range(n_rwkv_blocks):
        rwkv_block(tb)

    if n_conv_blocks:
        for tb in range(n_rwkv_blocks):
            op("pool", "memset", small[:, 8:9], 0.0, r=[("zall", tb)], w=[("zall", tb), "sm8"])

    CT = lambda cc: R5(cc)
    SV = lambda cc: Rt[:, 16 + cc // 2, :].bitcast(BF16)[:, (cc % 2) * 512:(cc % 2) * 512 + 512]
    skey_ = lambda cc: ("R", 16 + cc // 2)
    UB = [24, 25]
    YR0, YC0 = 0, 16

    def conv_block(b):
        first = (b == 0)
        if first:
            build_hT(xc, [(0, 32)] + [(32 + tt * 128, 128) for tt in range(4)], 0)
        else:
            build_hT(xc, [(HALO + b * 512 + tt * 128, 128) for tt in range(4)], 32)
        for cc in range(16):
            cp = lambda c: pc[:, 82 + cc * 4 + c:82 + cc * 4 + c + 1]
            ub = UB[cc % 2]
            U = Rt[:, ub, 0:544]
            wv = load_w(wc[2 * cc])
            proj(wv, 128, 32, hcols, pb[0][:, :], ("pb", 0))
            if first:
                proj(wv, 128, 32, lambda k: hT[:, k, 0:32], pb[2][:, 0:32], ("pb", 2))
            wg = load_w(wc[2 * cc + 1])
            proj(wg, 128, 32, hcols, pb[1][:, :], ("pb", 1))
            if first:
                proj(wg, 128, 32, lambda k: hT[:, k, 0:32], pb[2][:, 32:64], ("pb", 2))
            op("act", "activation", XT[:, 0, 0:512], pb[1][:, :], AF.Sigmoid, r=[("pb", 1)], w=[("XT", 0)])
            op("dve", "tensor_tensor", U[:, 32:544], pb[0][:, :], XT[:, 0, 0:512], ALU.mult, r=[("pb", 0), ("XT", 0)], w=[("R", ub)])
            if first:
                op("act", "activation", XT[:, 1, 0:32], pb[2][:, 32:64], AF.Sigmoid, r=[("pb", 2)], w=[("XT", 1)])
                op("dve", "tensor_tensor", U[:, 0:32], pb[2][:, 0:32], XT[:, 1, 0:32], ALU.mult, r=[("pb", 2), ("XT", 1)], w=[("R", ub)])
            else:
                op("act", "copy", U[:, 0:32], ucar[:, cc, :], r=[("ucar", cc)], w=[("R", ub)])
            op("act", "copy", ucar[:, cc, :], U[:, 512:544], r=[("R", ub)], w=[("ucar", cc)])
            ce = "dve"
            wk = lambda k: pc[:, 146 + cc * 31 + k:146 + cc * 31 + k + 1]
            op(ce, "tensor_scalar", CT(cc), U[:, 2:514], wk(0), cp(0), ALU.mult, ALU.add, r=[("R", ub), "pc"], w=[("R", cc)])
            for k in range(1, 31):
                op(ce, "scalar_tensor_tensor", CT(cc), U[:, 2 + k:514 + k], wk(k), CT(cc), ALU.mult, ALU.add,
                   r=[("R", ub), ("R", cc), "pc"], w=[("R", cc)])
            op("act", "activation", XT[:, 2, 0:512], CT(cc), AF.Square, r=[("R", cc)], w=[("XT", 2)])
            op("pe", "matmul", pb[3][:, :], ones_f, CT(cc), start=(cc == 0), stop=(cc == 15), r=["cst", ("R", cc)], w=[("pb", 3)])
            op("pe", "matmul", pb[4][:, :], ones_f, XT[:, 2, 0:512], start=(cc == 0), stop=(cc == 15), r=["cst", ("XT", 2)], w=[("pb", 4)])
        mean, rstd = XT[:, 0, 0:512], XT[:, 1, 0:512]
        op("act", "mul", mean, pb[3][:, :], 1.0 / 2048, r=[("pb", 3)], w=[("XT", 0)])
        op("dve", "tensor_tensor", XT[:, 2, 0:512], mean, mean, ALU.mult, r=[("XT", 0)], w=[("XT", 2)])
        op("dve", "scalar_tensor_tensor", rstd, pb[4][:, :], 1.0 / 2048, XT[:, 2, 0:512], ALU.mult, ALU.subtract, r=[("pb", 4), ("XT", 2)], w=[("XT", 1)])
        op("dve", "tensor_scalar", rstd, rstd, 1e-5, None, ALU.add, r=[("XT", 1)], w=[("XT", 1)])
        op("act", "activation", rstd, rstd, AF.Sqrt, r=[("XT", 1)], w=[("XT", 1)])
        op("dve", "reciprocal", rstd, rstd, r=[("XT", 1)], w=[("XT", 1)])
        for cc in range(16):
            cp = lambda c: pc[:, 82 + cc * 4 + c:82 + cc * 4 + c + 1]
            op("dve", "tensor_tensor", CT(cc), CT(cc), mean, ALU.subtract, r=[("R", cc), ("XT", 0)], w=[("R", cc)])
            op("pool", "tensor_tensor", CT(cc), CT(cc), rstd, ALU.mult, r=[("R", cc), ("XT", 1)], w=[("R", cc)])
            op("act", "activation", SV(cc), CT(cc), AF.Silu, bias=cp(2), scale=cp(1), r=[("R", cc), "pc"], w=[skey_(cc)])
        for e_ in range(16):
            cp = lambda c: pc[:, 82 + e_ * 4 + c:82 + e_ * 4 + c + 1]
            wgc = load_w(wc[32 + e_])
            proj(wgc, 128, 32, hcols, pb[0][:, :], ("pb", 0))
            op("act", "activation", XT[:, 2, 0:512], pb[0][:, :], AF.Silu, r=[("pb", 0)], w=[("XT", 2)])
            wpi = load_w(wp[e_], 2048)
            for k in range(16):
                op("pe", "matmul", pb[1][:, :], wbuf[wpi][:, k * 128:(k + 1) * 128], SV(k), start=(k == 0), stop=(k == 15),
                   r=[("w", wpi), skey_(k)], w=[("pb", 1)])
            op("dve", "scalar_tensor_tensor", M(YC0 + e_), pb[1][:, :], cp(3), XT[:, 2, 0:512], ALU.add, ALU.mult,
               r=[("pb", 1), ("XT", 2), "pc"], w=[("M", YC0 + e_)])
        for k in range(16):
            sl = k % 2
            dma("sp", ystg[:, 2 * sl, :], zall[b][k * 128:(k + 1) * 128, :], r=[("zall", b)], w=[("ys", 2 * sl)], key=("ys", 2 * sl))
            op("pool", "tensor_scalar", M(YR0 + k), ystg[:, 2 * sl, :], sel[:, 0:1], None, ALU.mult, r=[("ys", 2 * sl), "sel"], w=[("M", YR0 + k)])
            if 4 + b >= n_rwkv_blocks:
                continue
            dma("sp", ystg[:, 2 * sl + 1, :], zall[4 + b][k * 128:(k + 1) * 128, :], r=[("zall", 4 + b)], w=[("ys", 2 * sl + 1)],
                key=("ys", 2 * sl + 1))
            op("pool", "tensor_scalar", ystg[:, 2 * sl + 1, :], ystg[:, 2 * sl + 1, :], sel[:, 1:2], None, ALU.mult,
               r=[("ys", 2 * sl + 1), "sel"], w=[("ys", 2 * sl + 1)])
            op("pool", "tensor_tensor", M(YR0 + k), M(YR0 + k), ystg[:, 2 * sl + 1, :], ALU.add,
               r=[("ys", 2 * sl + 1), ("M", YR0 + k)], w=[("M", YR0 + k)])
        for dch in range(32):
            wi = load_w(wo[dch])
            k_ = dch % 2
            for k in range(32):
                op("pe", "matmul", pb[k_][:, :], wbuf[wi][:, k * 128:(k + 1) * 128], M(k), start=(k == 0), stop=(k == 31),
                   r=[("w", wi), ("M", k)], w=[("pb", k_)])
            op("act", "copy", R5(dch), pb[k_][:, :], r=[("pb", k_)], w=[("R", dch)])
            op("act", "activation", XT[:, dch % 2, 0:512], pb[k_][:, :], AF.Square, r=[("pb", k_)], w=[("XT", dch % 2)])
            op("pe", "matmul", pb[5][:, :], ones_f, XT[:, dch % 2, 0:512], start=(dch == 0), stop=(dch == 31), r=["cst", ("XT", dch % 2)], w=[("pb", 5)])
        rs = XT[:, 2, 0:512]
        op("dve", "tensor_scalar", rs, pb[5][:, :], 1.0 / D, 1e-6, ALU.mult, ALU.add, r=[("pb", 5)], w=[("XT", 2)])
        op("act", "activation", rs, rs, AF.Sqrt, r=[("XT", 2)], w=[("XT", 2)])
        op("dve", "reciprocal", rs, rs, r=[("XT", 2)], w=[("XT", 2)])
        for dch in range(32):
            op("dve", "scalar_tensor_tensor", R5(dch), R5(dch), pc[:, 674 + dch:675 + dch], rs, ALU.mult, ALU.mult,
               r=[("R", dch), ("XT", 2), "pc"], w=[("R", dch)])
        for tt in range(4):
            row_x = HALO + b * 512 + tt * 128
            dma("sp", io32[:, :], xc[row_x:row_x + 128, :], w=["io32"], key="io32")
            for g8 in range(8):
                k_ = g8 % 2
                for j in range(4):
                    dch = g8 * 4 + j
                    op("pe", "transpose", pb[k_][:, j * 128:(j + 1) * 128], R5(dch)[:, tt * 128:(tt + 1) * 128], ident_f,
                       r=[("R", dch), "cst"], w=[("pb", k_)])
                op("dve", "tensor_tensor", io32[:, g8 * 512:(g8 + 1) * 512], io32[:, g8 * 512:(g8 + 1) * 512], pb[k_][:, :], ALU.add,
                   r=["io32", ("pb", k_)], w=["io32"])
            row_y = b * 512 + tt * 128
            dma("sp", yout[row_y:row_y + 128, :], io32[:, :], r=["io32"], key="yout")

    for b in range(n_conv_blocks):
        conv_block(b)

    counts = S.emit(final_waits=["yout"] if n_conv_blocks else ["zs"])
    st.close()
    return nc, counts


def _chunk_layout(wcols):
    m = wcols.shape[1]
    nk = wcols.shape[0] // 128
    a = np.zeros((nk, 128, 128), np.float32)
    a[:, :, :m] = wcols.reshape(nk, 128, m)
    return np.ascontiguousarray(a.transpose(1, 0, 2)).reshape(128, nk * 128)


def _consts():
    cst = np.zeros((128, 1408), np.float32)
    cst[:, 0:128] = np.eye(128, dtype=np.float32)
    idx = np.arange(128)
    same = (idx[:, None] // 64) == (idx[None, :] // 64)
    cst[:, 128:256] = same.astype(np.float32)
    cst[:, 256:384] = 1.0
    s = idx[:, None] % 64
    t = idx[None, :] % 64
    cst[:, 384:512] = -(same & (s < t)).astype(np.float32)
    cst[:, 512:640] = -(same & (s > t)).astype(np.float32)
    cst[:, 640:768] = (same & (s < t)).astype(np.float32)
    incl = (s[:, 0:1] <= np.arange(64)[None, :]).astype(np.float32)
    cst[:, 768:832] = incl
    cst[:, 832:896] = incl
    r = np.ones(512, np.float32)
    r[0::64] = 0.0
    cst[:, 896:1408] = r[None, :]
    return cst


_CACHE = {}
_NB = (8, 4)


def kernel(x, norm_pre_g, w_in, mu_shift, w0, w_lora_up, a0, a_lora_up, k_k, k_a, r_k, lnx_g, lnx_b, conv_w, conv_b,
           cln_g, cln_b, w_pw2, b_pw2, w_out, norm_post_g):
    f = lambda a: np.asarray(a, dtype=np.float32)
    x, w_in, w_out, w_pw2 = f(x), f(w_in), f(w_out), f(w_pw2)
    mu_shift, w0, a0, k_k, k_a = f(mu_shift), f(w0), f(a0), f(k_k), f(k_a)
    r_k, lnx_g, lnx_b = f(r_k).reshape(-1), f(lnx_g), f(lnx_b)
    conv_w, conv_b, cln_g, cln_b, b_pw2 = f(conv_w), f(conv_b), f(cln_g), f(cln_b), f(b_pw2)
    norm_pre_g, norm_post_g = f(norm_pre_g), f(norm_post_g)
    w_lora_up, a_lora_up = f(w_lora_up), f(a_lora_up)

    if "nc" not in _CACHE:
        _CACHE["nc"] = build_program(*_NB)[0]
    nc = _CACHE["nc"]

    cst = _consts()
    wc_l = np.empty((48, 128, 4096), np.float32)
    c1 = 6336 + 2048
    for cc in range(16):
        wc_l[2 * cc] = _chunk_layout(w_in[:, c1 + cc * 128:c1 + (cc + 1) * 128])
        wc_l[2 * cc + 1] = _chunk_layout(w_in[:, c1 + 2048 + cc * 128:c1 + 2048 + (cc + 1) * 128])
        wc_l[32 + cc] = _chunk_layout(w_in[:, c1 + 4096 + cc * 128:c1 + 4096 + (cc + 1) * 128])
    wp_l = np.stack([_chunk_layout(w_pw2[:, e * 128:(e + 1) * 128]) for e in range(16)])
    wo_l = np.stack([_chunk_layout(w_out[:, dch * 128:(dch + 1) * 128]) for dch in range(32)])

    in_maps = []
    for c in range(NCORES):
        b, hh = c // 2, c % 2
        ch0 = hh * 1024
        wr_l = np.empty((34, 128, 4096), np.float32)
        wr_l[0] = _chunk_layout(w_in[:, 6144:6240])
        wr_l[1] = _chunk_layout(w_in[:, 6240:6336])
        pcm = np.zeros((128, NPC), np.float32)
        for hp in range(8):
            lo = ch0 + hp * 128
            for j in range(3):
                wr_l[2 + hp * 4 + j] = _chunk_layout(w_in[:, j * 2048 + lo:j * 2048 + lo + 128])
            wr_l[2 + hp * 4 + 3] = _chunk_layout(w_in[:, 6336 + lo:6336 + lo + 128])
            cols = [mu_shift[lo:lo + 128], mu_shift[2048 + lo:2048 + lo + 128], mu_shift[4096 + lo:4096 + lo + 128],
                    w0[lo:lo + 128], a0[lo:lo + 128], k_k[lo:lo + 128], k_a[lo:lo + 128], r_k[lo:lo + 128],
                    lnx_g[lo:lo + 128], lnx_b[lo:lo + 128]]
            for j, v in enumerate(cols):
                pcm[:, hp * 10 + j] = v
        pcm[0:96, 80] = mu_shift[6144:6240]
        pcm[0:96, 81] = mu_shift[6240:6336]
        for cc in range(16):
            sl = slice(cc * 128, (cc + 1) * 128)
            pcm[:, 82 + cc * 4 + 0] = conv_b[sl]
            pcm[:, 82 + cc * 4 + 1] = cln_g[sl]
            pcm[:, 82 + cc * 4 + 2] = cln_b[sl]
            pcm[:, 82 + cc * 4 + 3] = b_pw2[sl]
            pcm[:, 146 + cc * 31:146 + (cc + 1) * 31] = conv_w[:, sl].T
        pcm[:, 642:674] = norm_pre_g.reshape(32, 128).T
        pcm[:, 674:706] = norm_post_g.reshape(32, 128).T
        lor = np.concatenate([w_lora_up[:, ch0:ch0 + 1024], a_lora_up[:, ch0:ch0 + 1024]], axis=1)
        xcv = np.zeros((HALO + TOKC, D), np.float32)
        t0 = hh * TOKC
        if hh == 1:
            xcv[0:HALO] = x[b, t0 - HALO:t0]
        xcv[HALO:] = x[b, t0:t0 + TOKC]
        selv = np.zeros((128, 2), np.float32)
        selv[:, hh] = 1.0
        in_maps.append({"xf": np.ascontiguousarray(x[b]), "xc": xcv, "wr": wr_l, "wc": wc_l, "wp": wp_l, "wo": wo_l,
                        "lora": np.ascontiguousarray(lor), "pc": pcm, "cst": cst, "sel": selv})
    res = run_bass_kernel_spmd(nc, in_maps, core_ids=list(range(NCORES)))
    out = np.empty((4, T, D), np.float32)
    for c in range(NCORES):
        b, hh = c // 2, c % 2
        out[b, hh * TOKC:(hh + 1) * TOKC] = res.results[c]["y"]
    return out
```

```python
import contextlib
import numpy as np
import concourse.bass as bass
import concourse.mybir as mybir
from concourse.bass_utils import run_bass_kernel_spmd

F32 = mybir.dt.float32
BF16 = mybir.dt.bfloat16
AF = mybir.ActivationFunctionType
ALU = mybir.AluOpType

D = 4096
T = 4096
NCORES = 8
TOKC = 2048
HALO = 32
NPC = 720
C0 = 0.6065306597126334
EPOCH = 20000
OPT = dict(prefetch=True, aalt=True, ln=True, gdefer=False)


class Sched:
    def __init__(self, nc):
        self.nc = nc
        self.ops = []
        self.last_w = {}
        self.readers = {}
        self.dma_keys = {}

    def _deps(self, i, reads, writes):
        deps = set()
        for k in reads:
            w = self.last_w.get(k)
            if w is not None:
                deps.add((w, "raw"))
        for k in writes:
            w = self.last_w.get(k)
            if w is not None:
                deps.add((w, "waw"))
            for r in self.readers.get(k, ()):
                deps.add((r, "war"))
        for k in reads:
            self.readers.setdefault(k, []).append(i)
        for k in writes:
            self.last_w[k] = i
            self.readers[k] = []
        return deps

    def op(self, eng, fn, reads=(), writes=()):
        i = len(self.ops)
        self.ops.append(dict(eng=eng, fn=fn, deps=self._deps(i, reads, writes), dma=None))

    def dma(self, eng, fn, reads=(), writes=(), key=None, inc=16):
        i = len(self.ops)
        deps = self._deps(i, reads, writes)
        n = self.dma_keys.get(key, (0, inc))[0] + 1
        self.dma_keys[key] = (n, inc)
        self.ops.append(dict(eng=eng, fn=fn, deps=deps, dma=(key, n, inc)))

    def emit(self, final_waits=()):
        nc = self.nc
        ops = self.ops
        need = [False] * len(ops)
        for i, o in enumerate(ops):
            for (d, kind) in o["deps"]:
                p = ops[d]
                if p["dma"] is not None:
                    continue
                if p["eng"] == o["eng"] and (p["eng"] == "pe" or kind != "raw"):
                    continue
                need[d] = True
        cnt = {e: 0 for e in ("pe", "act", "dve", "pool", "sp")}
        sig = [None] * len(ops)
        for i, o in enumerate(ops):
            if need[i]:
                e = o["eng"]
                sig[i] = (e, cnt[e] // EPOCH, cnt[e] % EPOCH + 1)
                cnt[e] += 1
        stack = contextlib.ExitStack()
        sems = {}
        for e in cnt:
            for ep in range((cnt[e] + EPOCH - 1) // EPOCH):
                sems[(e, ep)] = stack.enter_context(nc.semaphore(f"c_{e}_{ep}"))
        dsems = {}
        for k in self.dma_keys:
            dsems[k] = stack.enter_context(nc.semaphore(f"d_{len(dsems)}"))
        per_eng = {e: [] for e in cnt}
        for i, o in enumerate(ops):
            per_eng[o["eng"]].append(i)
        block = stack.enter_context(nc.Block())
        engs = {"pe": nc.tensor, "act": nc.scalar, "dve": nc.vector, "pool": nc.gpsimd, "sp": nc.sync}

        def run_engine(ename, final):
            eng = engs[ename]
            seen = {}
            for i in per_eng[ename]:
                o = ops[i]
                waits = {}
                for (d, kind) in o["deps"]:
                    p = ops[d]
                    if p["dma"] is not None:
                        key, n, inc = p["dma"]
                        s, v = ("d", key), inc * n
                    else:
                        if sig[d] is None:
                            continue
                        if p["eng"] == ename and (ename == "pe" or kind != "raw"):
                            continue
                        e, ep, v = sig[d]
                        s = (e, ep)
                    if seen.get(s, 0) >= v:
                        continue
                    waits[s] = max(waits.get(s, 0), v)
                for s, v in waits.items():
                    seen[s] = v
                    eng.wait_ge(dsems[s[1]] if s[0] == "d" else sems[s], v)
                ins = o["fn"](eng)
                if o["dma"] is not None:
                    ins.then_inc(dsems[o["dma"][0]], o["dma"][2])
                elif sig[i] is not None:
                    ins.then_inc(sems[sig[i][:2]], 1)
            for key in final:
                n, inc = self.dma_keys[key]
                eng.wait_ge(dsems[key], inc * n)

        block.tensor(lambda e: run_engine("pe", ()))
        block.scalar(lambda e: run_engine("act", ()))
        block.vector(lambda e: run_engine("dve", ()))
        block.gpsimd(lambda e: run_engine("pool", ()))
        block.sync(lambda e: run_engine("sp", final_waits))
        stack.close()
        return {e: len(per_eng[e]) for e in per_eng}


def build_program(n_rwkv_blocks=8, n_conv_blocks=4):
    nc = bass.Bass("TRN2", target_bir_lowering=False)
    dt_in = lambda name, shape: nc.dram_tensor(name, shape, F32, kind="ExternalInput").ap()
    xf = dt_in("xf", [T, D])
    tiny = (n_conv_blocks == 0)
    xc = dt_in("xc", [HALO + TOKC, D] if not tiny else [128, 128])
    wr = dt_in("wr", [34, 128, 4096])
    wc = dt_in("wc", [48, 128, 4096] if not tiny else [1, 128, 128])
    wp = dt_in("wp", [16, 128, 2048] if not tiny else [1, 128, 128])
    wo = dt_in("wo", [32, 128, 4096] if not tiny else [1, 128, 128])
    lora = dt_in("lora", [96, 2048])
    pcd = dt_in("pc", [128, NPC])
    cstd = dt_in("cst", [128, 1408])
    seld = dt_in("sel", [128, 2])
    yout = nc.dram_tensor("y", [TOKC, D], F32, kind="ExternalOutput").ap()
    zsend = [nc.dram_tensor(f"zsend{i}", [1024, 512], BF16).ap() for i in range(8)]
    zall = [nc.dram_tensor(f"zall{i}", [2048, 512], BF16).ap() for i in range(8)]

    S = Sched(nc)
    st = contextlib.ExitStack()
    sb = lambda name, shape, dt: st.enter_context(nc.sbuf_tensor("s_" + name, shape, dt))
    ps = lambda name, shape, dt: st.enter_context(nc.psum_tensor(name, shape, dt))

    NR, NM = 32, 36
    hT = sb("hT", [128, 32, 544], BF16)
    NW = 3
    wbuf = [sb(f"wbuf{i}", [128, 4096], BF16) for i in range(NW)]
    io32 = sb("io32", [128, 4096], F32)
    Rt = sb("Rt", [128, NR, 544], F32)
    Mt = sb("Mt", [128, NM, 512], BF16)
    xbf = Mt[:, 28:36, :].rearrange("p a t -> p (a t)")
    XBK = [("M", i_) for i_ in range(28, 36)]
    XT = sb("XT", [128, 3, 512], F32)
    pc = sb("pc", [128, NPC], F32)
    cst = sb("cst", [128, 1408], F32)
    cstb = sb("cstb", [128, 128], BF16)
    lor = sb("lor", [96, 2, 256], F32)
    sel = sb("sel", [128, 2], F32)
    S32 = sb("S32", [128, 8, 128], F32)
    Sbf = sb("Sbf", [128, 8, 128], BF16)
    carry = sb("carry", [128, 32], F32)
    ucar = sb("ucar", [128, 16, 32], F32)
    small = sb("small", [128, 16], F32)
    ystg = sb("ystg", [128, 4, 512], BF16)
    pb = [ps(f"pb{i}", [128, 512], F32) for i in range(7)]
    pb7 = ps("pb7", [128, 1024], BF16)

    def R(i):
        return Rt[:, i, :]

    def R5(i):
        return Rt[:, i, 0:512]

    def M(i):
        return Mt[:, i, :]

    def M8(i):
        return Mt[:, i:i + 2, :].rearrange("p a (c t) -> p (a c) t", t=128)

    def M4(i):
        return Mt[:, i, :].rearrange("p (c t) -> p c t", t=128)

    ident_f = cst[:, 0:128]
    bdones = cst[:, 128:256]
    ones_f = cst[:, 256:384]
    maskA = cst[:, 384:896]
    rst = cst[:, 896:1408]
    ident_b = cstb[:, 0:128]

    def op(engn, meth, *args, r=(), w=(), **kw):
        S.op(engn, lambda e: getattr(e, meth)(*args, **kw), reads=list(r), writes=list(w))

    def dma(engn, out, in_, r=(), w=(), key=None):
        S.dma(engn, lambda e: e.dma_start(out=out, in_=in_), reads=list(r), writes=list(w), key=key)

    dma("sp", pc[:], pcd, w=["pc"], key="pc")
    dma("sp", cst[:], cstd, w=["cst"], key="cst")
    dma("sp", sel[:], seld, w=["sel"], key="sel")
    dma("pool", cstb[:], cstd[:, 0:128], w=["cstb"], key="cstb")
    op("dve", "memset", S32[:], 0.0, w=[("S32", i) for i in range(8)])
    op("dve", "memset", Sbf[:], 0.0, w=[("Sbf", i) for i in range(8)])
    op("dve", "memset", carry[:], 0.0, w=["carry"])
    op("pool", "memset", Mt[:, 0:8, :], 0.0, w=[("M", i) for i in range(8)])
    for hp in range(8):
        op("dve", "tensor_scalar", pc[:, 706 + hp:707 + hp], pc[:, hp * 10 + 6:hp * 10 + 7], -1.0, 1.0, ALU.mult, ALU.add,
           r=["pc"], w=["pc"])

    wctr = [0]

    pref = {}

    def load_w(src, ncols=4096, tag=None):
        if tag is not None and tag in pref:
            return pref.pop(tag)
        i = wctr[0] % NW
        wctr[0] += 1
        dma("pool", wbuf[i][:, 0:ncols], src, w=[("w", i)], key=("w", i))
        return i

    def prefetch(src, tag, ncols=4096):
        if OPT["prefetch"] and tag not in pref:
            pref[tag] = load_w(src, ncols)

    def proj(wi, M_, nk, rhs_fn, out_ap, wkey):
        for k in range(nk):
            op("pe", "matmul", out_ap, wbuf[wi][:, k * 128:k * 128 + M_], rhs_fn(k), start=(k == 0), stop=(k == nk - 1),
               r=[("w", wi), "hT"], w=[wkey])

    def build_hT(xsrc, tiles, col0):
        col = col0
        for (r0, n) in tiles:
            dma("sp", io32[0:n, :], xsrc[r0:r0 + n, :], w=["io32"], key="io32")
            op("act", "activation", xbf[0:n, :], io32[0:n, :], AF.Square, accum_out=small[0:n, 0:1], r=["io32"], w=XBK + ["sm0"])
            op("dve", "tensor_scalar", small[0:n, 1:2], small[0:n, 0:1], 1.0 / D, 1e-6, ALU.mult, ALU.add, r=["sm0"], w=["sm1"])
            op("act", "activation", small[0:n, 1:2], small[0:n, 1:2], AF.Sqrt, r=["sm1"], w=["sm1"])
            op("dve", "reciprocal", small[0:n, 2:3], small[0:n, 1:2], r=["sm1"], w=["sm2"])
            op("dve", "tensor_scalar", xbf[0:n, :], io32[0:n, :], small[0:n, 2:3], None, ALU.mult, r=["io32", "sm2"], w=XBK)
            for kg in range(4):
                for kk in range(8):
                    k = kg * 8 + kk
                    op("pe", "transpose", pb7[:, kk * 128:kk * 128 + n], xbf[0:n, k * 128:(k + 1) * 128], ident_b[0:n, 0:n],
                       r=XBK + ["cstb"], w=["pb7"])
                for kk in range(8):
                    k = kg * 8 + kk
                    if kk % 2 == 0:
                        op("act", "mul", hT[:, k, col:col + n], pb7[:, kk * 128:kk * 128 + n], pc[:, 642 + k:643 + k],
                           r=["pb7", "pc"], w=["hT"])
                    else:
                        op("dve", "tensor_scalar", hT[:, k, col:col + n], pb7[:, kk * 128:kk * 128 + n], pc[:, 642 + k:643 + k], None,
                           ALU.mult, r=["pb7", "pc"], w=["hT"])
            col += n

    pbi = [0]

    def next_pb():
        pbi[0] ^= 1
        return pbi[0]

    def shift(pbk, np_, raw, ccol, mu, tmp, out):
        op("act", "copy", R(raw)[0:np_, 1:513], pb[pbk][0:np_, :], r=[("pb", pbk)], w=[("R", raw)])
        op("act", "copy", R(raw)[0:np_, 0:1], carry[0:np_, ccol:ccol + 1], r=["carry"], w=[("R", raw)])
        op("dve", "tensor_tensor", R5(tmp)[0:np_], R(raw)[0:np_, 0:512], R(raw)[0:np_, 1:513], ALU.subtract, r=[("R", raw)], w=[("R", tmp)])
        op("dve", "scalar_tensor_tensor", R5(out)[0:np_], R5(tmp)[0:np_], mu, R(raw)[0:np_, 1:513], ALU.mult, ALU.add,
           r=[("R", tmp), ("R", raw), "pc"], w=[("R", out)])
        op("dve", "tensor_copy", carry[0:np_, ccol:ccol + 1], R(raw)[0:np_, 512:513], r=[("R", raw)], w=["carry"])

    hcols = lambda k: hT[:, k, 32:544]
    KB, BB, KHB, VB = 0, 2, 4, 6
    RT_ = 8
    AM0 = 9
    QA, QB, PA, PB, TA, TB = 17, 18, 19, 20, 21, 22
    TTF, BT, KHT, VT = 23, 25, 27, 29
    XU, YO = 31, 32

    wct = sb("wct", [128, 2, 8], F32)

    class Filler:
        def __init__(self):
            self.gens = []

        def add(self, g):
            self.gens.append(g)

        def fill(self, n):
            while n > 0 and self.gens:
                try:
                    next(self.gens[0])
                    n -= 1
                except StopIteration:
                    self.gens.pop(0)

        def drain(self):
            self.fill(10 ** 9)

    def gproj(wi, M_, nk, rhs_fn, out_ap, wkey, step=4):
        for k in range(nk):
            op("pe", "matmul", out_ap, wbuf[wi][:, k * 128:k * 128 + M_], rhs_fn(k), start=(k == 0), stop=(k == nk - 1),
               r=[("w", wi), "hT"], w=[wkey])
            if k % step == step - 1:
                yield

    SG = (6, 23)
    BON = (16, 24)
    G1, G2, G3 = 25, 26, 27

    def gen_P(tb, hp):
        par = hp % 2
        P = lambda c: pc[:, hp * 10 + c:hp * 10 + c + 1]
        for j, outslot in enumerate((3, 4, 5)):
            wi = load_w(wr[2 + hp * 4 + j], tag=("wr", tb, hp, j))
            k_ = next_pb()
            yield from gproj(wi, 128, 32, hcols, pb[k_][:, :], ("pb", k_))
            if j == 0:
                prefetch(wr[2 + hp * 4 + 3], ("wr", tb, hp, 3))
            shift(k_, 128, j, hp * 3 + j, P(j), 9, outslot)
            yield
        wi = load_w(wr[2 + hp * 4 + 3], tag=("wr", tb, hp, 3))
        k_ = next_pb()
        yield from gproj(wi, 128, 32, hcols, pb[k_][:, :], ("pb", k_))
        op("act", "activation", R5(SG[par]), pb[k_][:, :], AF.Silu, r=[("pb", k_)], w=[("R", SG[par])])
        yield

    def gen_ER(tb, hp):
        par = hp % 2
        P = lambda c: pc[:, hp * 10 + c:hp * 10 + c + 1]
        r32, kx, v32 = R5(3), R5(4), R5(5)
        dma("sp", lor[:, par, 0:128], lora[:, hp * 128:(hp + 1) * 128], w=[("lor", par, 0)], key=("lor", par, 0))
        dma("sp", lor[:, par, 128:256], lora[:, 1024 + hp * 128:1024 + (hp + 1) * 128], w=[("lor", par, 1)], key=("lor", par, 1))
        op("pe", "matmul", pb[2][:, :], lor[:, par, 0:128], R5(17)[0:96], start=True, stop=True,
           r=[("lor", par, 0), ("R", 17)], w=[("pb", 2)])
        op("act", "activation", R5(7), pb[2][:, :], AF.Sigmoid, bias=P(3), r=[("pb", 2), "pc"], w=[("R", 7)])
        yield
        op("pe", "matmul", pb[2][:, :], lor[:, par, 128:256], R5(18)[0:96], start=True, stop=True,
           r=[("lor", par, 1), ("R", 18)], w=[("pb", 2)])
        op("act", "activation", R5(8), pb[2][:, :], AF.Sigmoid, bias=P(4), r=[("pb", 2), "pc"], w=[("R", 8)])
        yield
        sgw, a32 = R5(7), R5(8)
        op("act", "activation", R5(9), kx, AF.Square, scale=P(5), r=[("R", 4), "pc"], w=[("R", 9)])
        op("pe", "matmul", pb[2][:, :], bdones, R5(9), start=True, stop=True, r=["cst", ("R", 9)], w=[("pb", 2)])
        op("dve", "tensor_scalar", R5(9), pb[2][:, :], 1e-24, None, ALU.max, r=[("pb", 2)], w=[("R", 9)])
        yield
        if OPT["ln"]:
            op("act", "activation", R5(9), R5(9), AF.Ln, r=[("R", 9)], w=[("R", 9)])
            op("act", "activation", R5(9), R5(9), AF.Exp, scale=-0.5, r=[("R", 9)], w=[("R", 9)])
        else:
            op("act", "activation", R5(9), R5(9), AF.Sqrt, r=[("R", 9)], w=[("R", 9)])
            op("dve", "reciprocal", R5(9), R5(9), r=[("R", 9)], w=[("R", 9)])
        yield
        op("dve", "scalar_tensor_tensor", R5(10), kx, P(5), R5(9), ALU.mult, ALU.mult, r=[("R", 4), ("R", 9), "pc"], w=[("R", 10)])
        yield
        op("dve", "tensor_scalar", R5(20), a32, P(6), pc[:, 706 + hp:707 + hp], ALU.mult, ALU.add, r=[("R", 8), "pc"], w=[("R", 20)])
        yield
        op("dve", "tensor_tensor", R5(11), kx, R5(20), ALU.mult, r=[("R", 4), ("R", 20)], w=[("R", 11)])
        yield
        op("dve", "tensor_tensor", R5(12), R5(10), a32, ALU.mult, r=[("R", 10), ("R", 8)], w=[("R", 12)])
        yield
        bon = BON[par]
        op("dve", "scalar_tensor_tensor", R5(bon), r32, P(7), R5(11), ALU.mult, ALU.mult, r=[("R", 3), ("R", 11), "pc"], w=[("R", bon)])
        op("pe", "matmul", pb[2][:, :], bdones, R5(bon), start=True, stop=True, r=["cst", ("R", bon)], w=[("pb", 2)])
        yield
        op("dve", "tensor_tensor", R5(bon), pb[2][:, :], v32, ALU.mult, r=[("pb", 2), ("R", 5)], w=[("R", bon)])
        yield
        op("dve", "tensor_tensor_scan", R5(21), rst, sgw, 0.0, ALU.mult, ALU.add, r=["cst", ("R", 7)], w=[("R", 21)])
        yield
        op("act", "activation", R5(13), R5(21), AF.Exp, scale=-C0, r=[("R", 21)], w=[("R", 13)])
        op("dve", "tensor_tensor", R5(20), R5(21), sgw, ALU.subtract, r=[("R", 21), ("R", 7)], w=[("R", 20)])
        yield
        op("act", "activation", R5(14), R5(20), AF.Exp, scale=-C0, r=[("R", 20)], w=[("R", 14)])
        op("act", "activation", R5(15), R5(21), AF.Exp, scale=C0, r=[("R", 21)], w=[("R", 15)])
        yield

    def rwkv_unit_rest(tb, hp, filler):
        par = hp % 2
        P = lambda c: pc[:, hp * 10 + c:hp * 10 + c + 1]
        r32, kx, v32 = R5(3), R5(4), R5(5)
        kap, kh, b32 = R5(10), R5(11), R5(12)
        eL, eLm, einv = R5(13), R5(14), R5(15)
        wc_ = wct[:, par, :]
        wkey = ("wct", par)
        if hp + 1 < 8:
            for j in range(3):
                prefetch(wr[2 + (hp + 1) * 4 + j], ("wr", tb, hp + 1, j))
        op("dve", "tensor_copy", wc_, eL.rearrange("p (c j) -> p c j", j=64)[:, :, 63], r=[("R", 13)], w=[wkey])
        op("dve", "tensor_tensor", R5(21).rearrange("p (c j) -> p c j", j=64), einv.rearrange("p (c j) -> p c j", j=64),
           wc_.unsqueeze(2).to_broadcast([128, 8, 64]), ALU.mult, r=[("R", 15), wkey], w=[("R", 21)])
        ehat = R5(21)
        op("dve", "tensor_tensor", M(RT_), r32, eL, ALU.mult, r=[("R", 3), ("R", 13)], w=[("M", RT_)])
        c3 = lambda ap, h: ap[h * 64:(h + 1) * 64, :].rearrange("p (c j) -> p c j", j=64)
        mk = lambda s: [("M", s), ("M", s + 1)]
        for h in range(2):
            blk = lambda s: M8(s)[h * 64:(h + 1) * 64, :, h * 64:(h + 1) * 64]
            e1 = "dve" if h == 0 else "pool"
            op(e1, "tensor_tensor", blk(KB), c3(kap, h), c3(eLm, h), ALU.mult, r=[("R", 10), ("R", 14)], w=mk(KB))
            op(e1, "tensor_tensor", blk(BB), c3(b32, h), c3(einv, h), ALU.mult, r=[("R", 12), ("R", 15)], w=mk(BB))
            op(e1, "tensor_tensor", blk(KHB), c3(kh, h), c3(einv, h), ALU.mult, r=[("R", 11), ("R", 15)], w=mk(KHB))
            op("pool", "tensor_copy", blk(VB), c3(v32, h), r=[("R", 5)], w=mk(VB))
        for c in range(8):
            kb, bbv, khb = M8(KB)[:, c, :], M8(BB)[:, c, :], M8(KHB)[:, c, :]
            rt = M(RT_)[:, c * 64:(c + 1) * 64]
            rd = mk(KB) + mk(BB) + mk(KHB) + [("M", RT_)]
            k3 = 3 if (c % 2 == 0 or not OPT["aalt"]) else 5
            op("pe", "matmul", pb[k3][:, 0:128], bbv, kb, start=True, stop=True, r=rd, w=[("pb", k3)])
            op("pe", "matmul", pb[k3][:, 128:256], kb, bbv, start=True, stop=True, r=rd, w=[("pb", k3)])
            op("pe", "matmul", pb[k3][:, 256:384], khb, kb, start=True, stop=True, r=rd, w=[("pb", k3)])
            op("pe", "matmul", pb[k3][:, 384:448], bbv, rt, start=True, stop=True, r=rd, w=[("pb", k3)])
            op("pe", "matmul", pb[k3][:, 448:512], khb, rt, start=True, stop=True, r=rd, w=[("pb", k3)])
            op("dve" if c % 2 == 0 else "dve", "tensor_tensor", M(AM0 + c), pb[k3][:, :], maskA, ALU.mult, r=[("pb", k3), "cst"], w=[("M", AM0 + c)])
        for h in range(2):
            blk = lambda s: M8(s)[h * 64:(h + 1) * 64, :, h * 64:(h + 1) * 64]
            e1 = "dve" if h == 0 else "pool"
            op(e1, "tensor_tensor", blk(BB), c3(b32, h), c3(ehat, h), ALU.mult, r=[("R", 12), ("R", 21)], w=mk(BB))
            op(e1, "tensor_tensor", blk(KHB), c3(kh, h), c3(ehat, h), ALU.mult, r=[("R", 11), ("R", 21)], w=mk(KHB))
        LQ = [(QA, QB), (25, 26)]
        LP = [(PA, PB), (27, 28)]
        LT = [(TA, TB), (29, 30)]
        QBANK, PBANK, TBANK = (5, 3), (6, 2), (4, 5)
        pkeys = lambda bk: [("pb6", i_) for i_ in range(4)] if bk == 6 else [("pb", bk)]
        state = []
        for g in range(2):
            amk = [("M", AM0 + 4 * g + j) for j in range(4)]
            AMg = Mt[:, AM0 + 4 * g:AM0 + 4 * g + 4, :]
            op("dve", "tensor_tensor", M4(LT[g][0]), AMg[:, :, 0:128], ident_b.unsqueeze(1).to_broadcast([128, 4, 128]), ALU.add,
               r=amk + ["cstb"], w=[("M", LT[g][0])])
            state.append(dict(q_of=(lambda A: (lambda j: A[:, j, 0:128]))(AMg), p_of=(lambda A: (lambda j: A[:, j, 128:256]))(AMg),
                              qk=amk, pk=amk, tcur=LT[g][0]))
        for l in range(1, 6):
            for g in range(2):
                s_ = state[g]
                qn, pn = LQ[g][l % 2], LP[g][l % 2]
                if l < 5:
                    bk = QBANK[g]
                    for j in range(4):
                        op("pe", "matmul", pb[bk][:, j * 128:(j + 1) * 128], s_["p_of"](j), s_["q_of"](j), start=True, stop=True,
                           r=s_["qk"] + s_["pk"], w=pkeys(bk))
                    op("act", "copy", M(qn), pb[bk][:, :], r=pkeys(bk), w=[("M", qn)])
                bk = PBANK[g]
                for j in range(4):
                    op("pe", "matmul", pb[bk][:, j * 128:(j + 1) * 128], s_["q_of"](j), s_["p_of"](j), start=True, stop=True,
                       r=s_["qk"] + s_["pk"], w=pkeys(bk))
                op("dve", "tensor_copy", M(pn), pb[bk][:, :], r=pkeys(bk), w=[("M", pn)])
            filler.fill(3)
            for g in range(2):
                s_ = state[g]
                qn, pn = LQ[g][l % 2], LP[g][l % 2]
                tcur = s_["tcur"]
                tn = LT[g][1] if tcur == LT[g][0] else LT[g][0]
                bk = TBANK[g]
                for j in range(4):
                    op("pe", "matmul", pb[bk][:, j * 128:(j + 1) * 128], M4(pn)[:, j, :], M4(tcur)[:, j, :], start=True, stop=False,
                       r=[("M", pn), ("M", tcur)], w=pkeys(bk))
                    op("pe", "matmul", pb[bk][:, j * 128:(j + 1) * 128], ident_b, M4(tcur)[:, j, :], start=False, stop=True,
                       r=["cstb", ("M", tcur)], w=pkeys(bk))
                if l < 5:
                    op("act" if g == 0 else "dve", "copy" if g == 0 else "tensor_copy", M(tn), pb[bk][:, :], r=pkeys(bk), w=[("M", tn)])
                    s_["tcur"] = tn
                    s_["q_of"] = (lambda qn_: (lambda j: M4(qn_)[:, j, :]))(qn)
                    s_["qk"] = [("M", qn)]
                else:
                    op("act" if g == 0 else "dve", "copy" if g == 0 else "tensor_copy", M(TTF + g), pb[bk][:, :], r=pkeys(bk), w=[("M", TTF + g)])
                s_["p_of"] = (lambda pn_: (lambda j: M4(pn_)[:, j, :]))(pn)
                s_["pk"] = [("M", pn)]
            filler.fill(3)
        for src, dst in ((BB, BT), (KHB, KHT), (VB, VT)):
            for c in range(8):
                op("pe", "transpose", pb7[:, c * 128:(c + 1) * 128], M8(src)[:, c, :], ident_b, r=mk(src) + ["cstb"], w=["pb7"])
            op("act", "copy", Mt[:, dst:dst + 2, :].rearrange("p a t -> p (a t)"), pb7[:, :], r=["pb7"], w=mk(dst))
            filler.fill(2)
        sbf = Sbf[:, hp, :]
        s32 = S32[:, hp, :]
        skey, s32key = ("Sbf", hp), ("S32", hp)
        xn, uu = M(XU)[:, 0:128], M(XU)[:, 128:256]
        for c in range(8):
            am = M(AM0 + c)
            amk1 = [("M", AM0 + c)]
            rt = M(RT_)[:, c * 64:(c + 1) * 64]
            vt, bt, kht = M8(VT)[:, c, :], M8(BT)[:, c, :], M8(KHT)[:, c, :]
            op("pe", "matmul", pb[6][:, 0:128], M8(KB)[:, c, :], sbf, start=True, stop=False, r=mk(KB) + [skey], w=[("pb6", 0)])
            op("pe", "matmul", pb[6][:, 0:128], am[:, 256:384], vt, start=False, stop=True, r=amk1 + mk(VT), w=[("pb6", 0)])
            ysl = pb[4][:, c * 64:(c + 1) * 64]
            op("pe", "matmul", ysl, sbf, rt, start=True, stop=False, r=[skey, ("M", RT_)], w=[("pb", 4)])
            op("pe", "matmul", ysl, vt, am[:, 448:512], start=False, stop=False, r=mk(VT) + amk1, w=[("pb", 4)])
            op("act", "mul", xn, pb[6][:, 0:128], -1.0, r=[("pb6", 0)], w=[("XU", 0)])
            filler.fill(1)
            op("pe", "matmul", pb[6][:, 128:256], M8(TTF)[:, c, :], xn, start=True, stop=True, r=mk(TTF) + [("XU", 0)], w=[("pb6", 1)])
            op("dve", "tensor_copy", uu, pb[6][:, 128:256], r=[("pb6", 1)], w=[("XU", 1)])
            filler.fill(1)
            op("pe", "matmul", pb[6][:, 256:384], bt, uu, start=True, stop=False, r=mk(BT) + [("XU", 1)], w=[("pb6", 2)])
            op("pe", "matmul", pb[6][:, 256:384], kht, vt, start=False, stop=True, r=mk(KHT) + mk(VT), w=[("pb6", 2)])
            op("pe", "matmul", ysl, uu, am[:, 384:448], start=False, stop=True, r=[("XU", 1)] + amk1, w=[("pb", 4)])
            op("dve", "scalar_tensor_tensor", sbf, s32, wc_[:, c:c + 1], pb[6][:, 256:384], ALU.mult, ALU.add,
               r=[s32key, wkey, ("pb6", 2)], w=[skey])
            op("dve", "scalar_tensor_tensor", s32, s32, wc_[:, c:c + 1], pb[6][:, 256:384], ALU.mult, ALU.add,
               r=[s32key, wkey, ("pb6", 2)], w=[s32key])
            filler.fill(1)
        filler.drain()
        sgs, bon = SG[par], BON[par]
        op("act", "copy", R5(G1), pb[4][:, :], r=[("pb", 4)], w=[("R", G1)])

        def gen_G():
            op("act", "activation", R5(G2), R5(G1), AF.Square, r=[("R", G1)], w=[("R", G2)])
            op("pe", "matmul", pb[2][:, :], bdones, R5(G1), start=True, stop=True, r=["cst", ("R", G1)], w=[("pb", 2)])
            op("act", "mul", R5(G3), pb[2][:, :], 1.0 / 64, r=[("pb", 2)], w=[("R", G3)])
            yield
            op("pe", "matmul", pb[2][:, :], bdones, R5(G2), start=True, stop=True, r=["cst", ("R", G2)], w=[("pb", 2)])
            op("dve", "tensor_tensor", R5(G2), R5(G3), R5(G3), ALU.mult, r=[("R", G3)], w=[("R", G2)])
            op("dve", "scalar_tensor_tensor", R5(G2), pb[2][:, :], 1.0 / 64, R5(G2), ALU.mult, ALU.subtract, r=[("pb", 2), ("R", G2)], w=[("R", G2)])
            yield
            op("dve", "tensor_scalar", R5(G2), R5(G2), 64e-5, None, ALU.add, r=[("R", G2)], w=[("R", G2)])
            if OPT["ln"]:
                op("act", "activation", R5(G2), R5(G2), AF.Ln, r=[("R", G2)], w=[("R", G2)])
                op("act", "activation", R5(G2), R5(G2), AF.Exp, scale=-0.5, r=[("R", G2)], w=[("R", G2)])
            else:
                op("act", "activation", R5(G2), R5(G2), AF.Sqrt, r=[("R", G2)], w=[("R", G2)])
                op("dve", "reciprocal", R5(G2), R5(G2), r=[("R", G2)], w=[("R", G2)])
            yield
            op("dve", "tensor_tensor", R5(G1), R5(G1), R5(G3), ALU.subtract, r=[("R", G1), ("R", G3)], w=[("R", G1)])
            yield
            op("dve", "tensor_tensor", R5(G1), R5(G1), R5(G2), ALU.mult, r=[("R", G1), ("R", G2)], w=[("R", G1)])
            yield
            op("dve", "tensor_scalar", R5(G1), R5(G1), P(8), P(9), ALU.mult, ALU.add, r=[("R", G1), "pc"], w=[("R", G1)])
            yield
            op("dve", "tensor_tensor", R5(G1), R5(G1), R5(bon), ALU.add, r=[("R", G1), ("R", bon)], w=[("R", G1)])
            yield
            op("dve", "tensor_tensor", M(YO), R5(G1), R5(sgs), ALU.mult, r=[("R", G1), ("R", sgs)], w=[("M", YO)])
            dma("sp", zsend[tb][hp * 128:(hp + 1) * 128, :], M(YO), r=[("M", YO)], w=[("zsend", tb)], key="zs")
            yield

        return gen_G()

    def rwkv_block(tb):
        prefetch(wr[0], ("wa", tb, 0))
        prefetch(wr[1], ("wa", tb, 1))
        prefetch(wr[2], ("wr", tb, 0, 0))
        build_hT(xf, [(tb * 512 + tt * 128, 128) for tt in range(4)], 32)
        for j in range(2):
            wi = load_w(wr[j], tag=("wa", tb, j))
            k_ = next_pb()
            proj(wi, 96, 32, hcols, pb[k_][0:96, :], ("pb", k_))
            shift(k_, 96, 0, 24 + j, pc[0:96, 80 + j:81 + j], 1, 17 + j)
        op("act", "activation", R5(17)[0:96], R5(17)[0:96], AF.Tanh, r=[("R", 17)], w=[("R", 17)])
        filler = Filler()
        filler.add(gen_P(tb, 0))
        filler.add(gen_ER(tb, 0))
        filler.drain()
        gG = None
        for hp in range(8):
            if gG is not None:
                filler.add(gG)
            if hp + 1 < 8:
                filler.add(gen_P(tb, hp + 1))
                filler.add(gen_ER(tb, hp + 1))
            gG = rwkv_unit_rest(tb, hp, filler)
            if not OPT["gdefer"]:
                for _ in gG:
                    pass
                gG = None
        if gG is not None:
            filler.add(gG)
        filler.drain()
        if n_conv_blocks:
            S.dma("pool", lambda e: e.collective_compute("AllGather", ALU.bypass, replica_groups=[[0, 1], [2, 3], [4, 5], [6, 7]],
                                                         ins=[zsend[tb]], outs=[zall[tb]]), reads=[("zsend", tb)], writes=[("zall", tb)],
                  key=("cc", tb), inc=1)

    for tb in range(n_rwkv_blocks):
        rwkv_block(tb)

    if n_conv_blocks:
        for tb in range(n_rwkv_blocks):
            op("pool", "memset", small[:, 8:9], 0.0, r=[("zall", tb)], w=[("zall", tb), "sm8"])

    CT = lambda cc: R5(cc)
    SV = lambda cc: Rt[:, 16 + cc // 2, :].bitcast(BF16)[:, (cc % 2) * 512:(cc % 2) * 512 + 512]
    skey_ = lambda cc: ("R", 16 + cc // 2)
    UB = [24, 25]
    YR0, YC0 = 0, 16

    def conv_block(b):
        first = (b == 0)
        if first:
            build_hT(xc, [(0, 32)] + [(32 + tt * 128, 128) for tt in range(4)], 0)
        else:
            build_hT(xc, [(HALO + b * 512 + tt * 128, 128) for tt in range(4)], 32)
        def PROJ(cc):
            wv = load_w(wc[2 * cc])
            proj(wv, 128, 32, hcols, pb[0][:, :], ("pb", 0))
            if first:
                proj(wv, 128, 32, lambda k: hT[:, k, 0:32], pb[2][:, 0:32], ("pb", 2))
            wg = load_w(wc[2 * cc + 1])
            proj(wg, 128, 32, hcols, pb[1][:, :], ("pb", 1))
            if first:
                proj(wg, 128, 32, lambda k: hT[:, k, 0:32], pb[2][:, 32:64], ("pb", 2))

        def GLU(cc):
            ub = UB[cc % 2]
            U = Rt[:, ub, 0:544]
            op("act", "activation", XT[:, 0, 0:512], pb[1][:, :], AF.Sigmoid, r=[("pb", 1)], w=[("XT", 0)])
            op("dve", "tensor_tensor", U[:, 32:544], pb[0][:, :], XT[:, 0, 0:512], ALU.mult, r=[("pb", 0), ("XT", 0)], w=[("R", ub)])
            if first:
                op("act", "activation", XT[:, 1, 0:32], pb[2][:, 32:64], AF.Sigmoid, r=[("pb", 2)], w=[("XT", 1)])
                op("dve", "tensor_tensor", U[:, 0:32], pb[2][:, 0:32], XT[:, 1, 0:32], ALU.mult, r=[("pb", 2), ("XT", 1)], w=[("R", ub)])
            else:
                op("act", "copy", U[:, 0:32], ucar[:, cc, :], r=[("ucar", cc)], w=[("R", ub)])
            op("act", "copy", ucar[:, cc, :], U[:, 512:544], r=[("R", ub)], w=[("ucar", cc)])

        def CONV(cc):
            cp = lambda c: pc[:, 82 + cc * 4 + c:82 + cc * 4 + c + 1]
            ub = UB[cc % 2]
            U = Rt[:, ub, 0:544]
            wk = lambda k: pc[:, 146 + cc * 31 + k:146 + cc * 31 + k + 1]
            op("dve", "tensor_scalar", CT(cc), U[:, 2:514], wk(0), cp(0), ALU.mult, ALU.add, r=[("R", ub), "pc"], w=[("R", cc)])
            for k in range(1, 31):
                op("dve", "scalar_tensor_tensor", CT(cc), U[:, 2 + k:514 + k], wk(k), CT(cc), ALU.mult, ALU.add,
                   r=[("R", ub), ("R", cc), "pc"], w=[("R", cc)])

        def STATS(cc):
            op("act", "activation", XT[:, 2, 0:512], CT(cc), AF.Square, r=[("R", cc)], w=[("XT", 2)])
            op("pe", "matmul", pb[3][:, :], ones_f, CT(cc), start=(cc == 0), stop=(cc == 15), r=["cst", ("R", cc)], w=[("pb", 3)])
            op("pe", "matmul", pb[4][:, :], ones_f, XT[:, 2, 0:512], start=(cc == 0), stop=(cc == 15), r=["cst", ("XT", 2)], w=[("pb", 4)])

        def BLEND(k):
            sl = k % 2
            dma("sp", ystg[:, 2 * sl, :], zall[b][k * 128:(k + 1) * 128, :], r=[("zall", b)], w=[("ys", 2 * sl)], key=("ys", 2 * sl))
            op("act", "mul", M(YR0 + k), ystg[:, 2 * sl, :], sel[:, 0:1], r=[("ys", 2 * sl), "sel"], w=[("M", YR0 + k)])
            if 4 + b >= n_rwkv_blocks:
                return
            dma("sp", ystg[:, 2 * sl + 1, :], zall[4 + b][k * 128:(k + 1) * 128, :], r=[("zall", 4 + b)], w=[("ys", 2 * sl + 1)],
                key=("ys", 2 * sl + 1))
            op("dve", "scalar_tensor_tensor", M(YR0 + k), ystg[:, 2 * sl + 1, :], sel[:, 1:2], M(YR0 + k), ALU.mult, ALU.add,
               r=[("ys", 2 * sl + 1), ("M", YR0 + k), "sel"], w=[("M", YR0 + k)])

        PROJ(0)
        GLU(0)
        for cc in range(16):
            if cc + 1 < 16:
                PROJ(cc + 1)
            CONV(cc)
            BLEND(cc)
            STATS(cc)
            if cc + 1 < 16:
                GLU(cc + 1)
        mean, rstd = XT[:, 0, 0:512], XT[:, 1, 0:512]
        op("act", "mul", mean, pb[3][:, :], 1.0 / 2048, r=[("pb", 3)], w=[("XT", 0)])
        op("dve", "tensor_tensor", XT[:, 2, 0:512], mean, mean, ALU.mult, r=[("XT", 0)], w=[("XT", 2)])
        op("dve", "scalar_tensor_tensor", rstd, pb[4][:, :], 1.0 / 2048, XT[:, 2, 0:512], ALU.mult, ALU.subtract, r=[("pb", 4), ("XT", 2)], w=[("XT", 1)])
        op("dve", "tensor_scalar", rstd, rstd, 1e-5, None, ALU.add, r=[("XT", 1)], w=[("XT", 1)])
        op("act", "activation", rstd, rstd, AF.Sqrt, r=[("XT", 1)], w=[("XT", 1)])
        op("dve", "reciprocal", rstd, rstd, r=[("XT", 1)], w=[("XT", 1)])
        for cc in range(16):
            cp = lambda c: pc[:, 82 + cc * 4 + c:82 + cc * 4 + c + 1]
            op("dve", "tensor_tensor", CT(cc), CT(cc), mean, ALU.subtract, r=[("R", cc), ("XT", 0)], w=[("R", cc)])
            op("dve", "tensor_tensor", CT(cc), CT(cc), rstd, ALU.mult, r=[("R", cc), ("XT", 1)], w=[("R", cc)])
            op("act", "activation", SV(cc), CT(cc), AF.Silu, bias=cp(2), scale=cp(1), r=[("R", cc), "pc"], w=[skey_(cc)])
        for e_ in range(16):
            cp = lambda c: pc[:, 82 + e_ * 4 + c:82 + e_ * 4 + c + 1]
            wgc = load_w(wc[32 + e_])
            proj(wgc, 128, 32, hcols, pb[0][:, :], ("pb", 0))
            op("act", "activation", XT[:, 2, 0:512], pb[0][:, :], AF.Silu, r=[("pb", 0)], w=[("XT", 2)])
            wpi = load_w(wp[e_], 2048)
            for k in range(16):
                op("pe", "matmul", pb[1][:, :], wbuf[wpi][:, k * 128:(k + 1) * 128], SV(k), start=(k == 0), stop=(k == 15),
                   r=[("w", wpi), skey_(k)], w=[("pb", 1)])
            op("dve", "scalar_tensor_tensor", M(YC0 + e_), pb[1][:, :], cp(3), XT[:, 2, 0:512], ALU.add, ALU.mult,
               r=[("pb", 1), ("XT", 2), "pc"], w=[("M", YC0 + e_)])
        for dch in range(32):
            wi = load_w(wo[dch])
            k_ = dch % 2
            for k in range(32):
                op("pe", "matmul", pb[k_][:, :], wbuf[wi][:, k * 128:(k + 1) * 128], M(k), start=(k == 0), stop=(k == 31),
                   r=[("w", wi), ("M", k)], w=[("pb", k_)])
            op("act", "copy", R5(dch), pb[k_][:, :], r=[("pb", k_)], w=[("R", dch)])
            op("act", "activation", XT[:, dch % 2, 0:512], pb[k_][:, :], AF.Square, r=[("pb", k_)], w=[("XT", dch % 2)])
            op("pe", "matmul", pb[5][:, :], ones_f, XT[:, dch % 2, 0:512], start=(dch == 0), stop=(dch == 31), r=["cst", ("XT", dch % 2)], w=[("pb", 5)])
        rs = XT[:, 2, 0:512]
        op("dve", "tensor_scalar", rs, pb[5][:, :], 1.0 / D, 1e-6, ALU.mult, ALU.add, r=[("pb", 5)], w=[("XT", 2)])
        op("act", "activation", rs, rs, AF.Sqrt, r=[("XT", 2)], w=[("XT", 2)])
        op("dve", "reciprocal", rs, rs, r=[("XT", 2)], w=[("XT", 2)])
        for dch in range(32):
            op("dve", "scalar_tensor_tensor", R5(dch), R5(dch), pc[:, 674 + dch:675 + dch], rs, ALU.mult, ALU.mult,
               r=[("R", dch), ("XT", 2), "pc"], w=[("R", dch)])
        for tt in range(4):
            row_x = HALO + b * 512 + tt * 128
            dma("sp", io32[:, :], xc[row_x:row_x + 128, :], w=["io32"], key="io32")
            for g8 in range(8):
                k_ = g8 % 2
                for j in range(4):
                    dch = g8 * 4 + j
                    op("pe", "transpose", pb[k_][:, j * 128:(j + 1) * 128], R5(dch)[:, tt * 128:(tt + 1) * 128], ident_f,
                       r=[("R", dch), "cst"], w=[("pb", k_)])
                op("dve", "tensor_tensor", io32[:, g8 * 512:(g8 + 1) * 512], io32[:, g8 * 512:(g8 + 1) * 512], pb[k_][:, :], ALU.add,
                   r=["io32", ("pb", k_)], w=["io32"])
            row_y = b * 512 + tt * 128
            dma("sp", yout[row_y:row_y + 128, :], io32[:, :], r=["io32"], key="yout")

    for b in range(n_conv_blocks):
        conv_block(b)

    counts = S.emit(final_waits=["yout"] if n_conv_blocks else ["zs"])
    st.close()
    return nc, counts


def _chunk_layout(wcols):
    m = wcols.shape[1]
    nk = wcols.shape[0] // 128
    a = np.zeros((nk, 128, 128), np.float32)
    a[:, :, :m] = wcols.reshape(nk, 128, m)
    return np.ascontiguousarray(a.transpose(1, 0, 2)).reshape(128, nk * 128)


def _consts():
    cst = np.zeros((128, 1408), np.float32)
    cst[:, 0:128] = np.eye(128, dtype=np.float32)
    idx = np.arange(128)
    same = (idx[:, None] // 64) == (idx[None, :] // 64)
    cst[:, 128:256] = same.astype(np.float32)
    cst[:, 256:384] = 1.0
    s = idx[:, None] % 64
    t = idx[None, :] % 64
    cst[:, 384:512] = -(same & (s < t)).astype(np.float32)
    cst[:, 512:640] = -(same & (s > t)).astype(np.float32)
    cst[:, 640:768] = (same & (s < t)).astype(np.float32)
    incl = (s[:, 0:1] <= np.arange(64)[None, :]).astype(np.float32)
    cst[:, 768:832] = incl
    cst[:, 832:896] = incl
    r = np.ones(512, np.float32)
    r[0::64] = 0.0
    cst[:, 896:1408] = r[None, :]
    return cst


_CACHE = {}
_NB = (8, 4)


def kernel(x, norm_pre_g, w_in, mu_shift, w0, w_lora_up, a0, a_lora_up, k_k, k_a, r_k, lnx_g, lnx_b, conv_w, conv_b,
           cln_g, cln_b, w_pw2, b_pw2, w_out, norm_post_g):
    f = lambda a: np.asarray(a, dtype=np.float32)
    x, w_in, w_out, w_pw2 = f(x), f(w_in), f(w_out), f(w_pw2)
    mu_shift, w0, a0, k_k, k_a = f(mu_shift), f(w0), f(a0), f(k_k), f(k_a)
    r_k, lnx_g, lnx_b = f(r_k).reshape(-1), f(lnx_g), f(lnx_b)
    conv_w, conv_b, cln_g, cln_b, b_pw2 = f(conv_w), f(conv_b), f(cln_g), f(cln_b), f(b_pw2)
    norm_pre_g, norm_post_g = f(norm_pre_g), f(norm_post_g)
    w_lora_up, a_lora_up = f(w_lora_up), f(a_lora_up)

    if "nc" not in _CACHE:
        _CACHE["nc"] = build_program(*_NB)[0]
    nc = _CACHE["nc"]

    cst = _consts()
    wc_l = np.empty((48, 128, 4096), np.float32)
    c1 = 6336 + 2048
    for cc in range(16):
        wc_l[2 * cc] = _chunk_layout(w_in[:, c1 + cc * 128:c1 + (cc + 1) * 128])
        wc_l[2 * cc + 1] = _chunk_layout(w_in[:, c1 + 2048 + cc * 128:c1 + 2048 + (cc + 1) * 128])
        wc_l[32 + cc] = _chunk_layout(w_in[:, c1 + 4096 + cc * 128:c1 + 4096 + (cc + 1) * 128])
    wp_l = np.stack([_chunk_layout(w_pw2[:, e * 128:(e + 1) * 128]) for e in range(16)])
    wo_l = np.stack([_chunk_layout(w_out[:, dch * 128:(dch + 1) * 128]) for dch in range(32)])

    in_maps = []
    for c in range(NCORES):
        b, hh = c // 2, c % 2
        ch0 = hh * 1024
        wr_l = np.empty((34, 128, 4096), np.float32)
        wr_l[0] = _chunk_layout(w_in[:, 6144:6240])
        wr_l[1] = _chunk_layout(w_in[:, 6240:6336])
        pcm = np.zeros((128, NPC), np.float32)
        for hp in range(8):
            lo = ch0 + hp * 128
            for j in range(3):
                wr_l[2 + hp * 4 + j] = _chunk_layout(w_in[:, j * 2048 + lo:j * 2048 + lo + 128])
            wr_l[2 + hp * 4 + 3] = _chunk_layout(w_in[:, 6336 + lo:6336 + lo + 128])
            cols = [mu_shift[lo:lo + 128], mu_shift[2048 + lo:2048 + lo + 128], mu_shift[4096 + lo:4096 + lo + 128],
                    w0[lo:lo + 128], a0[lo:lo + 128], k_k[lo:lo + 128], k_a[lo:lo + 128], r_k[lo:lo + 128],
                    lnx_g[lo:lo + 128], lnx_b[lo:lo + 128]]
            for j, v in enumerate(cols):
                pcm[:, hp * 10 + j] = v
        pcm[0:96, 80] = mu_shift[6144:6240]
        pcm[0:96, 81] = mu_shift[6240:6336]
        for cc in range(16):
            sl = slice(cc * 128, (cc + 1) * 128)
            pcm[:, 82 + cc * 4 + 0] = conv_b[sl]
            pcm[:, 82 + cc * 4 + 1] = cln_g[sl]
            pcm[:, 82 + cc * 4 + 2] = cln_b[sl]
            pcm[:, 82 + cc * 4 + 3] = b_pw2[sl]
            pcm[:, 146 + cc * 31:146 + (cc + 1) * 31] = conv_w[:, sl].T
        pcm[:, 642:674] = norm_pre_g.reshape(32, 128).T
        pcm[:, 674:706] = norm_post_g.reshape(32, 128).T
        lor = np.concatenate([w_lora_up[:, ch0:ch0 + 1024], a_lora_up[:, ch0:ch0 + 1024]], axis=1)
        xcv = np.zeros((HALO + TOKC, D), np.float32)
        t0 = hh * TOKC
        if hh == 1:
            xcv[0:HALO] = x[b, t0 - HALO:t0]
        xcv[HALO:] = x[b, t0:t0 + TOKC]
        selv = np.zeros((128, 2), np.float32)
        selv[:, hh] = 1.0
        in_maps.append({"xf": np.ascontiguousarray(x[b]), "xc": xcv, "wr": wr_l, "wc": wc_l, "wp": wp_l, "wo": wo_l,
                        "lora": np.ascontiguousarray(lor), "pc": pcm, "cst": cst, "sel": selv})
    res = run_bass_kernel_spmd(nc, in_maps, core_ids=list(range(NCORES)))
    out = np.empty((4, T, D), np.float32)
    for c in range(NCORES):
        b, hh = c // 2, c % 2
        out[b, hh * TOKC:(hh + 1) * TOKC] = res.results[c]["y"]
    return out
```
